# Optimizing a Trainium2 kernel written in Bass

```python
import jax, jax.numpy as jnp
from jax import lax
import numpy as np

D_MODEL = 1024
BATCH = 8
SEQ = 2048
DEPTH = 4

CONF_WIDTH = 512
CONF_KERNEL = 31
SC_WIDTH = 512
SC_KERNEL = 3
MLA_HEADS = 8
QK_NOPE = 64
QK_ROPE = 32
V_HEAD = 64
Q_LORA = 384
KV_LORA = 256
MLA_WIDTH = MLA_HEADS * V_HEAD
ROPE_THETA = 10000.0
Q_BLOCK = 128
N_BRANCH = 3
LN_EPS = 1e-5
RMS_EPS = 1e-6
DEEPNORM_ALPHA = (2 * DEPTH) ** 0.25
DEEPNORM_BETA = (8 * DEPTH) ** -0.25
ADA_SCALE = 0.5

IN_SIZES = (2 * CONF_WIDTH, CONF_WIDTH, 3 * SC_WIDTH, SC_WIDTH, Q_LORA, KV_LORA, QK_ROPE, MLA_WIDTH, N_BRANCH * D_MODEL)
D_IN = 2 * CONF_WIDTH + CONF_WIDTH + 3 * SC_WIDTH + SC_WIDTH + Q_LORA + KV_LORA + QK_ROPE + MLA_WIDTH + N_BRANCH * D_MODEL

kernel_name = "hybrid_conformer_shortconv_mla_deepnorm_adaln"


def layer_norm(x, g, b):
    x32 = x.astype(jnp.float32)
    mu = jnp.mean(x32, axis=-1, keepdims=True)
    var = jnp.mean(jnp.square(x32 - mu), axis=-1, keepdims=True)
    y = (x32 - mu) * lax.rsqrt(var + LN_EPS)
    return (y * g.astype(jnp.float32) + b.astype(jnp.float32)).astype(x.dtype)


def rms_norm(x, g):
    x32 = x.astype(jnp.float32)
    y = x32 * lax.rsqrt(jnp.mean(jnp.square(x32), axis=-1, keepdims=True) + RMS_EPS)
    return (y * g.astype(jnp.float32)).astype(x.dtype)


def causal_depthwise_conv(x, w):
    k_width, ch = w.shape
    return lax.conv_general_dilated(
        x, w.astype(x.dtype)[:, None, :], window_strides=(1,), padding=((k_width - 1, 0),),
        dimension_numbers=("NWC", "WIO", "NWC"), feature_group_count=ch)


def rope_tables(positions):
    inv_freq = ROPE_THETA ** (-jnp.arange(0, QK_ROPE, 2, dtype=jnp.float32) / QK_ROPE)
    ang = positions.astype(jnp.float32)[..., None] * inv_freq
    return jnp.cos(ang), jnp.sin(ang)


def apply_rope(x, cos, sin):
    cos = cos.astype(x.dtype)
    sin = sin.astype(x.dtype)
    x1, x2 = jnp.split(x, 2, axis=-1)
    return jnp.concatenate([x1 * cos - x2 * sin, x2 * cos + x1 * sin], axis=-1)


def causal_block_attention(q, k, v):
    b, s, h, dv = v.shape
    scale = (QK_NOPE + QK_ROPE) ** -0.5
    key_idx = jnp.arange(s)

    def one_block(i):
        start = i * Q_BLOCK
        qb = lax.dynamic_slice_in_dim(q, start, Q_BLOCK, axis=1)
        sc = jnp.einsum("bqhd,bkhd->bhqk", qb, k, preferred_element_type=jnp.float32) * scale
        q_idx = start + jnp.arange(Q_BLOCK)
        mask = key_idx[None, :] <= q_idx[:, None]
        sc = jnp.where(mask[None, None], sc, -jnp.inf)
        p = jax.nn.softmax(sc, axis=-1).astype(v.dtype)
        return jnp.einsum("bhqk,bkhd->bqhd", p, v)

    out = lax.map(one_block, jnp.arange(s // Q_BLOCK))
    return out.transpose(1, 0, 2, 3, 4).reshape(b, s, h, dv)


def hybrid_layer(x, c_act, cos, sin, w_ada, b_ada, w_in, conv_a_w, conv_a_b, ln_a_g, ln_a_b, w_a_out,
                 conv_b_w, w_b_out, q_norm_g, kv_norm_g, w_uq, w_ukv, w_c_out, w_o, ln_g, ln_b):
    b, s, _ = x.shape
    ada = c_act @ w_ada + b_ada
    shift, scale, gate = jnp.split(ada, 3, axis=-1)
    u = x * (1.0 + scale[:, None, :]) + shift[:, None, :]

    proj = jnp.einsum("bsd,dp->bsp", u, w_in)
    split_pts = [int(v) for v in np.cumsum(IN_SIZES)[:-1]]
    a_in, a_gate, b_in, b_gate, q_lat, kv_lat, k_rope, c_gate, merge_logits = jnp.split(proj, split_pts, axis=-1)

    a1, a2 = jnp.split(a_in, 2, axis=-1)
    a = a1 * jax.nn.sigmoid(a2)
    a = causal_depthwise_conv(a, conv_a_w) + conv_a_b
    a = jax.nn.silu(layer_norm(a, ln_a_g, ln_a_b))
    y_a = jnp.einsum("bsc,cd->bsd", a * jax.nn.silu(a_gate), w_a_out)

    xb, gb, gc = jnp.split(b_in, 3, axis=-1)
    yb = gb * causal_depthwise_conv(gc * xb, conv_b_w)
    y_b = jnp.einsum("bsc,cd->bsd", yb * jax.nn.silu(b_gate), w_b_out)

    q = jnp.einsum("bsr,rk->bsk", rms_norm(q_lat, q_norm_g), w_uq).reshape(b, s, MLA_HEADS, QK_NOPE + QK_ROPE)
    q_nope, q_pe = jnp.split(q, [QK_NOPE], axis=-1)
    q_pe = apply_rope(q_pe, cos[:, :, None, :], sin[:, :, None, :])
    kv = jnp.einsum("bsr,rk->bsk", rms_norm(kv_lat, kv_norm_g), w_ukv).reshape(b, s, MLA_HEADS, QK_NOPE + V_HEAD)
    k_nope, v = jnp.split(kv, [QK_NOPE], axis=-1)
    k_pe = apply_rope(k_rope, cos, sin)[:, :, None, :]
    qf = jnp.concatenate([q_nope, q_pe], axis=-1)
    kf = jnp.concatenate([k_nope, jnp.broadcast_to(k_pe, (b, s, MLA_HEADS, QK_ROPE))], axis=-1)
    o = causal_block_attention(qf, kf, v).reshape(b, s, MLA_WIDTH)
    y_c = jnp.einsum("bsc,cd->bsd", o * jax.nn.silu(c_gate), w_c_out)

    g_a, g_b, g_c = jnp.split(jax.nn.sigmoid(merge_logits), N_BRANCH, axis=-1)
    m = g_a * y_a + g_b * y_b + g_c * y_c
    out = jnp.einsum("bsd,de->bse", m, w_o)

    return layer_norm(DEEPNORM_ALPHA * x + gate[:, None, :] * out, ln_g, ln_b)


def setup_inputs(seed: int = 0) -> dict:
    key = jax.random.key(seed)
    ks = jax.random.split(key, 24)
    L, D = DEPTH, D_MODEL

    def nrm(k, shape, scale):
        return jax.random.normal(k, shape, dtype=jnp.float32) * scale

    x = nrm(ks[0], (BATCH, SEQ, D), 1.0)
    c = nrm(ks[1], (BATCH, D), 1.0)
    offsets = jax.random.randint(ks[2], (BATCH, 1), 0, 1024, dtype=jnp.int32)
    positions = (jnp.arange(SEQ, dtype=jnp.int32)[None, :] + offsets).astype(jnp.int32)
    return {
        "x": x,
        "c": c,
        "positions": positions,
        "w_ada": nrm(ks[3], (L, D, 3 * D), ADA_SCALE * D ** -0.5),
        "b_ada": nrm(ks[4], (L, 3 * D), 0.02),
        "w_in": nrm(ks[5], (L, D, D_IN), D ** -0.5),
        "conv_a_w": nrm(ks[6], (L, CONF_KERNEL, CONF_WIDTH), CONF_KERNEL ** -0.5),
        "conv_a_b": nrm(ks[7], (L, CONF_WIDTH), 0.02),
        "ln_a_g": 1.0 + nrm(ks[8], (L, CONF_WIDTH), 0.02),
        "ln_a_b": nrm(ks[9], (L, CONF_WIDTH), 0.02),
        "w_a_out": nrm(ks[10], (L, CONF_WIDTH, D), DEEPNORM_BETA * CONF_WIDTH ** -0.5),
        "conv_b_w": nrm(ks[11], (L, SC_KERNEL, SC_WIDTH), SC_KERNEL ** -0.5),
        "w_b_out": nrm(ks[12], (L, SC_WIDTH, D), DEEPNORM_BETA * SC_WIDTH ** -0.5),
        "q_norm_g": 1.0 + nrm(ks[13], (L, Q_LORA), 0.02),
        "kv_norm_g": 1.0 + nrm(ks[14], (L, KV_LORA), 0.02),
        "w_uq": nrm(ks[15], (L, Q_LORA, MLA_HEADS * (QK_NOPE + QK_ROPE)), Q_LORA ** -0.5),
        "w_ukv": nrm(ks[16], (L, KV_LORA, MLA_HEADS * (QK_NOPE + V_HEAD)), KV_LORA ** -0.5),
        "w_c_out": nrm(ks[17], (L, MLA_WIDTH, D), DEEPNORM_BETA * MLA_WIDTH ** -0.5),
        "w_o": nrm(ks[18], (L, D, D), DEEPNORM_BETA * D ** -0.5),
        "ln_g": 1.0 + nrm(ks[19], (L, D), 0.02),
        "ln_b": nrm(ks[20], (L, D), 0.02),
    }


def reference(x, c, positions, w_ada, b_ada, w_in, conv_a_w, conv_a_b, ln_a_g, ln_a_b, w_a_out,
              conv_b_w, w_b_out, q_norm_g, kv_norm_g, w_uq, w_ukv, w_c_out, w_o, ln_g, ln_b):
    c_act = jax.nn.silu(c)
    cos, sin = rope_tables(positions)
    h = x
    for l in range(DEPTH):
        h = hybrid_layer(h, c_act, cos, sin, w_ada[l], b_ada[l], w_in[l], conv_a_w[l], conv_a_b[l],
                         ln_a_g[l], ln_a_b[l], w_a_out[l], conv_b_w[l], w_b_out[l], q_norm_g[l],
                         kv_norm_g[l], w_uq[l], w_ukv[l], w_c_out[l], w_o[l], ln_g[l], ln_b[l])
    return h
```

```python
import numpy as np
import concourse.bass as bass
import concourse.mybir as mybir
from concourse.bass_utils import run_bass_kernel_spmd

F32 = mybir.dt.float32
BF16 = mybir.dt.bfloat16
I32 = mybir.dt.int32
ALU = mybir.AluOpType
AF = mybir.ActivationFunctionType
AX = mybir.AxisListType

_ISZ = {F32: 4, BF16: 2, I32: 4}
SB_BASE = 16512
SB_SIZE = 212800


class Op:
    __slots__ = ("eng", "fn", "deps", "idx", "sig", "seq", "is_dma", "dsem", "dval")


class View:
    __slots__ = ("ap", "blocks")

    def __init__(self, ap, blocks):
        self.ap = ap
        self.blocks = blocks

    def re(self, pat, **kw):
        return View(self.ap.rearrange(pat, **kw), self.blocks)


class Buf:
    def __init__(self, sch, name, free_shape, dtype, offset=None, parts=128, space="sb"):
        self.sch = sch
        self.name = name
        self.shape = tuple(free_shape)
        self.dtype = dtype
        self.isz = _ISZ[dtype]
        self.space = space
        self.parts = parts
        nc = sch.nc
        if space == "sb":
            self.off = offset
            self.h = nc.alloc_sbuf_tensor_at(name, [parts] + list(free_shape), dtype, offset=SB_BASE + offset)
            self.gran = 64
        else:
            self.off = 0
            self.h = nc.alloc_psum_tensor(name, [parts] + list(free_shape), dtype)
            self.gran = 2048
        st = []
        s = 1
        for d in reversed(self.shape):
            st.append(s)
            s *= d
        self.strides = tuple(reversed(st))
        self.nbytes = s * self.isz

    def _runs(self, key):
        shape, strides = self.shape, self.strides
        n = len(shape)
        d0 = n - 1
        while d0 > 0 and key[d0] == (0, shape[d0]):
            d0 -= 1
        runs = []

        def rec(d, base):
            a, b = key[d]
            if d == d0:
                runs.append((base + a * strides[d], base + b * strides[d]))
            else:
                for i in range(a, b):
                    rec(d + 1, base + i * strides[d])

        rec(0, 0)
        return runs

    def __getitem__(self, key):
        if not isinstance(key, tuple):
            key = (key,)
        pk = key[0]
        fk = list(key[1:])
        while len(fk) < len(self.shape):
            fk.append(slice(None))
        norm = []
        for d, k in enumerate(fk):
            if isinstance(k, int):
                norm.append((k, k + 1))
            else:
                a = 0 if k.start is None else k.start
                b = self.shape[d] if k.stop is None else k.stop
                norm.append((a, b))
        blocks = set()
        g = self.gran
        for (a, b) in self._runs(norm):
            a = self.off + a * self.isz
            b = self.off + b * self.isz
            for blk in range(a // g, (b - 1) // g + 1):
                blocks.add((self.space, blk))
        ap = self.h[(pk,) + tuple(fk)]
        return View(ap, blocks)


class Sched:
    ENGS = ("pe", "act", "dve", "pool", "sp")

    def __init__(self, nc, n_dsem=6):
        self.nc = nc
        self.q = {e: [] for e in self.ENGS}
        self.mem = {}
        self.n_dsem = n_dsem
        self.dma_count = {"sp": 0, "pool": 0, "act": 0}
        self.dsem_cnt = {}
        self.arena = nc.alloc_sbuf_tensor("arena", [128, SB_SIZE // 4], F32)
        assert nc.lookup_mloc(self.arena).addr == SB_BASE

    def add(self, eng, fn, reads=(), writes=(), dma=False):
        op = Op()
        op.eng = eng
        op.fn = fn
        op.is_dma = dma
        op.sig = dma
        op.seq = 0
        op.idx = len(self.q[eng])
        op.dsem = None
        op.dval = 0
        deps = {}
        mem = self.mem

        def dep(o, raw):
            if o is None or o is op:
                return
            k = id(o)
            if k in deps:
                deps[k] = (o, deps[k][1] or raw)
            else:
                deps[k] = (o, raw)

        rblocks = set()
        for v in reads:
            rblocks |= v.blocks
        wblocks = set()
        for v in writes:
            wblocks |= v.blocks
        for b in rblocks:
            st = mem.get(b)
            if st is not None:
                dep(st[0], True)
        for b in wblocks:
            st = mem.get(b)
            if st is not None:
                dep(st[0], False)
                for r in st[1]:
                    dep(r, False)
        for b in rblocks:
            st = mem.get(b)
            if st is None:
                mem[b] = [None, [op]]
            else:
                st[1].append(op)
        for b in wblocks:
            mem[b] = [op, []]
        best = {}
        final = []
        for (o, raw) in deps.values():
            if o.is_dma:
                final.append(o)
                continue
            if o.eng == eng and not dma:
                if eng == "pe":
                    continue
            cur = best.get(o.eng)
            if cur is None or o.idx > cur.idx:
                best[o.eng] = o
        final.extend(best.values())
        op.deps = final
        for o in final:
            o.sig = True
        if dma:
            k = self.dma_count[eng] % self.n_dsem
            self.dma_count[eng] += 1
            op.dsem = (eng, k)
            c = self.dsem_cnt.get(op.dsem, 0) + 16
            self.dsem_cnt[op.dsem] = c
            op.dval = c
        self.q[eng].append(op)
        return op

    def pe(self, fn, reads=(), writes=()):
        return self.add("pe", fn, reads, writes)

    def act(self, fn, reads=(), writes=()):
        return self.add("act", fn, reads, writes)

    def dve(self, fn, reads=(), writes=()):
        return self.add("dve", fn, reads, writes)

    def pool(self, fn, reads=(), writes=()):
        return self.add("pool", fn, reads, writes)

    def dma(self, eng, out_ap, in_ap, reads=(), writes=()):
        return self.add(eng, lambda e: e.dma_start(out=out_ap, in_=in_ap), reads, writes, dma=True)

    def emit(self, final_waits=()):
        nc = self.nc
        for o in final_waits:
            o.sig = True
        for e in self.ENGS:
            s = 0
            for op in self.q[e]:
                if op.sig and not op.is_dma:
                    s += 1
                    op.seq = s
        from contextlib import ExitStack
        with ExitStack() as es:
            esem = {e: es.enter_context(nc.semaphore("s_" + e)) for e in self.ENGS}
            dsem = {}
            for (q, k) in self.dsem_cnt:
                dsem[(q, k)] = es.enter_context(nc.semaphore("d_%s_%d" % (q, k)))
            block = es.enter_context(nc.Block())

            def run(ename, e):
                waited = {}
                for op in self.q[ename]:
                    for o in op.deps:
                        if o.is_dma:
                            sem, val, key = dsem[o.dsem], o.dval, o.dsem
                        else:
                            sem, val, key = esem[o.eng], o.seq, o.eng
                        if waited.get(key, 0) < val:
                            e.wait_ge(sem, val)
                            waited[key] = val
                    ins = op.fn(e)
                    if op.is_dma:
                        ins.then_inc(dsem[op.dsem], 16)
                    elif op.sig:
                        ins.then_inc(esem[ename], 1)
                if ename == "sp":
                    for o in final_waits:
                        if o.is_dma:
                            e.wait_ge(dsem[o.dsem], o.dval)
                        else:
                            e.wait_ge(esem[o.eng], o.seq)

            @block.tensor
            def _(e):
                run("pe", e)

            @block.scalar
            def _(e):
                run("act", e)

            @block.vector
            def _(e):
                run("dve", e)

            @block.gpsimd
            def _(e):
                run("pool", e)

            @block.sync
            def _(e):
                run("sp", e)


D = 1024
T = 2048
NTT = 4
DEPTH = 4
ALPHA = (2 * DEPTH) ** 0.25
LN_EPS = 1e-5
RMS_EPS = 1e-6
ATT_SCALE = 96 ** -0.5
NV = 200
V_BADA, V_CAW, V_CAB, V_LAG, V_LAB, V_CBW, V_QG, V_KVG, V_LG, V_LB = 0, 24, 148, 152, 156, 160, 172, 175, 177, 185
NGRP = 16
TWO_PI = 6.283185307179586


def _chunk_cols():
    ar = np.arange(128)
    ch = []
    for c in range(3):
        ch.append(3584 + 128 * c + ar)
    for c in range(2):
        ch.append(3968 + 128 * c + ar)
    kr = np.full(128, -1)
    kr[64:96] = 4224 + np.arange(32)
    ch.append(kr)
    krs = np.full(128, -1)
    krs[64:80] = 4240 + np.arange(16)
    krs[80:96] = 4224 + np.arange(16)
    ch.append(krs)
    for c in range(4):
        ch.append(4256 + 128 * c + ar)
    ch.append(np.full(128, -1))
    for base in (512, 0, 1024):
        for c in range(4):
            ch.append(base + 128 * c + ar)
    for base in (2560, 1536, 2048, 3072):
        for c in range(4):
            ch.append(base + 128 * c + ar)
    for j in range(8):
        for base in (4768, 5792, 6816):
            ch.append(base + 128 * j + ar)
    assert len(ch) == 64
    return np.concatenate(ch)


def _prep_weights(inp, layers):
    cols = _chunk_cols()
    L = len(layers)
    f = np.float32
    wada = np.empty((L, 6, 128, 8, 512), f)
    win = np.empty((L, NGRP, 128, 8, 512), f)
    wuq = np.empty((L, 128, 3, 768), f)
    wuqs = np.empty((L, 128, 3, 256), f)
    wukv = np.empty((L, 128, 2, 1024), f)
    wy = np.empty((L, 8, 128, 3, 4, 128), f)
    wo = np.empty((L, 2, 128, 8, 512), f)
    vecs = np.zeros((128, L, NV), f)
    for i, l in enumerate(layers):
        wada[i] = np.asarray(inp["w_ada"][l]).reshape(8, 128, 6, 512).transpose(2, 1, 0, 3)
        w = np.asarray(inp["w_in"][l])
        wp = np.zeros((1024, cols.size), f)
        m = cols >= 0
        wp[:, m] = w[:, cols[m]]
        win[i] = wp.reshape(8, 128, NGRP, 512).transpose(2, 1, 0, 3)
        uq = np.asarray(inp["w_uq"][l])
        wuq[i] = uq.reshape(3, 128, 768).transpose(1, 0, 2)
        sc = np.concatenate([np.concatenate([h * 96 + 80 + np.arange(16), h * 96 + 64 + np.arange(16)]) for h in range(8)])
        wuqs[i] = uq[:, sc].reshape(3, 128, 256).transpose(1, 0, 2)
        ukv = np.asarray(inp["w_ukv"][l])
        kc_ = np.concatenate([h * 128 + np.arange(64) for h in range(8)])
        vc_ = np.concatenate([h * 128 + 64 + np.arange(64) for h in range(8)])
        wukv[i] = np.concatenate([ukv[:, kc_], ukv[:, vc_]], axis=1).reshape(2, 128, 1024).transpose(1, 0, 2)
        for b, nm in enumerate(("w_a_out", "w_b_out", "w_c_out")):
            wx = np.asarray(inp[nm][l])
            wy[i, :, :, b] = wx.reshape(4, 128, 8, 128).transpose(2, 1, 0, 3)
        wo[i] = np.asarray(inp["w_o"][l]).reshape(8, 128, 2, 512).transpose(2, 1, 0, 3)
        v = vecs[:, i]
        v[:, V_BADA:V_BADA + 24] = np.asarray(inp["b_ada"][l]).reshape(24, 128).T
        v[:, V_CAW:V_CAW + 124] = np.asarray(inp["conv_a_w"][l]).reshape(31, 4, 128).transpose(2, 1, 0).reshape(128, 124)
        v[:, V_CAB:V_CAB + 4] = np.asarray(inp["conv_a_b"][l]).reshape(4, 128).T
        v[:, V_LAG:V_LAG + 4] = np.asarray(inp["ln_a_g"][l]).reshape(4, 128).T
        v[:, V_LAB:V_LAB + 4] = np.asarray(inp["ln_a_b"][l]).reshape(4, 128).T
        v[:, V_CBW:V_CBW + 12] = np.asarray(inp["conv_b_w"][l]).reshape(3, 4, 128).transpose(2, 1, 0).reshape(128, 12)
        v[:, V_QG:V_QG + 3] = np.asarray(inp["q_norm_g"][l]).reshape(3, 128).T
        v[:, V_KVG:V_KVG + 2] = np.asarray(inp["kv_norm_g"][l]).reshape(2, 128).T
        v[:, V_LG:V_LG + 8] = np.asarray(inp["ln_g"][l]).reshape(8, 128).T
        v[:, V_LB:V_LB + 8] = np.asarray(inp["ln_b"][l]).reshape(8, 128).T
    cst = np.zeros((128, 256), f)
    invc = np.zeros((128, 1), f)
    cst[:, 0:128] = np.eye(128, dtype=f)
    kk = np.arange(128)[:, None]
    qq = np.arange(128)[None, :]
    cst[:, 128:256] = np.where(kk > qq, -30000.0, 0.0)
    invf = (10000.0 ** (-np.arange(0, 32, 2, dtype=np.float32) / 32)).astype(f)
    for p in range(64, 96):
        invc[p, 0] = invf[(p - 64) % 16]
    return dict(invf=invc, wada=wada, win=win, wuq=wuq, wuqs=wuqs, wukv=wukv, wy=wy, wo=wo, vecs=vecs, cst=cst)


def build(L):
    nc = bass.Bass("TRN2", target_bir_lowering=False)

    def din(name, shape, dt=F32):
        return nc.dram_tensor(name, list(shape), dt, kind="ExternalInput").ap()

    x_d = din("x", [128, 8, T])
    c_d = din("cT", [128, 8])
    pos_d = din("pos", [128, T], I32)
    cst_d = din("cst", [128, 256])
    invf_d = din("invf", [128, 1])
    wada_d = din("wada", [L, 6, 128, 8, 512])
    win_d = din("win", [L, NGRP, 128, 8, 512])
    wuq_d = din("wuq", [L, 128, 3, 768])
    wuqs_d = din("wuqs", [L, 128, 3, 256])
    wukv_d = din("wukv", [L, 128, 2, 1024])
    wy_d = din("wy", [L, 8, 128, 3, 4, 128])
    wo_d = din("wo", [L, 2, 128, 8, 512])
    vecs_d = din("vecs", [128, L, NV])
    y_d = nc.dram_tensor("y", [128, 8, T], F32, kind="ExternalOutput").ap()

    S = Sched(nc)

    def sb(name, shape, dt, off):
        b = Buf(S, name, shape, dt, offset=off)
        assert off + b.nbytes <= SB_SIZE, name
        return b

    xT = sb("xT", [8, T], F32, 0)
    O_SM = 65536
    ident = sb("ident", [128], BF16, O_SM)
    maskn = sb("maskn", [128], BF16, O_SM + 256)
    ones_bf = sb("ones_bf", [128], BF16, O_SM + 512)
    onesf = sb("onesf", [64], F32, O_SM + 768)
    invf = sb("invf", [1], F32, O_SM + 1024)
    cact = sb("cact", [8], F32, O_SM + 1088)
    cact_bf = sb("cact_bf", [8], BF16, O_SM + 1152)
    ada = sb("ada", [24], F32, O_SM + 1216)
    gq = sb("gq", [8], F32, O_SM + 1344)
    gatea = sb("gatea", [8], F32, O_SM + 1408)
    one1 = sb("one1", [1], F32, O_SM + 1472)
    epsv = sb("epsv", [4], F32, O_SM + 1504)
    vecs = sb("vecs", [L, NV], F32, O_SM + 1536)
    expo = sb("expo", [512], F32, O_SM + 1536 + 3200)
    adarow = sb("adarow", [512], F32, O_SM + 6784)
    ada_t = [ada, sb("ada1", [24], F32, O_SM + 8832)]
    gq_t = [gq, sb("gq1", [8], F32, O_SM + 8928)]
    gatea_t = [gatea, sb("gatea1", [8], F32, O_SM + 8960)]
    Gp = sb("Gp", [8], F32, O_SM + 8992)
    Bp = sb("Bp", [8], F32, O_SM + 9024)
    O_FIN = O_SM + 9216
    c_fin = sb("c_fin", [4, T], BF16, O_FIN)
    a_fin = sb("a_fin", [4, T], BF16, O_FIN + 16384)
    b_fin = sb("b_fin", [4, T], BF16, O_FIN + 32768)
    O_U = O_FIN + 49152
    uT = sb("uT", [8, T], BF16, O_U)
    O_S = O_U + 32768
    S_SIZE = SB_SIZE - O_S
    NSLOT = 2
    wbuf = [sb("wbuf%d" % i, [8, 512], BF16, O_S + 8192 * i) for i in range(NSLOT)]
    SS = O_S + 8192 * NSLOT
    ps = Buf(S, "ps", [8, 512], F32, space="ps")

    gcs = sb("gcs", [4, T], BF16, O_FIN + 16384)
    qn = sb("qn", [3, T], BF16, O_FIN + 32768)
    kpe = sb("kpe", [T], BF16, O_FIN + 32768 + 12288)
    kvn = sb("kvn", [2, T], BF16, SS)
    sqb = sb("sqb", [5, 512], BF16, SS + 8192)
    rq = sb("rq", [512], F32, SS + 13312)
    rkv = sb("rkv", [512], F32, SS + 15360)
    cosT = sb("cosT", [T], F32, SS + 17408)
    sinT = sb("sinT", [T], F32, SS + 25600)
    rt1 = sb("rt1", [512], F32, SS + 33792)
    rt2 = sb("rt2", [512], F32, SS + 35840)
    wuq_sb = sb("wuq_sb", [3, 768], BF16, SS + 8192)
    wukv_sb = sb("wukv_sb", [2, 1024], BF16, SS + 12800)
    wuqs_sb = sb("wuqs_sb", [3, 256], BF16, SS + 37888)
    tang = sb("tang", [T], F32, O_FIN)
    ttmp = sb("ttmp", [T], F32, O_FIN + 8192)
    tki = sb("tki", [T], I32, O_FIN + 8192)
    qTb = [sb("qT%d" % i, [T], BF16, O_U + 4096 * i) for i in range(2)]
    kTb = [sb("kT%d" % i, [T], BF16, O_U + 8192 + 4096 * i) for i in range(2)]
    Vp = sb("Vp", [16, 192], BF16, O_U + 16384)
    rs = sb("rs", [512], F32, O_U + 22528)
    Msb = sb("Msb", [512], F32, O_U + 24576)
    pbuf = [sb("pbuf%d" % i, [512], BF16, O_U + 26624 + 1024 * i) for i in range(4)]
    a_pad = sb("a_pad", [4, 2080], BF16, SS)
    ac = sb("ac", [4, 1024], F32, SS + 16896)
    acbs = [sb("acb%d" % i, [1, 1024], BF16, O_FIN + 32768 + 8192 + 2048 * i) for i in range(2)]
    sqas = [sb("sqa%d" % i, [1, 1024], BF16, O_FIN + 32768 + 12288 + 2048 * i) for i in range(2)]
    mean_a = sb("mean_a", [1024], F32, O_FIN + 32768)
    rstd_a = sb("rstd_a", [1024], F32, O_FIN + 32768 + 4096)
    diag = [sb("diag%d" % i, [128], BF16, SS + 33280 + 256 * i) for i in range(16)]
    p_pad = sb("p_pad", [4, 2112], BF16, SS)
    gtmp = sb("gtmp", [T], BF16, O_FIN + 32768 + 8192)
    stmp = sb("stmp", [T], BF16, SS + 20992)
    diagb = [sb("diagb%d" % i, [128], BF16, SS + 25088 + 256 * i) for i in range(8)]
    mT = sb("mT", [8, 1024], BF16, SS)
    wyb = [sb("wy%d" % i, [3, 4, 128], BF16, SS + 16384 + 3072 * i) for i in range(2)]
    wo_h = [sb("woA", [8, 512], BF16, SS + 16384), sb("woB", [8, 512], BF16, SS + 26624)]
    sg = [sb("sg%d" % i, [512], F32, SS + 26624 + 2048 * i) for i in range(3)]
    zb = [sb("zb%d" % i, [512], BF16, SS + 34816 + 1024 * i) for i in range(2)]
    zq = [sb("zq%d" % i, [512], BF16, SS + 36864 + 1024 * i) for i in range(2)]
    msq = sb("msq", [512], F32, SS + 24576)
    assert SS + 39424 <= SB_SIZE

    bank_rr = [0]

    def nb(pool_=(0, 1, 2, 3, 4, 5, 6, 7)):
        b = pool_[bank_rr[0] % len(pool_)]
        bank_rr[0] += 1
        return b

    def mm(out, lhsT, rhs, start, stop):
        S.pe(lambda e: e.matmul(out.ap, lhsT=lhsT.ap, rhs=rhs.ap, start=start, stop=stop),
             reads=[lhsT, rhs], writes=[out])

    def actf(out, in_, func, bias=None, scale=None, extra_reads=()):
        kw = {}
        if bias is not None:
            kw["bias"] = bias.ap if isinstance(bias, View) else bias
        if scale is not None:
            kw["scale"] = scale.ap if isinstance(scale, View) else scale
        rd = [in_] + [v for v in (bias, scale) if isinstance(v, View)] + list(extra_reads)
        S.act(lambda e: e.activation(out=out.ap, in_=in_.ap, func=func, **kw), reads=rd, writes=[out])

    def tt_(eng, out, a, b, op):
        S.add(eng, lambda e: e.tensor_tensor(out=out.ap, in0=a.ap, in1=b.ap, op=op), reads=[a, b], writes=[out])

    def ts_(eng, out, a, s1, s2, op0, op1=None):
        rd = [a] + [v for v in (s1, s2) if isinstance(v, View)]
        s1a = s1.ap if isinstance(s1, View) else s1
        s2a = s2.ap if isinstance(s2, View) else s2
        if op1 is None:
            S.add(eng, lambda e: e.tensor_scalar(out=out.ap, in0=a.ap, scalar1=s1a, scalar2=None, op0=op0), reads=rd, writes=[out])
        else:
            S.add(eng, lambda e: e.tensor_scalar(out=out.ap, in0=a.ap, scalar1=s1a, scalar2=s2a, op0=op0, op1=op1), reads=rd, writes=[out])

    def stt(out, a, s, b, op0, op1):
        rd = [a, b] + ([s] if isinstance(s, View) else [])
        sa = s.ap if isinstance(s, View) else s
        S.dve(lambda e: e.scalar_tensor_tensor(out=out.ap, in0=a.ap, scalar=sa, in1=b.ap, op0=op0, op1=op1), reads=rd, writes=[out])

    def cp(eng, out, in_):
        S.add(eng, lambda e: e.tensor_copy(out=out.ap, in_=in_.ap), reads=[in_], writes=[out])

    def rsqrt_eps(out, in_, eps_col, pw=-0.5):
        if eps_col is None:
            actf(out, in_, AF.Ln)
        else:
            actf(out, in_, AF.Ln, bias=epsv[in_part(out), eps_col:eps_col + 1])
        actf(out, out, AF.Exp, scale=pw)

    def in_part(v):
        return slice(None)

    def ps4(bs):
        return ps[:, bs:bs + 4, :].re("p a b -> p (a b)")

    def ps2(bs):
        return ps[:, bs:bs + 2, :].re("p a b -> p (a b)")

    bset_c = [0]

    def bset():
        bset_c[0] += 1
        return (bset_c[0] % 2) * 4

    stream = []
    for g in range(6):
        stream.append(wada_d[0, g])
    for l in range(L):
        for g in range(16):
            stream.append(win_d[l, g])
        if l + 1 < L:
            for g in range(6):
                stream.append(wada_d[l + 1, g])
        for g in range(10, 16):
            stream.append(win_d[l, g])
    issued = [0]

    def grp(si):
        while issued[0] < min(len(stream), si + 1):
            k = issued[0]
            b = wbuf[k % NSLOT]
            S.dma("pool", b[:].ap, stream[k], writes=[b[:]])
            issued[0] += 1
        return wbuf[si % NSLOT]

    spos = [0]

    S.dma("sp", xT[:].ap, x_d, writes=[xT[:]])
    S.dma("sp", cact[:].ap, c_d, writes=[cact[:]])
    S.dma("sp", vecs[:].ap, vecs_d, writes=[vecs[:]])
    S.dma("sp", invf[:].ap, invf_d, writes=[invf[:]])
    S.dma("pool", ident[:].ap, cst_d[:, 0:128], writes=[ident[:]])
    S.dma("pool", maskn[:].ap, cst_d[:, 128:256], writes=[maskn[:]])
    grp(1)
    S.pool(lambda e: e.memset(ones_bf[:].ap, 1.0), writes=[ones_bf[:]])
    S.pool(lambda e: e.memset(onesf[:].ap, 1.0), writes=[onesf[:]])
    S.pool(lambda e: e.memset(one1[:].ap, 1.0), writes=[one1[:]])
    S.pool(lambda e: e.memset(epsv[:, 0:1].ap, 384 * RMS_EPS), writes=[epsv[:]])
    S.pool(lambda e: e.memset(epsv[:, 1:2].ap, 256 * RMS_EPS), writes=[epsv[:]])
    S.pool(lambda e: e.memset(epsv[:, 2:3].ap, LN_EPS), writes=[epsv[:]])
    S.pool(lambda e: e.memset(epsv[:, 3:4].ap, LN_EPS / (ALPHA * ALPHA)), writes=[epsv[:]])
    actf(cact[:], cact[:], AF.Silu)
    cp("dve", cact_bf[:], cact[:])

    def vcol(l, idx, n=1):
        return vecs[:, l, idx:idx + n]

    def ada_steps(l):
        ada, gq, gatea = ada_t[l % 2], gq_t[l % 2], gatea_t[l % 2]
        steps = []
        for g in range(6):
            def st(g=g):
                wg = grp(spos[0])
                grp(spos[0] + 1)
                for kc in range(8):
                    mm(ps[0:1, 6, :], cact_bf[:, kc:kc + 1], wg[:, kc, :], kc == 0, kc == 7)
                spos[0] += 1
                cp("dve", adarow[0:1, :], ps[0:1, 6, :])
                for jj in range(4):
                    mm(ps[:, 7, g * 4 + jj: g * 4 + jj + 1], adarow[0:1, jj * 128:(jj + 1) * 128], one1[0:1, :], True, True)
            steps.append(st)

        def fin():
            tt_("dve", ada[:, 0:24], ps[:, 7, 0:24], vcol(l, V_BADA, 24), ALU.add)
            ts_("dve", ada[:, 8:16], ada[:, 8:16], 1.0, None, ALU.add)
            ts_("dve", gatea[:], ada[:, 16:24], 1.0 / ALPHA, None, ALU.mult)
            ts_("dve", gq[:, 0:3], vcol(l, V_QG, 3), float(384 ** 0.5), None, ALU.mult)
            ts_("dve", gq[:, 3:5], vcol(l, V_KVG, 2), float(256 ** 0.5), None, ALU.mult)
            if l >= 1:
                tt_("dve", Gp[:], vcol(l - 1, V_LG, 8), ada[:, 8:16], ALU.mult)
                tt_("dve", Bp[:], vcol(l - 1, V_LB, 8), ada[:, 8:16], ALU.mult)
                tt_("dve", Bp[:], Bp[:], ada[:, 0:8], ALU.add)
        steps.append(fin)
        return steps

    def ada_compute(l):
        for st in ada_steps(l):
            st()

    lnq = []

    def drain(n=10 ** 9):
        while lnq and n > 0:
            lnq.pop(0)()
            n -= 1

    def ln_pieces(l, tts, fuse_u):
        b1, b2 = 6, 7
        for tt in tts:
            ts = slice(tt * 512, (tt + 1) * 512)

            def stat_mm(c):
                mm(ps[:, b1, :], ones_bf[:], zb[c % 2][:], c == 0, c == 7)
                mm(ps[:, b2, :], ones_bf[:], zq[c % 2][:], c == 0, c == 7)

            for c in range(8):
                def p1(c=c, ts=ts, stat_mm=stat_mm):
                    cp("dve", zb[c % 2][:], xT[:, c, ts])
                    actf(zq[c % 2][:], xT[:, c, ts], AF.Square)
                    if c > 0:
                        stat_mm(c - 1)
                lnq.append(p1)

            def chain(stat_mm=stat_mm):
                stat_mm(7)
                actf(msq[:], ps[:, b1, :], AF.Square, scale=1.0 / D)
                ts_("dve", ps[:, b1, :], ps[:, b1, :], 1.0 / D, None, ALU.mult)
                stt(ps[:, b2, :], ps[:, b2, :], 1.0 / D, msq[:], ALU.mult, ALU.subtract)
                rsqrt_eps(ps[:, b2, :], ps[:, b2, :], 3)
            lnq.append(chain)
            for c in range(8):
                def p2(c=c, ts=ts):
                    tt_("dve", xT[:, c, ts], xT[:, c, ts], ps[:, b1, :], ALU.subtract)
                    tt_("dve", xT[:, c, ts], xT[:, c, ts], ps[:, b2, :], ALU.mult)
                    if fuse_u:
                        actf(uT[:, c, ts], xT[:, c, ts], AF.Identity, bias=Bp[:, c:c + 1], scale=Gp[:, c:c + 1])
                    actf(xT[:, c, ts], xT[:, c, ts], AF.Identity, bias=vcol(l, V_LB + c), scale=vcol(l, V_LG + c))
                lnq.append(p2)

    def layer(l):
        ada, gq, gatea = ada_t[l % 2], gq_t[l % 2], gatea_t[l % 2]

        def make_u():
            for c in range(8):
                if c % 2 == 0:
                    actf(uT[:, c, :], xT[:, c, :], AF.Identity, bias=ada[:, c:c + 1], scale=ada[:, 8 + c:9 + c])
                else:
                    ts_("dve", uT[:, c, :], xT[:, c, :], ada[:, 8 + c:9 + c], ada[:, c:c + 1], ALU.mult, ALU.add)

        if l == 0:
            make_u()
        R = slice(64, 96)
        g0 = spos[0]

        def wchunk(ci):
            gi = g0 + ci // 4
            wgb = grp(gi)
            return wgb, (ci % 4) * 128

        ln_busy = bool(lnq)
        for tt in range(NTT):
            ts = slice(tt * 512, (tt + 1) * 512)
            banks = []
            for ci in range(5):
                wgb, co = wchunk(ci)
                b = nb((0, 1, 2, 3, 4)) if ln_busy else nb((0, 1, 2, 3, 4, 7))
                banks.append(b)
                for kc in range(8):
                    mm(ps[:, b, :], wgb[:, kc, co:co + 128], uT[:, kc, ts], kc == 0, kc == 7)
                actf(sqb[:, ci, :], ps[:, b, :], AF.Square)
                drain(4)
            if tt == 1:
                drain()
            for (lo, hi, sbank, rbuf, n) in ((0, 3, 5, rq, 384), (3, 5, 5 if ln_busy else 6, rkv, 256)):
                for ci in range(lo, hi):
                    mm(ps[:, sbank, :], ones_bf[:], sqb[:, ci, :], ci == lo, ci == hi - 1)
                rsqrt_eps(rbuf[:], ps[:, sbank, :], 0 if n == 384 else 1)
                for ci in range(lo, hi):
                    dst = qn[:, ci, ts] if ci < 3 else kvn[:, ci - 3, ts]
                    stt(dst, ps[:, banks[ci], :], gq[:, ci:ci + 1], rbuf[:], ALU.mult, ALU.mult)
            if tt == 1:
                ln_busy = False
        S.dma("pool", wuq_sb[:].ap, wuq_d[l], writes=[wuq_sb[:]])
        S.dma("pool", wuqs_sb[:].ap, wuqs_d[l], writes=[wuqs_sb[:]])
        S.dma("pool", wukv_sb[:].ap, wukv_d[l], writes=[wukv_sb[:]])
        S.dma("sp", tki[R, :].ap, pos_d[64:96, :], writes=[tki[R, :]])
        cp("dve", tang[R, :], tki[R, :])
        ts_("dve", tang[R, :], tang[R, :], invf[R, :], None, ALU.mult)
        ts_("dve", ttmp[R, :], tang[R, :], 1.0 / TWO_PI, None, ALU.mult)
        cp("dve", tki[R, :], ttmp[R, :])
        cp("dve", ttmp[R, :], tki[R, :])
        C1 = 6.28125
        C2 = TWO_PI - C1
        stt(tang[R, :], ttmp[R, :], -C1, tang[R, :], ALU.mult, ALU.add)
        stt(tang[R, :], ttmp[R, :], -C2, tang[R, :], ALU.mult, ALU.add)
        ts_("dve", ttmp[R, :], tang[R, :], float(np.pi), None, ALU.is_gt)
        stt(tang[R, :], ttmp[R, :], -TWO_PI, tang[R, :], ALU.mult, ALU.add)
        ts_("dve", ttmp[R, :], tang[R, :], float(-np.pi), None, ALU.is_lt)
        stt(tang[R, :], ttmp[R, :], TWO_PI, tang[R, :], ALU.mult, ALU.add)
        ts_("dve", tang[R, :], tang[R, :], 3.1415925, -3.1415925, ALU.min, ALU.max)
        actf(sinT[R, :], tang[R, :], AF.Sin)
        ts_("dve", ttmp[R, :], tang[R, :], float(np.pi / 2), None, ALU.is_gt)
        stt(tang[R, :], ttmp[R, :], -TWO_PI, tang[R, :], ALU.mult, ALU.add)
        ts_("dve", tang[R, :], tang[R, :], float(np.pi / 2), None, ALU.add)
        ts_("dve", tang[R, :], tang[R, :], 3.1415925, -3.1415925, ALU.min, ALU.max)
        actf(cosT[R, :], tang[R, :], AF.Sin)
        ts_("dve", sinT[64:80, :], sinT[64:80, :], -1.0, None, ALU.mult)

        grp(g0 + 2)
        for c in range(4):
            wgb, co = wchunk(7 + c)
            bs = (c % 2) * 4
            for tt in range(NTT):
                ts = slice(tt * 512, (tt + 1) * 512)
                for kc in range(8):
                    mm(ps[:, bs + tt, :], wgb[:, kc, co:co + 128], uT[:, kc, ts], kc == 0, kc == 7)
            actf(gcs[:, c, :], ps4(bs), AF.Silu)
        for tt in range(NTT):
            ts = slice(tt * 512, (tt + 1) * 512)
            bk = []
            for ci in (5, 6):
                wgb, co = wchunk(ci)
                b = nb((0, 1, 2, 3, 4))
                bk.append(b)
                for kc in range(8):
                    mm(ps[0:96, b, :], wgb[:, kc, co:co + 96], uT[:, kc, ts], kc == 0, kc == 7)
            tt_("dve", rt1[R, :], ps[R, bk[0], :], cosT[R, ts], ALU.mult)
            tt_("dve", rt2[R, :], ps[R, bk[1], :], sinT[R, ts], ALU.mult)
            tt_("dve", kpe[R, ts], rt1[R, :], rt2[R, :], ALU.add)
        spos[0] = g0 + 3
        grp(g0 + 3)
        grp(g0 + 4)

        for qb_ in qTb + kTb:
            S.pool(lambda e, qb_=qb_: e.memset(qb_[96:128, :].ap, 0.0), writes=[qb_[96:128, :]])
        S.pool(lambda e: e.memset(Vp[:, :, 64:128].ap, 0.0), writes=[Vp[:]])
        S.pool(lambda e: e.memset(Vp[:, :, 64:65].ap, 1.0), writes=[Vp[:]])
        AP_ = (2, 3, 4, 5, 6, 7)

        def prod_qk(h):
            qT, kT = qTb[h % 2], kTb[h % 2]
            for tt in range(NTT):
                ts = slice(tt * 512, (tt + 1) * 512)
                b = nb(AP_)
                for kc in range(2):
                    mm(ps[0:64, b, :], wukv_sb[:, kc, h * 64:(h + 1) * 64], kvn[:, kc, ts], kc == 0, kc == 1)
                cp("dve", kT[0:64, ts], ps[0:64, b, :])
                bq = nb(AP_)
                for kc in range(3):
                    mm(ps[0:96, bq, :], wuq_sb[:, kc, h * 96:(h + 1) * 96], qn[:, kc, ts], kc == 0, kc == 2)
                bs_ = nb(AP_)
                for kc in range(3):
                    mm(ps[64:96, bs_, :], wuqs_sb[:, kc, h * 32:(h + 1) * 32], qn[:, kc, ts], kc == 0, kc == 2)
                cp("dve", qT[0:64, ts], ps[0:64, bq, :])
                tt_("dve", rt1[R, :], ps[R, bq, :], cosT[R, ts], ALU.mult)
                tt_("dve", rt2[R, :], ps[R, bs_, :], sinT[R, ts], ALU.mult)
                tt_("dve", qT[R, ts], rt1[R, :], rt2[R, :], ALU.add)
            cp("dve", kT[R, :], kpe[R, :])

        def prod_v(j):
            for g4 in range(4):
                b = nb(AP_)
                for k4 in range(4):
                    kt = g4 * 4 + k4
                    for kc in range(2):
                        mm(ps[:, b, k4 * 128:(k4 + 1) * 128], kvn[:, kc, kt * 128:(kt + 1) * 128],
                           wukv_sb[:, kc, 512 + j * 128:512 + (j + 1) * 128], kc == 0, kc == 1)
                pv = ps[:, b, :].re("p (k c) -> p k c", c=128)
                S.dve(lambda e, pv=pv, g4=g4: e.tensor_copy(out=Vp[:, g4 * 4:(g4 + 1) * 4, 0:64].ap, in_=pv.ap[:, :, 0:64]),
                      reads=[pv], writes=[Vp[:, g4 * 4:(g4 + 1) * 4, :]])
                S.dve(lambda e, pv=pv, g4=g4: e.tensor_copy(out=Vp[:, g4 * 4:(g4 + 1) * 4, 128:192].ap, in_=pv.ap[:, :, 64:128]),
                      reads=[pv], writes=[Vp[:, g4 * 4:(g4 + 1) * 4, :]])

        def attn(h):
            j, odd = h // 2, h % 2
            qT, kT = qTb[h % 2], kTb[h % 2]
            if not odd:
                sr, orows = slice(64, 65), slice(0, 64)
            else:
                sr, orows = slice(0, 1), slice(64, 128)
            steps = []
            for tt in range(NTT):
                nk = 4 * tt + 4
                for kt in range(nk):
                    q0 = max(tt * 512, kt * 128)
                    steps.append((tt, kt, q0, (tt + 1) * 512 - q0, kt >= 4 * tt, kt == 0, kt == nk - 1))
            LA = 3
            sb_ = {}
            ns = len(steps)

            def s_mm(i):
                tt, kt, q0, n, dg, first, last = steps[i]
                b = nb(AP_[0:5])
                sb_[i] = b
                mm(ps[:, b, 0:n], kT[:, kt * 128:(kt + 1) * 128], qT[:, q0:q0 + n], True, not dg)
                if dg:
                    mm(ps[:, b, 0:128], ident[:], maskn[:], False, True)

            for i in range(min(LA, ns)):
                s_mm(i)
            for i in range(ns):
                if i + LA < ns:
                    s_mm(i + LA)
                tt, kt, q0, n, dg, first, last = steps[i]
                ob = tt % 2
                pb = pbuf[i % 4]
                actf(pb[:, 0:n], ps[:, sb_[i], 0:n], AF.Exp, scale=ATT_SCALE)
                o0 = q0 - tt * 512
                if not odd:
                    mm(ps[:, ob, o0:512], Vp[:, kt, 0:128], pb[:, 0:n], first, last)
                else:
                    mm(ps[:, ob, o0:512], Vp[:, kt, 64:192], pb[:, 0:n], first, last)
                for pn in list(pend_norm):
                    pn[0] -= 1
                    if pn[0] <= 0:
                        pend_norm.remove(pn)
                        pn[1]()
                if last:
                    ts = slice(tt * 512, (tt + 1) * 512)
                    actf(rs[sr, :], ps[sr, ob, :], AF.Ln)
                    actf(rs[sr, :], rs[sr, :], AF.Exp, scale=-1.0)

                    def norm_b(sr=sr, orows=orows, ob=ob, j=j, ts=ts):
                        mm(ps[orows, 7, :], onesf[sr, 0:64], rs[sr, :], True, True)
                        tt_("dve", Msb[orows, :], ps[orows, 7, :], gcs[orows, j, ts], ALU.mult)
                        tt_("dve", c_fin[orows, j, ts], ps[orows, ob, :], Msb[orows, :], ALU.mult)
                    pend_norm.append([3, norm_b])

        pend_norm = []
        prod_qk(0)
        for h in range(8):
            if h % 2 == 0:
                prod_v(h // 2)
            if h + 1 < 8:
                prod_qk(h + 1)
            attn(h)
        while pend_norm:
            pend_norm.pop(0)[1]()

        make_u()
        gA = spos[0]

        def wchunkA(ci):
            gi = gA + ci // 4
            return grp(gi), (ci % 4) * 128

        S.pool(lambda e: e.memset(a_pad[:, :, 0:32].ap, 0.0), writes=[a_pad[:, :, 0:32]])
        PADA = 32
        for c in range(4):
            wgb, co = wchunkA(c)
            bs = (c % 2) * 4
            for tt in range(NTT):
                for kc in range(8):
                    mm(ps[:, bs + tt, :], wgb[:, kc, co:co + 128], uT[:, kc, tt * 512:(tt + 1) * 512], kc == 0, kc == 7)
            actf(a_fin[:, c, :], ps4(bs), AF.Sigmoid)
        grp(gA + 2)
        for c in range(4):
            wgb, co = wchunkA(4 + c)
            bs = (c % 2) * 4
            for tt in range(NTT):
                for kc in range(8):
                    mm(ps[:, bs + tt, :], wgb[:, kc, co:co + 128], uT[:, kc, tt * 512:(tt + 1) * 512], kc == 0, kc == 7)
            tt_("dve", a_pad[:, c, PADA:PADA + T], ps4(bs), a_fin[:, c, :], ALU.mult)
        grp(gA + 3)
        for c in range(4):
            wgb, co = wchunkA(8 + c)
            bs = (c % 2) * 4
            for tt in range(NTT):
                for kc in range(8):
                    mm(ps[:, bs + tt, :], wgb[:, kc, co:co + 128], uT[:, kc, tt * 512:(tt + 1) * 512], kc == 0, kc == 7)
            actf(a_fin[:, c, :], ps4(bs), AF.Silu)
        spos[0] = gA + 3
        grp(gA + 4)
        dcnt = [0]
        pend_stats = []
        deferred = []
        ac2 = ac

        def emit_stats(c):
            for t2 in range(2):
                mm(ps[:, 4 + t2, :], ones_bf[:], acbs[c % 2][:, 0, t2 * 512:(t2 + 1) * 512], c == 0, c == 3)
                mm(ps[:, 6 + t2, :], ones_bf[:], sqas[c % 2][:, 0, t2 * 512:(t2 + 1) * 512], c == 0, c == 3)

        for hf in range(2):
            for c in range(4):
                bs = (c % 2) * 2
                acb, sqa = acbs[c % 2], sqas[c % 2]
                for k in range(31):
                    dg = diag[dcnt[0] % 16]
                    dcnt[0] += 1
                    ts_("dve", dg[:], ident[:], vcol(l, V_CAW + c * 31 + k), None, ALU.mult)
                    for t2 in range(2):
                        t0 = (hf * 2 + t2) * 512 + 2 + k
                        mm(ps[:, bs + t2, :], dg[:], a_pad[:, c, t0:t0 + 512], k == 0, k == 30)
                pc = ps2(bs)
                if deferred:
                    deferred.pop(0)()
                actf(ac[:, c, :], pc, AF.Identity, bias=vcol(l, V_CAB + c))
                actf(sqa[:, 0, :], pc, AF.Square, bias=vcol(l, V_CAB + c))
                actf(acb[:, 0, :], pc, AF.Identity, bias=vcol(l, V_CAB + c))
                pend_stats.append(c)
                if c > 0:
                    emit_stats(pend_stats.pop(0))
            emit_stats(pend_stats.pop(0))
            ts_("dve", mean_a[:], ps2(4), 1.0 / 512, None, ALU.mult)
            tt_("dve", rstd_a[:], mean_a[:], mean_a[:], ALU.mult)
            stt(rstd_a[:], ps2(6), 1.0 / 512, rstd_a[:], ALU.mult, ALU.subtract)
            rsqrt_eps(rstd_a[:], rstd_a[:], 2)
            hs_ = slice(hf * 1024, (hf + 1) * 1024)
            for c in range(4):
                def norm_piece(c=c, hs=hs_):
                    tt_("dve", ac2[:, c, :], ac2[:, c, :], mean_a[:], ALU.subtract)
                    stt(ac2[:, c, :], ac2[:, c, :], vcol(l, V_LAG + c), rstd_a[:], ALU.mult, ALU.mult)
                    actf(ac2[:, c, :], ac2[:, c, :], AF.Silu, bias=vcol(l, V_LAB + c))
                    tt_("dve", a_fin[:, c, hs], ac2[:, c, :], a_fin[:, c, hs], ALU.mult)
                deferred.append(norm_piece)

        gB = spos[0]

        def wchunkB(ci):
            gi = gB + ci // 4
            return grp(gi), (ci % 4) * 128

        S.pool(lambda e: e.memset(p_pad[:, :, 0:32].ap, 0.0), writes=[p_pad[:, :, 0:32]])

        def proj4(ci):
            wgb, co = wchunkB(ci)
            bs = bset()
            for tt in range(NTT):
                for kc in range(8):
                    mm(ps[:, bs + tt, :], wgb[:, kc, co:co + 128], uT[:, kc, tt * 512:(tt + 1) * 512], kc == 0, kc == 7)
            return ps4(bs)

        for c in range(4):
            pg = proj4(c)
            if deferred:
                deferred.pop(0)()
            actf(gtmp[:], pg, AF.Copy)
            px = proj4(4 + c)
            tt_("dve", p_pad[:, c, 32:32 + T], px, gtmp[:], ALU.mult)
        grp(gB + 2)
        grp(gB + 3)
        for c in range(4):
            pgb = proj4(8 + c)
            actf(b_fin[:, c, :], pgb, AF.Copy)
        grp(gB + 4)
        for c in range(4):
            pbg = proj4(12 + c)
            actf(stmp[:], pbg, AF.Silu)
            tt_("dve", b_fin[:, c, :], b_fin[:, c, :], stmp[:], ALU.mult)
        spos[0] = gB + 4
        grp(gB + 5)
        dcb = [0]
        for c in range(4):
            bs = (c % 2) * 4
            for k in range(3):
                dg = diagb[dcb[0] % 8]
                dcb[0] += 1
                ts_("dve", dg[:], ident[:], vcol(l, V_CBW + c * 3 + k), None, ALU.mult)
                for tt in range(NTT):
                    t0 = tt * 512 + 30 + k
                    mm(ps[:, bs + tt, :], dg[:], p_pad[:, c, t0:t0 + 512], k == 0, k == 2)
            tt_("dve", b_fin[:, c, :], ps4(bs), b_fin[:, c, :], ALU.mult)

        for hf in range(2):
            gH = spos[0]
            pool_ = (0, 1, 2, 3, 4, 5, 6, 7) if hf == 0 else (0, 1, 2, 3, 4, 5)
            for j in range(8):
                wb_ = wyb[j % 2]
                S.dma("pool", wb_[:].ap, wy_d[l, j], writes=[wb_[:]])
                for t2 in range(2):
                    tt = hf * 2 + t2
                    ts = slice(tt * 512, (tt + 1) * 512)
                    for br in range(3):
                        ci = j * 3 + br
                        wgb = grp(gH + ci // 4)
                        co = (ci % 4) * 128
                        b = nb(pool_)
                        for kc in range(8):
                            mm(ps[:, b, :], wgb[:, kc, co:co + 128], uT[:, kc, ts], kc == 0, kc == 7)
                        actf(sg[br][:], ps[:, b, :], AF.Sigmoid)
                    yb_ = []
                    for br, fin in enumerate((a_fin, b_fin, c_fin)):
                        b = nb(pool_)
                        yb_.append(b)
                        for kc in range(4):
                            mm(ps[:, b, :], wb_[:, br, kc, :], fin[:, kc, ts], kc == 0, kc == 3)
                    tt_("dve", sg[0][:], ps[:, yb_[0], :], sg[0][:], ALU.mult)
                    tt_("dve", sg[1][:], ps[:, yb_[1], :], sg[1][:], ALU.mult)
                    tt_("dve", sg[0][:], sg[0][:], sg[1][:], ALU.add)
                    tt_("dve", sg[2][:], ps[:, yb_[2], :], sg[2][:], ALU.mult)
                    tt_("dve", mT[:, j, t2 * 512:(t2 + 1) * 512], sg[0][:], sg[2][:], ALU.add)
                    drain(3)
                nxt = gH + ((j + 1) * 3 + 2) // 4 if j < 7 else gH + 6
                grp(nxt)
            drain()
            spos[0] = gH + 6
            ada_q = ada_steps(l + 1) if (hf == 0 and l + 1 < L) else []
            for eh in range(2):
                S.dma("pool", wo_h[eh][:].ap, wo_d[l, eh], writes=[wo_h[eh][:]])
            for t2 in range(2):
                tt = hf * 2 + t2
                ts = slice(tt * 512, (tt + 1) * 512)
                for e_ in range(8):
                    wo_ = wo_h[e_ // 4]
                    eo = (e_ % 4) * 128
                    b = nb((0, 1, 2, 3, 4, 5))
                    for kc in range(8):
                        mm(ps[:, b, :], wo_[:, kc, eo:eo + 128], mT[:, kc, t2 * 512:(t2 + 1) * 512], kc == 0, kc == 7)
                    stt(xT[:, e_, ts], ps[:, b, :], gatea[:, e_:e_ + 1], xT[:, e_, ts], ALU.mult, ALU.add)
                    if ada_q:
                        ada_q.pop(0)()
                    drain(2)
                while ada_q:
                    ada_q.pop(0)()
                ln_pieces(l, (tt,), l + 1 < L)
        if l + 1 == L:
            drain()

    ada_compute(0)
    for l in range(L):
        layer(l)
    outs = []
    for c in range(8):
        outs.append(S.dma("sp", y_d[:, c, :], xT[:, c, :].ap, reads=[xT[:, c, :]]))
    S.emit(final_waits=outs)
    return nc


_NC_CACHE = {}


def _get_nc(L):
    if L not in _NC_CACHE:
        _NC_CACHE[L] = build(L)
    return _NC_CACHE[L]


def _core_inputs(x_b, c_b, pos_b):
    xt = np.ascontiguousarray(x_b.T.reshape(8, 128, T).transpose(1, 0, 2))
    ct = np.ascontiguousarray(c_b.reshape(8, 128).T)
    pp = np.ascontiguousarray(np.broadcast_to(pos_b[None, :].astype(np.int32), (128, T)))
    return xt, ct, pp


FUSED = True


def kernel(**inputs):
    inp = {k: np.asarray(v) for k, v in inputs.items()}
    x = inp["x"].astype(np.float32)
    B = x.shape[0]
    groups = [list(range(DEPTH))] if FUSED else [[l] for l in range(DEPTH)]
    cur = [None] * B
    for b in range(B):
        cur[b] = _core_inputs(x[b], inp["c"][b], inp["positions"][b])
    for layers in groups:
        w = _prep_weights(inp, layers)
        nc = _get_nc(len(layers))
        in_maps = []
        for b in range(B):
            m = dict(w)
            m["x"], m["cT"], m["pos"] = cur[b]
            in_maps.append(m)
        res = run_bass_kernel_spmd(nc, in_maps, core_ids=list(range(B)))
        for b in range(B):
            cur[b] = (np.ascontiguousarray(res.results[b]["y"]), cur[b][1], cur[b][2])
    out = np.empty((B, T, D), np.float32)
    for b in range(B):
        out[b] = cur[b][0].transpose(1, 0, 2).reshape(D, T).T
    return out
```

```python
import numpy as np
import concourse.bass as bass
import concourse.mybir as mybir
from concourse.bass_utils import run_bass_kernel_spmd

F32 = mybir.dt.float32
BF16 = mybir.dt.bfloat16
I32 = mybir.dt.int32
ALU = mybir.AluOpType
AF = mybir.ActivationFunctionType
AX = mybir.AxisListType

_ISZ = {F32: 4, BF16: 2, I32: 4}
SB_BASE = 16512
SB_SIZE = 212800


class Op:
    __slots__ = ("eng", "fn", "deps", "idx", "sig", "seq", "is_dma", "dsem", "dval")


class View:
    __slots__ = ("ap", "blocks")

    def __init__(self, ap, blocks):
        self.ap = ap
        self.blocks = blocks

    def re(self, pat, **kw):
        return View(self.ap.rearrange(pat, **kw), self.blocks)


class Buf:
    def __init__(self, sch, name, free_shape, dtype, offset=None, parts=128, space="sb"):
        self.sch = sch
        self.name = name
        self.shape = tuple(free_shape)
        self.dtype = dtype
        self.isz = _ISZ[dtype]
        self.space = space
        self.parts = parts
        nc = sch.nc
        if space == "sb":
            self.off = offset
            self.h = nc.alloc_sbuf_tensor_at(name, [parts] + list(free_shape), dtype, offset=SB_BASE + offset)
            self.gran = 64
        else:
            self.off = 0
            self.h = nc.alloc_psum_tensor(name, [parts] + list(free_shape), dtype)
            self.gran = 2048
        st = []
        s = 1
        for d in reversed(self.shape):
            st.append(s)
            s *= d
        self.strides = tuple(reversed(st))
        self.nbytes = s * self.isz

    def _runs(self, key):
        shape, strides = self.shape, self.strides
        n = len(shape)
        d0 = n - 1
        while d0 > 0 and key[d0] == (0, shape[d0]):
            d0 -= 1
        runs = []

        def rec(d, base):
            a, b = key[d]
            if d == d0:
                runs.append((base + a * strides[d], base + b * strides[d]))
            else:
                for i in range(a, b):
                    rec(d + 1, base + i * strides[d])

        rec(0, 0)
        return runs

    def __getitem__(self, key):
        if not isinstance(key, tuple):
            key = (key,)
        pk = key[0]
        fk = list(key[1:])
        while len(fk) < len(self.shape):
            fk.append(slice(None))
        norm = []
        for d, k in enumerate(fk):
            if isinstance(k, int):
                norm.append((k, k + 1))
            else:
                a = 0 if k.start is None else k.start
                b = self.shape[d] if k.stop is None else k.stop
                norm.append((a, b))
        blocks = set()
        g = self.gran
        for (a, b) in self._runs(norm):
            a = self.off + a * self.isz
            b = self.off + b * self.isz
            for blk in range(a // g, (b - 1) // g + 1):
                blocks.add((self.space, blk))
        ap = self.h[(pk,) + tuple(fk)]
        return View(ap, blocks)


class Sched:
    ENGS = ("pe", "act", "dve", "pool", "sp")

    def __init__(self, nc, n_dsem=6):
        self.nc = nc
        self.q = {e: [] for e in self.ENGS}
        self.mem = {}
        self.n_dsem = n_dsem
        self.dma_count = {"sp": 0, "pool": 0, "act": 0}
        self.dsem_cnt = {}
        self.arena = nc.alloc_sbuf_tensor("arena", [128, SB_SIZE // 4], F32)
        assert nc.lookup_mloc(self.arena).addr == SB_BASE

    def add(self, eng, fn, reads=(), writes=(), dma=False):
        op = Op()
        op.eng = eng
        op.fn = fn
        op.is_dma = dma
        op.sig = dma
        op.seq = 0
        op.idx = len(self.q[eng])
        op.dsem = None
        op.dval = 0
        deps = {}
        mem = self.mem

        def dep(o, raw):
            if o is None or o is op:
                return
            k = id(o)
            if k in deps:
                deps[k] = (o, deps[k][1] or raw)
            else:
                deps[k] = (o, raw)

        rblocks = set()
        for v in reads:
            rblocks |= v.blocks
        wblocks = set()
        for v in writes:
            wblocks |= v.blocks
        for b in rblocks:
            st = mem.get(b)
            if st is not None:
                dep(st[0], True)
        for b in wblocks:
            st = mem.get(b)
            if st is not None:
                dep(st[0], False)
                for r in st[1]:
                    dep(r, False)
        for b in rblocks:
            st = mem.get(b)
            if st is None:
                mem[b] = [None, [op]]
            else:
                st[1].append(op)
        for b in wblocks:
            mem[b] = [op, []]
        best = {}
        final = []
        for (o, raw) in deps.values():
            if o.is_dma:
                final.append(o)
                continue
            if o.eng == eng and not dma:
                if eng == "pe":
                    continue
            cur = best.get(o.eng)
            if cur is None or o.idx > cur.idx:
                best[o.eng] = o
        final.extend(best.values())
        op.deps = final
        for o in final:
            o.sig = True
        if dma:
            k = self.dma_count[eng] % self.n_dsem
            self.dma_count[eng] += 1
            op.dsem = (eng, k)
            c = self.dsem_cnt.get(op.dsem, 0) + 16
            self.dsem_cnt[op.dsem] = c
            op.dval = c
        self.q[eng].append(op)
        return op

    def pe(self, fn, reads=(), writes=()):
        return self.add("pe", fn, reads, writes)

    def act(self, fn, reads=(), writes=()):
        return self.add("act", fn, reads, writes)

    def dve(self, fn, reads=(), writes=()):
        return self.add("dve", fn, reads, writes)

    def pool(self, fn, reads=(), writes=()):
        return self.add("pool", fn, reads, writes)

    def dma(self, eng, out_ap, in_ap, reads=(), writes=()):
        return self.add(eng, lambda e: e.dma_start(out=out_ap, in_=in_ap), reads, writes, dma=True)

    def emit(self, final_waits=()):
        nc = self.nc
        for o in final_waits:
            o.sig = True
        for e in self.ENGS:
            s = 0
            for op in self.q[e]:
                if op.sig and not op.is_dma:
                    s += 1
                    op.seq = s
        from contextlib import ExitStack
        with ExitStack() as es:
            esem = {e: es.enter_context(nc.semaphore("s_" + e)) for e in self.ENGS}
            dsem = {}
            for (q, k) in self.dsem_cnt:
                dsem[(q, k)] = es.enter_context(nc.semaphore("d_%s_%d" % (q, k)))
            block = es.enter_context(nc.Block())

            def run(ename, e):
                waited = {}
                for op in self.q[ename]:
                    for o in op.deps:
                        if o.is_dma:
                            sem, val, key = dsem[o.dsem], o.dval, o.dsem
                        else:
                            sem, val, key = esem[o.eng], o.seq, o.eng
                        if waited.get(key, 0) < val:
                            e.wait_ge(sem, val)
                            waited[key] = val
                    ins = op.fn(e)
                    if op.is_dma:
                        ins.then_inc(dsem[op.dsem], 16)
                    elif op.sig:
                        ins.then_inc(esem[ename], 1)
                if ename == "sp":
                    for o in final_waits:
                        if o.is_dma:
                            e.wait_ge(dsem[o.dsem], o.dval)
                        else:
                            e.wait_ge(esem[o.eng], o.seq)

            @block.tensor
            def _(e):
                run("pe", e)

            @block.scalar
            def _(e):
                run("act", e)

            @block.vector
            def _(e):
                run("dve", e)

            @block.gpsimd
            def _(e):
                run("pool", e)

            @block.sync
            def _(e):
                run("sp", e)


D = 1024
T = 2048
NTT = 4
DEPTH = 4
ALPHA = (2 * DEPTH) ** 0.25
LN_EPS = 1e-5
RMS_EPS = 1e-6
ATT_SCALE = 96 ** -0.5
NV = 200
V_BADA, V_CAW, V_CAB, V_LAG, V_LAB, V_CBW, V_QG, V_KVG, V_LG, V_LB = 0, 24, 148, 152, 156, 160, 172, 175, 177, 185
NGRP = 16
TWO_PI = 6.283185307179586


def _chunk_cols():
    ar = np.arange(128)
    ch = []
    for c in range(3):
        ch.append(3584 + 128 * c + ar)
    for c in range(2):
        ch.append(3968 + 128 * c + ar)
    kr = np.full(128, -1)
    kr[64:96] = 4224 + np.arange(32)
    ch.append(kr)
    krs = np.full(128, -1)
    krs[64:80] = 4240 + np.arange(16)
    krs[80:96] = 4224 + np.arange(16)
    ch.append(krs)
    for c in range(4):
        ch.append(4256 + 128 * c + ar)
    ch.append(np.full(128, -1))
    for base in (512, 0, 1024):
        for c in range(4):
            ch.append(base + 128 * c + ar)
    for base in (2560, 1536, 2048, 3072):
        for c in range(4):
            ch.append(base + 128 * c + ar)
    for j in range(8):
        for base in (4768, 5792, 6816):
            ch.append(base + 128 * j + ar)
    assert len(ch) == 64
    return np.concatenate(ch)


def _prep_weights(inp, layers):
    cols = _chunk_cols()
    L = len(layers)
    f = np.float32
    wada = np.empty((L, 6, 128, 8, 512), f)
    win = np.empty((L, NGRP, 128, 8, 512), f)
    wuq = np.empty((L, 128, 3, 768), f)
    wuqs = np.empty((L, 128, 3, 256), f)
    wukv = np.empty((L, 128, 2, 1024), f)
    wy = np.empty((L, 8, 128, 3, 4, 128), f)
    wo = np.empty((L, 2, 128, 8, 512), f)
    vecs = np.zeros((128, L, NV), f)
    for i, l in enumerate(layers):
        wada[i] = np.asarray(inp["w_ada"][l]).reshape(8, 128, 6, 512).transpose(2, 1, 0, 3)
        w = np.asarray(inp["w_in"][l])
        wp = np.zeros((1024, cols.size), f)
        m = cols >= 0
        wp[:, m] = w[:, cols[m]]
        win[i] = wp.reshape(8, 128, NGRP, 512).transpose(2, 1, 0, 3)
        uq = np.asarray(inp["w_uq"][l])
        wuq[i] = uq.reshape(3, 128, 768).transpose(1, 0, 2)
        sc = np.concatenate([np.concatenate([h * 96 + 80 + np.arange(16), h * 96 + 64 + np.arange(16)]) for h in range(8)])
        wuqs[i] = uq[:, sc].reshape(3, 128, 256).transpose(1, 0, 2)
        ukv = np.asarray(inp["w_ukv"][l])
        kc_ = np.concatenate([h * 128 + np.arange(64) for h in range(8)])
        vc_ = np.concatenate([h * 128 + 64 + np.arange(64) for h in range(8)])
        wukv[i] = np.concatenate([ukv[:, kc_], ukv[:, vc_]], axis=1).reshape(2, 128, 1024).transpose(1, 0, 2)
        for b, nm in enumerate(("w_a_out", "w_b_out", "w_c_out")):
            wx = np.asarray(inp[nm][l])
            wy[i, :, :, b] = wx.reshape(4, 128, 8, 128).transpose(2, 1, 0, 3)
        wo[i] = np.asarray(inp["w_o"][l]).reshape(8, 128, 2, 512).transpose(2, 1, 0, 3)
        v = vecs[:, i]
        v[:, V_BADA:V_BADA + 24] = np.asarray(inp["b_ada"][l]).reshape(24, 128).T
        v[:, V_CAW:V_CAW + 124] = np.asarray(inp["conv_a_w"][l]).reshape(31, 4, 128).transpose(2, 1, 0).reshape(128, 124)
        v[:, V_CAB:V_CAB + 4] = np.asarray(inp["conv_a_b"][l]).reshape(4, 128).T
        v[:, V_LAG:V_LAG + 4] = np.asarray(inp["ln_a_g"][l]).reshape(4, 128).T
        v[:, V_LAB:V_LAB + 4] = np.asarray(inp["ln_a_b"][l]).reshape(4, 128).T
        v[:, V_CBW:V_CBW + 12] = np.asarray(inp["conv_b_w"][l]).reshape(3, 4, 128).transpose(2, 1, 0).reshape(128, 12)
        v[:, V_QG:V_QG + 3] = np.asarray(inp["q_norm_g"][l]).reshape(3, 128).T
        v[:, V_KVG:V_KVG + 2] = np.asarray(inp["kv_norm_g"][l]).reshape(2, 128).T
        v[:, V_LG:V_LG + 8] = np.asarray(inp["ln_g"][l]).reshape(8, 128).T
        v[:, V_LB:V_LB + 8] = np.asarray(inp["ln_b"][l]).reshape(8, 128).T
    cst = np.zeros((128, 256), f)
    invc = np.zeros((128, 1), f)
    cst[:, 0:128] = np.eye(128, dtype=f)
    kk = np.arange(128)[:, None]
    qq = np.arange(128)[None, :]
    cst[:, 128:256] = np.where(kk > qq, -30000.0, 0.0)
    invf = (10000.0 ** (-np.arange(0, 32, 2, dtype=np.float32) / 32)).astype(f)
    for p in range(64, 96):
        invc[p, 0] = invf[(p - 64) % 16]
    return dict(invf=invc, wada=wada, win=win, wuq=wuq, wuqs=wuqs, wukv=wukv, wy=wy, wo=wo, vecs=vecs, cst=cst)


def build(L):
    nc = bass.Bass("TRN2", target_bir_lowering=False)

    def din(name, shape, dt=F32):
        return nc.dram_tensor(name, list(shape), dt, kind="ExternalInput").ap()

    x_d = din("x", [128, 8, T])
    c_d = din("cT", [128, 8])
    pos_d = din("pos", [128, T], I32)
    cst_d = din("cst", [128, 256])
    invf_d = din("invf", [128, 1])
    wada_d = din("wada", [L, 6, 128, 8, 512])
    win_d = din("win", [L, NGRP, 128, 8, 512])
    wuq_d = din("wuq", [L, 128, 3, 768])
    wuqs_d = din("wuqs", [L, 128, 3, 256])
    wukv_d = din("wukv", [L, 128, 2, 1024])
    wy_d = din("wy", [L, 8, 128, 3, 4, 128])
    wo_d = din("wo", [L, 2, 128, 8, 512])
    vecs_d = din("vecs", [128, L, NV])
    y_d = nc.dram_tensor("y", [128, 8, T], F32, kind="ExternalOutput").ap()
    rope_d = nc.dram_tensor("rope_scr", [2, 32, T], F32).ap()

    S = Sched(nc)

    def sb(name, shape, dt, off):
        b = Buf(S, name, shape, dt, offset=off)
        assert off + b.nbytes <= SB_SIZE, name
        return b

    xT = sb("xT", [8, T], F32, 0)
    O_SM = 65536
    ident = sb("ident", [128], BF16, O_SM)
    maskn = sb("maskn", [128], BF16, O_SM + 256)
    ones_bf = sb("ones_bf", [128], BF16, O_SM + 512)
    onesf = sb("onesf", [64], F32, O_SM + 768)
    invf = sb("invf", [1], F32, O_SM + 1024)
    cact = sb("cact", [8], F32, O_SM + 1088)
    cact_bf = sb("cact_bf", [8], BF16, O_SM + 1152)
    ada = sb("ada", [24], F32, O_SM + 1216)
    gq = sb("gq", [8], F32, O_SM + 1344)
    gatea = sb("gatea", [8], F32, O_SM + 1408)
    one1 = sb("one1", [1], F32, O_SM + 1472)
    epsv = sb("epsv", [4], F32, O_SM + 1504)
    vecs = sb("vecs", [L, NV], F32, O_SM + 1536)
    expo = sb("expo", [512], F32, O_SM + 1536 + 3200)
    adarow = sb("adarow", [512], F32, O_SM + 6784)
    ada_t = [ada, sb("ada1", [24], F32, O_SM + 8832)]
    gq_t = [gq, sb("gq1", [8], F32, O_SM + 8928)]
    gatea_t = [gatea, sb("gatea1", [8], F32, O_SM + 8960)]
    Gp = sb("Gp", [8], F32, O_SM + 8992)
    Bp = sb("Bp", [8], F32, O_SM + 9024)
    O_FIN = O_SM + 9216
    c_fin = sb("c_fin", [4, T], BF16, O_FIN)
    a_fin = sb("a_fin", [4, T], BF16, O_FIN + 16384)
    b_fin = sb("b_fin", [4, T], BF16, O_FIN + 32768)
    O_U = O_FIN + 49152
    uT = sb("uT", [8, T], BF16, O_U)
    O_S = O_U + 32768
    S_SIZE = SB_SIZE - O_S
    NSLOT = 2
    wbuf = [sb("wbuf%d" % i, [8, 512], BF16, O_S + 8192 * i) for i in range(NSLOT)]
    SS = O_S + 8192 * NSLOT
    ps = Buf(S, "ps", [8, 512], F32, space="ps")

    gcs = sb("gcs", [4, T], BF16, O_FIN + 16384)
    qn = sb("qn", [3, T], BF16, O_FIN + 32768)
    kpe = sb("kpe", [T], BF16, O_FIN + 32768 + 12288)
    kvn = sb("kvn", [2, T], BF16, SS)
    sqb = sb("sqb", [5, 512], BF16, SS + 8192)
    rq = sb("rq", [512], F32, SS + 13312)
    rkv = sb("rkv", [512], F32, SS + 15360)
    cosT = sb("cosT", [T], F32, SS + 17408)
    sinT = sb("sinT", [T], F32, SS + 25600)
    rt1 = sb("rt1", [512], F32, SS + 33792)
    rt2 = sb("rt2", [512], F32, SS + 35840)
    wuq_sb = sb("wuq_sb", [3, 768], BF16, SS + 8192)
    wukv_sb = sb("wukv_sb", [2, 1024], BF16, SS + 12800)
    wuqs_sb = sb("wuqs_sb", [3, 256], BF16, SS + 37888)
    tang = sb("tang", [T], F32, O_FIN)
    ttmp = sb("ttmp", [T], F32, O_FIN + 8192)
    tki = sb("tki", [T], I32, O_FIN + 8192)
    qTb = [sb("qT%d" % i, [T], BF16, O_U + 4096 * i) for i in range(2)]
    kTb = [sb("kT%d" % i, [T], BF16, O_U + 8192 + 4096 * i) for i in range(2)]
    Vp = sb("Vp", [16, 192], BF16, O_U + 16384)
    rs = sb("rs", [512], F32, O_U + 22528)
    Msb = sb("Msb", [512], F32, O_U + 24576)
    pbuf = [sb("pbuf%d" % i, [512], BF16, O_U + 26624 + 1024 * i) for i in range(4)]
    a_pad = sb("a_pad", [4, 2080], BF16, SS)
    ac = sb("ac", [4, 1024], F32, SS + 16896)
    acbs = [sb("acb%d" % i, [1, 1024], BF16, O_FIN + 32768 + 8192 + 2048 * i) for i in range(2)]
    sqas = [sb("sqa%d" % i, [1, 1024], BF16, O_FIN + 32768 + 12288 + 2048 * i) for i in range(2)]
    mean_a = sb("mean_a", [1024], F32, O_FIN + 32768)
    rstd_a = sb("rstd_a", [1024], F32, O_FIN + 32768 + 4096)
    diag = [sb("diag%d" % i, [128], BF16, SS + 33280 + 256 * i) for i in range(16)]
    p_pad = sb("p_pad", [4, 2112], BF16, SS)
    gtmp = sb("gtmp", [T], BF16, O_FIN + 32768 + 8192)
    stmp = sb("stmp", [T], BF16, SS + 20992)
    diagb = [sb("diagb%d" % i, [128], BF16, SS + 25088 + 256 * i) for i in range(8)]
    mT = sb("mT", [8, 1024], BF16, SS)
    wyb = [sb("wy%d" % i, [3, 4, 128], BF16, SS + 16384 + 3072 * i) for i in range(2)]
    wo_h = [sb("woA", [8, 512], BF16, SS + 16384), sb("woB", [8, 512], BF16, SS + 26624)]
    sg = [sb("sg%d" % i, [512], F32, SS + 26624 + 2048 * i) for i in range(3)]
    zb = [sb("zb%d" % i, [512], BF16, SS + 34816 + 1024 * i) for i in range(2)]
    zq = [sb("zq%d" % i, [512], BF16, SS + 36864 + 1024 * i) for i in range(2)]
    msq = sb("msq", [512], F32, SS + 24576)
    assert SS + 39424 <= SB_SIZE

    bank_rr = [0]

    def nb(pool_=(0, 1, 2, 3, 4, 5, 6, 7)):
        b = pool_[bank_rr[0] % len(pool_)]
        bank_rr[0] += 1
        return b

    def mm(out, lhsT, rhs, start, stop):
        S.pe(lambda e: e.matmul(out.ap, lhsT=lhsT.ap, rhs=rhs.ap, start=start, stop=stop),
             reads=[lhsT, rhs], writes=[out])

    def actf(out, in_, func, bias=None, scale=None, extra_reads=()):
        kw = {}
        if bias is not None:
            kw["bias"] = bias.ap if isinstance(bias, View) else bias
        if scale is not None:
            kw["scale"] = scale.ap if isinstance(scale, View) else scale
        rd = [in_] + [v for v in (bias, scale) if isinstance(v, View)] + list(extra_reads)
        S.act(lambda e: e.activation(out=out.ap, in_=in_.ap, func=func, **kw), reads=rd, writes=[out])

    def tt_(eng, out, a, b, op):
        S.add(eng, lambda e: e.tensor_tensor(out=out.ap, in0=a.ap, in1=b.ap, op=op), reads=[a, b], writes=[out])

    def ts_(eng, out, a, s1, s2, op0, op1=None):
        rd = [a] + [v for v in (s1, s2) if isinstance(v, View)]
        s1a = s1.ap if isinstance(s1, View) else s1
        s2a = s2.ap if isinstance(s2, View) else s2
        if op1 is None:
            S.add(eng, lambda e: e.tensor_scalar(out=out.ap, in0=a.ap, scalar1=s1a, scalar2=None, op0=op0), reads=rd, writes=[out])
        else:
            S.add(eng, lambda e: e.tensor_scalar(out=out.ap, in0=a.ap, scalar1=s1a, scalar2=s2a, op0=op0, op1=op1), reads=rd, writes=[out])

    def stt(out, a, s, b, op0, op1):
        rd = [a, b] + ([s] if isinstance(s, View) else [])
        sa = s.ap if isinstance(s, View) else s
        S.dve(lambda e: e.scalar_tensor_tensor(out=out.ap, in0=a.ap, scalar=sa, in1=b.ap, op0=op0, op1=op1), reads=rd, writes=[out])

    def cp(eng, out, in_):
        S.add(eng, lambda e: e.tensor_copy(out=out.ap, in_=in_.ap), reads=[in_], writes=[out])

    def rsqrt_eps(out, in_, eps_col, pw=-0.5):
        if eps_col is None:
            actf(out, in_, AF.Ln)
        else:
            actf(out, in_, AF.Ln, bias=epsv[in_part(out), eps_col:eps_col + 1])
        actf(out, out, AF.Exp, scale=pw)

    def in_part(v):
        return slice(None)

    def ps4(bs):
        return ps[:, bs:bs + 4, :].re("p a b -> p (a b)")

    def ps2(bs):
        return ps[:, bs:bs + 2, :].re("p a b -> p (a b)")

    bset_c = [0]

    def bset():
        bset_c[0] += 1
        return (bset_c[0] % 2) * 4

    stream = []
    for g in range(6):
        stream.append(wada_d[0, g])
    for l in range(L):
        for g in range(16):
            stream.append(win_d[l, g])
        if l + 1 < L:
            for g in range(6):
                stream.append(wada_d[l + 1, g])
        for g in range(10, 16):
            stream.append(win_d[l, g])
    issued = [0]

    def grp(si):
        while issued[0] < min(len(stream), si + 1):
            k = issued[0]
            b = wbuf[k % NSLOT]
            S.dma("pool", b[:].ap, stream[k], writes=[b[:]])
            issued[0] += 1
        return wbuf[si % NSLOT]

    spos = [0]

    S.dma("sp", xT[:].ap, x_d, writes=[xT[:]])
    S.dma("sp", cact[:].ap, c_d, writes=[cact[:]])
    S.dma("sp", vecs[:].ap, vecs_d, writes=[vecs[:]])
    S.dma("sp", invf[:].ap, invf_d, writes=[invf[:]])
    S.dma("pool", ident[:].ap, cst_d[:, 0:128], writes=[ident[:]])
    S.dma("pool", maskn[:].ap, cst_d[:, 128:256], writes=[maskn[:]])
    grp(1)
    S.pool(lambda e: e.memset(ones_bf[:].ap, 1.0), writes=[ones_bf[:]])
    S.pool(lambda e: e.memset(onesf[:].ap, 1.0), writes=[onesf[:]])
    S.pool(lambda e: e.memset(one1[:].ap, 1.0), writes=[one1[:]])
    S.pool(lambda e: e.memset(epsv[:, 0:1].ap, 384 * RMS_EPS), writes=[epsv[:]])
    S.pool(lambda e: e.memset(epsv[:, 1:2].ap, 256 * RMS_EPS), writes=[epsv[:]])
    S.pool(lambda e: e.memset(epsv[:, 2:3].ap, LN_EPS), writes=[epsv[:]])
    S.pool(lambda e: e.memset(epsv[:, 3:4].ap, LN_EPS / (ALPHA * ALPHA)), writes=[epsv[:]])
    actf(cact[:], cact[:], AF.Silu)
    cp("dve", cact_bf[:], cact[:])

    def vcol(l, idx, n=1):
        return vecs[:, l, idx:idx + n]

    def ada_steps(l):
        ada, gq, gatea = ada_t[l % 2], gq_t[l % 2], gatea_t[l % 2]
        steps = []
        for g in range(6):
            def st(g=g):
                wg = grp(spos[0])
                grp(spos[0] + 1)
                for kc in range(8):
                    mm(ps[0:1, 6, :], cact_bf[:, kc:kc + 1], wg[:, kc, :], kc == 0, kc == 7)
                spos[0] += 1
                cp("dve", adarow[0:1, :], ps[0:1, 6, :])
                for jj in range(4):
                    mm(ps[:, 7, g * 4 + jj: g * 4 + jj + 1], adarow[0:1, jj * 128:(jj + 1) * 128], one1[0:1, :], True, True)
            steps.append(st)

        def fin():
            tt_("dve", ada[:, 0:24], ps[:, 7, 0:24], vcol(l, V_BADA, 24), ALU.add)
            ts_("dve", ada[:, 8:16], ada[:, 8:16], 1.0, None, ALU.add)
            ts_("dve", gatea[:], ada[:, 16:24], 1.0 / ALPHA, None, ALU.mult)
            ts_("dve", gq[:, 0:3], vcol(l, V_QG, 3), float(384 ** 0.5), None, ALU.mult)
            ts_("dve", gq[:, 3:5], vcol(l, V_KVG, 2), float(256 ** 0.5), None, ALU.mult)
            if l >= 1:
                tt_("dve", Gp[:], vcol(l - 1, V_LG, 8), ada[:, 8:16], ALU.mult)
                tt_("dve", Bp[:], vcol(l - 1, V_LB, 8), ada[:, 8:16], ALU.mult)
                tt_("dve", Bp[:], Bp[:], ada[:, 0:8], ALU.add)
        steps.append(fin)
        return steps

    def ada_compute(l):
        for st in ada_steps(l):
            st()

    lnq = []

    def drain(n=10 ** 9):
        while lnq and n > 0:
            lnq.pop(0)()
            n -= 1

    def ln_pieces(l, tts, fuse_u):
        b1, b2 = 6, 7
        for tt in tts:
            ts = slice(tt * 512, (tt + 1) * 512)

            def stat_mm(c):
                mm(ps[:, b1, :], ones_bf[:], zb[c % 2][:], c == 0, c == 7)
                mm(ps[:, b2, :], ones_bf[:], zq[c % 2][:], c == 0, c == 7)

            for c in range(8):
                def p1(c=c, ts=ts, stat_mm=stat_mm):
                    cp("dve", zb[c % 2][:], xT[:, c, ts])
                    actf(zq[c % 2][:], xT[:, c, ts], AF.Square)
                    if c > 0:
                        stat_mm(c - 1)
                lnq.append(p1)

            def chain(stat_mm=stat_mm):
                stat_mm(7)
                actf(msq[:], ps[:, b1, :], AF.Square, scale=1.0 / D)
                ts_("dve", ps[:, b1, :], ps[:, b1, :], 1.0 / D, None, ALU.mult)
                stt(ps[:, b2, :], ps[:, b2, :], 1.0 / D, msq[:], ALU.mult, ALU.subtract)
                rsqrt_eps(ps[:, b2, :], ps[:, b2, :], 3)
            lnq.append(chain)
            for c in range(8):
                def p2(c=c, ts=ts):
                    tt_("dve", xT[:, c, ts], xT[:, c, ts], ps[:, b1, :], ALU.subtract)
                    tt_("dve", xT[:, c, ts], xT[:, c, ts], ps[:, b2, :], ALU.mult)
                    if fuse_u:
                        actf(uT[:, c, ts], xT[:, c, ts], AF.Identity, bias=Bp[:, c:c + 1], scale=Gp[:, c:c + 1])
                    actf(xT[:, c, ts], xT[:, c, ts], AF.Identity, bias=vcol(l, V_LB + c), scale=vcol(l, V_LG + c))
                lnq.append(p2)

    def layer(l):
        ada, gq, gatea = ada_t[l % 2], gq_t[l % 2], gatea_t[l % 2]

        def make_u():
            for tt in range(NTT):
                ts = slice(tt * 512, (tt + 1) * 512)
                for c in range(8):
                    if c % 2 == 0:
                        actf(uT[:, c, ts], xT[:, c, ts], AF.Identity, bias=ada[:, c:c + 1], scale=ada[:, 8 + c:9 + c])
                    else:
                        ts_("dve", uT[:, c, ts], xT[:, c, ts], ada[:, 8 + c:9 + c], ada[:, c:c + 1], ALU.mult, ALU.add)

        if l == 0:
            make_u()
        R = slice(64, 96)
        g0 = spos[0]

        def wchunk(ci):
            gi = g0 + ci // 4
            wgb = grp(gi)
            return wgb, (ci % 4) * 128

        ln_busy = bool(lnq)
        for tt in range(NTT):
            ts = slice(tt * 512, (tt + 1) * 512)
            banks = []
            for ci in range(5):
                wgb, co = wchunk(ci)
                b = nb((0, 1, 2, 3, 4)) if ln_busy else nb((0, 1, 2, 3, 4, 7))
                banks.append(b)
                for kc in range(8):
                    mm(ps[:, b, :], wgb[:, kc, co:co + 128], uT[:, kc, ts], kc == 0, kc == 7)
                actf(sqb[:, ci, :], ps[:, b, :], AF.Square)
                drain(4)
            if tt == 1:
                drain()
            for (lo, hi, sbank, rbuf, n) in ((0, 3, 5, rq, 384), (3, 5, 5 if ln_busy else 6, rkv, 256)):
                for ci in range(lo, hi):
                    mm(ps[:, sbank, :], ones_bf[:], sqb[:, ci, :], ci == lo, ci == hi - 1)
                rsqrt_eps(rbuf[:], ps[:, sbank, :], 0 if n == 384 else 1)
                for ci in range(lo, hi):
                    dst = qn[:, ci, ts] if ci < 3 else kvn[:, ci - 3, ts]
                    stt(dst, ps[:, banks[ci], :], gq[:, ci:ci + 1], rbuf[:], ALU.mult, ALU.mult)
            if tt == 1:
                ln_busy = False
        S.dma("pool", wuq_sb[:].ap, wuq_d[l], writes=[wuq_sb[:]])
        S.dma("pool", wuqs_sb[:].ap, wuqs_d[l], writes=[wuqs_sb[:]])
        S.dma("pool", wukv_sb[:].ap, wukv_d[l], writes=[wukv_sb[:]])
        dr_cos, dr_sin = View(rope_d[0], {("dr", 0)}), View(rope_d[1], {("dr", 1)})
        if l > 0:
            S.dma("sp", cosT[R, :].ap, dr_cos.ap, reads=[dr_cos], writes=[cosT[R, :]])
            S.dma("sp", sinT[R, :].ap, dr_sin.ap, reads=[dr_sin], writes=[sinT[R, :]])
        else:
            S.dma("sp", tki[R, :].ap, pos_d[64:96, :], writes=[tki[R, :]])
            cp("dve", tang[R, :], tki[R, :])
            ts_("dve", tang[R, :], tang[R, :], invf[R, :], None, ALU.mult)
            ts_("dve", ttmp[R, :], tang[R, :], 1.0 / TWO_PI, None, ALU.mult)
            cp("dve", tki[R, :], ttmp[R, :])
            cp("dve", ttmp[R, :], tki[R, :])
            C1 = 6.28125
            C2 = TWO_PI - C1
            stt(tang[R, :], ttmp[R, :], -C1, tang[R, :], ALU.mult, ALU.add)
            stt(tang[R, :], ttmp[R, :], -C2, tang[R, :], ALU.mult, ALU.add)
            ts_("dve", ttmp[R, :], tang[R, :], float(np.pi), None, ALU.is_gt)
            stt(tang[R, :], ttmp[R, :], -TWO_PI, tang[R, :], ALU.mult, ALU.add)
            ts_("dve", ttmp[R, :], tang[R, :], float(-np.pi), None, ALU.is_lt)
            stt(tang[R, :], ttmp[R, :], TWO_PI, tang[R, :], ALU.mult, ALU.add)
            ts_("dve", tang[R, :], tang[R, :], 3.1415925, -3.1415925, ALU.min, ALU.max)
            actf(sinT[R, :], tang[R, :], AF.Sin)
            ts_("dve", ttmp[R, :], tang[R, :], float(np.pi / 2), None, ALU.is_gt)
            stt(tang[R, :], ttmp[R, :], -TWO_PI, tang[R, :], ALU.mult, ALU.add)
            ts_("dve", tang[R, :], tang[R, :], float(np.pi / 2), None, ALU.add)
            ts_("dve", tang[R, :], tang[R, :], 3.1415925, -3.1415925, ALU.min, ALU.max)
            actf(cosT[R, :], tang[R, :], AF.Sin)
            ts_("dve", sinT[64:80, :], sinT[64:80, :], -1.0, None, ALU.mult)

            S.dma("sp", dr_cos.ap, cosT[R, :].ap, reads=[cosT[R, :]], writes=[dr_cos])
            S.dma("sp", dr_sin.ap, sinT[R, :].ap, reads=[sinT[R, :]], writes=[dr_sin])
        grp(g0 + 2)
        for c in range(4):
            wgb, co = wchunk(7 + c)
            bs = (c % 2) * 4
            for tt in range(NTT):
                ts = slice(tt * 512, (tt + 1) * 512)
                for kc in range(8):
                    mm(ps[:, bs + tt, :], wgb[:, kc, co:co + 128], uT[:, kc, ts], kc == 0, kc == 7)
            actf(gcs[:, c, :], ps4(bs), AF.Silu)
        for tt in range(NTT):
            ts = slice(tt * 512, (tt + 1) * 512)
            bk = []
            for ci in (5, 6):
                wgb, co = wchunk(ci)
                b = nb((0, 1, 2, 3, 4))
                bk.append(b)
                for kc in range(8):
                    mm(ps[0:96, b, :], wgb[:, kc, co:co + 96], uT[:, kc, ts], kc == 0, kc == 7)
            tt_("dve", rt1[R, :], ps[R, bk[0], :], cosT[R, ts], ALU.mult)
            tt_("dve", rt2[R, :], ps[R, bk[1], :], sinT[R, ts], ALU.mult)
            tt_("dve", kpe[R, ts], rt1[R, :], rt2[R, :], ALU.add)
        spos[0] = g0 + 3
        grp(g0 + 3)
        grp(g0 + 4)

        for qb_ in qTb + kTb:
            S.pool(lambda e, qb_=qb_: e.memset(qb_[96:128, :].ap, 0.0), writes=[qb_[96:128, :]])
        S.pool(lambda e: e.memset(Vp[:, :, 64:128].ap, 0.0), writes=[Vp[:]])
        S.pool(lambda e: e.memset(Vp[:, :, 64:65].ap, 1.0), writes=[Vp[:]])
        AP_ = (2, 3, 4, 5, 6, 7)

        def prod_qk(h):
            qT, kT = qTb[h % 2], kTb[h % 2]
            for tt in range(NTT):
                ts = slice(tt * 512, (tt + 1) * 512)
                b = nb(AP_)
                for kc in range(2):
                    mm(ps[0:64, b, :], wukv_sb[:, kc, h * 64:(h + 1) * 64], kvn[:, kc, ts], kc == 0, kc == 1)
                cp("dve", kT[0:64, ts], ps[0:64, b, :])
                bq = nb(AP_)
                for kc in range(3):
                    mm(ps[0:96, bq, :], wuq_sb[:, kc, h * 96:(h + 1) * 96], qn[:, kc, ts], kc == 0, kc == 2)
                bs_ = nb(AP_)
                for kc in range(3):
                    mm(ps[64:96, bs_, :], wuqs_sb[:, kc, h * 32:(h + 1) * 32], qn[:, kc, ts], kc == 0, kc == 2)
                cp("dve", qT[0:64, ts], ps[0:64, bq, :])
                tt_("dve", rt1[R, :], ps[R, bq, :], cosT[R, ts], ALU.mult)
                tt_("dve", rt2[R, :], ps[R, bs_, :], sinT[R, ts], ALU.mult)
                tt_("dve", qT[R, ts], rt1[R, :], rt2[R, :], ALU.add)
            cp("dve", kT[R, :], kpe[R, :])

        def prod_v(j):
            for g4 in range(4):
                b = nb(AP_)
                for k4 in range(4):
                    kt = g4 * 4 + k4
                    for kc in range(2):
                        mm(ps[:, b, k4 * 128:(k4 + 1) * 128], kvn[:, kc, kt * 128:(kt + 1) * 128],
                           wukv_sb[:, kc, 512 + j * 128:512 + (j + 1) * 128], kc == 0, kc == 1)
                pv = ps[:, b, :].re("p (k c) -> p k c", c=128)
                S.dve(lambda e, pv=pv, g4=g4: e.tensor_copy(out=Vp[:, g4 * 4:(g4 + 1) * 4, 0:64].ap, in_=pv.ap[:, :, 0:64]),
                      reads=[pv], writes=[Vp[:, g4 * 4:(g4 + 1) * 4, :]])
                S.dve(lambda e, pv=pv, g4=g4: e.tensor_copy(out=Vp[:, g4 * 4:(g4 + 1) * 4, 128:192].ap, in_=pv.ap[:, :, 64:128]),
                      reads=[pv], writes=[Vp[:, g4 * 4:(g4 + 1) * 4, :]])

        def attn(h):
            j, odd = h // 2, h % 2
            qT, kT = qTb[h % 2], kTb[h % 2]
            for tt in range(NTT):
                ob = tt % 2
                nk = 4 * tt + 4
                steps = []
                for kt in range(nk):
                    q0 = max(tt * 512, kt * 128)
                    steps.append((kt, q0, (tt + 1) * 512 - q0, kt >= 4 * tt))
                LA = 3
                sb_ = {}

                def s_mm(i):
                    kt, q0, n, dg = steps[i]
                    b = nb(AP_[0:5])
                    sb_[i] = b
                    mm(ps[:, b, 0:n], kT[:, kt * 128:(kt + 1) * 128], qT[:, q0:q0 + n], True, not dg)
                    if dg:
                        mm(ps[:, b, 0:128], ident[:], maskn[:], False, True)

                for i in range(min(LA, nk)):
                    s_mm(i)
                while pend_norm:
                    pend_norm.pop(0)()
                for i in range(nk):
                    if i + LA < nk:
                        s_mm(i + LA)
                    kt, q0, n, dg = steps[i]
                    pb = pbuf[i % 4]
                    actf(pb[:, 0:n], ps[:, sb_[i], 0:n], AF.Exp, scale=ATT_SCALE)
                    o0 = q0 - tt * 512
                    if not odd:
                        mm(ps[:, ob, o0:512], Vp[:, kt, 0:128], pb[:, 0:n], i == 0, i == nk - 1)
                    else:
                        mm(ps[:, ob, o0:512], Vp[:, kt, 64:192], pb[:, 0:n], i == 0, i == nk - 1)
                ts = slice(tt * 512, (tt + 1) * 512)
                if not odd:
                    sr, orows = slice(64, 65), slice(0, 64)
                else:
                    sr, orows = slice(0, 1), slice(64, 128)
                actf(rs[sr, :], ps[sr, ob, :], AF.Ln)
                actf(rs[sr, :], rs[sr, :], AF.Exp, scale=-1.0)

                def norm_b(sr=sr, orows=orows, ob=ob, j=j, ts=ts):
                    mm(ps[orows, 7, :], onesf[sr, 0:64], rs[sr, :], True, True)
                    tt_("dve", Msb[orows, :], ps[orows, 7, :], gcs[orows, j, ts], ALU.mult)
                    tt_("dve", c_fin[orows, j, ts], ps[orows, ob, :], Msb[orows, :], ALU.mult)
                pend_norm.append(norm_b)

        pend_norm = []
        prod_qk(0)
        for h in range(8):
            if h % 2 == 0:
                prod_v(h // 2)
            if h + 1 < 8:
                prod_qk(h + 1)
            attn(h)
        while pend_norm:
            pend_norm.pop(0)()

        make_u()
        gA = spos[0]

        def wchunkA(ci):
            gi = gA + ci // 4
            return grp(gi), (ci % 4) * 128

        S.pool(lambda e: e.memset(a_pad[:, :, 0:32].ap, 0.0), writes=[a_pad[:, :, 0:32]])
        PADA = 32
        for c in range(4):
            wgb, co = wchunkA(c)
            bs = (c % 2) * 4
            for tt in range(NTT):
                for kc in range(8):
                    mm(ps[:, bs + tt, :], wgb[:, kc, co:co + 128], uT[:, kc, tt * 512:(tt + 1) * 512], kc == 0, kc == 7)
            actf(a_fin[:, c, :], ps4(bs), AF.Sigmoid)
        grp(gA + 2)
        for c in range(4):
            wgb, co = wchunkA(4 + c)
            bs = (c % 2) * 4
            for tt in range(NTT):
                for kc in range(8):
                    mm(ps[:, bs + tt, :], wgb[:, kc, co:co + 128], uT[:, kc, tt * 512:(tt + 1) * 512], kc == 0, kc == 7)
            tt_("dve", a_pad[:, c, PADA:PADA + T], ps4(bs), a_fin[:, c, :], ALU.mult)
        grp(gA + 3)
        for c in range(4):
            wgb, co = wchunkA(8 + c)
            bs = (c % 2) * 4
            for tt in range(NTT):
                for kc in range(8):
                    mm(ps[:, bs + tt, :], wgb[:, kc, co:co + 128], uT[:, kc, tt * 512:(tt + 1) * 512], kc == 0, kc == 7)
            actf(a_fin[:, c, :], ps4(bs), AF.Silu)
        spos[0] = gA + 3
        grp(gA + 4)
        dcnt = [0]
        pend_stats = []
        deferred = []
        ac2 = ac

        def emit_stats(c):
            for t2 in range(2):
                mm(ps[:, 4 + t2, :], ones_bf[:], acbs[c % 2][:, 0, t2 * 512:(t2 + 1) * 512], c == 0, c == 3)
                mm(ps[:, 6 + t2, :], ones_bf[:], sqas[c % 2][:, 0, t2 * 512:(t2 + 1) * 512], c == 0, c == 3)

        for hf in range(2):
            for c in range(4):
                bs = (c % 2) * 2
                acb, sqa = acbs[c % 2], sqas[c % 2]
                for k in range(31):
                    dg = diag[dcnt[0] % 16]
                    dcnt[0] += 1
                    ts_("dve", dg[:], ident[:], vcol(l, V_CAW + c * 31 + k), None, ALU.mult)
                    for t2 in range(2):
                        t0 = (hf * 2 + t2) * 512 + 2 + k
                        mm(ps[:, bs + t2, :], dg[:], a_pad[:, c, t0:t0 + 512], k == 0, k == 30)
                pc = ps2(bs)
                if deferred:
                    deferred.pop(0)()
                actf(ac[:, c, :], pc, AF.Identity, bias=vcol(l, V_CAB + c))
                actf(sqa[:, 0, :], pc, AF.Square, bias=vcol(l, V_CAB + c))
                actf(acb[:, 0, :], pc, AF.Identity, bias=vcol(l, V_CAB + c))
                pend_stats.append(c)
                if c > 0:
                    emit_stats(pend_stats.pop(0))
            emit_stats(pend_stats.pop(0))
            ts_("dve", mean_a[:], ps2(4), 1.0 / 512, None, ALU.mult)
            tt_("dve", rstd_a[:], mean_a[:], mean_a[:], ALU.mult)
            stt(rstd_a[:], ps2(6), 1.0 / 512, rstd_a[:], ALU.mult, ALU.subtract)
            rsqrt_eps(rstd_a[:], rstd_a[:], 2)
            hs_ = slice(hf * 1024, (hf + 1) * 1024)
            for c in range(4):
                def norm_piece(c=c, hs=hs_):
                    tt_("dve", ac2[:, c, :], ac2[:, c, :], mean_a[:], ALU.subtract)
                    stt(ac2[:, c, :], ac2[:, c, :], vcol(l, V_LAG + c), rstd_a[:], ALU.mult, ALU.mult)
                    actf(ac2[:, c, :], ac2[:, c, :], AF.Silu, bias=vcol(l, V_LAB + c))
                    tt_("dve", a_fin[:, c, hs], ac2[:, c, :], a_fin[:, c, hs], ALU.mult)
                deferred.append(norm_piece)

        gB = spos[0]

        def wchunkB(ci):
            gi = gB + ci // 4
            return grp(gi), (ci % 4) * 128

        S.pool(lambda e: e.memset(p_pad[:, :, 0:32].ap, 0.0), writes=[p_pad[:, :, 0:32]])

        def proj4(ci):
            wgb, co = wchunkB(ci)
            bs = bset()
            for tt in range(NTT):
                for kc in range(8):
                    mm(ps[:, bs + tt, :], wgb[:, kc, co:co + 128], uT[:, kc, tt * 512:(tt + 1) * 512], kc == 0, kc == 7)
            return ps4(bs)

        for c in range(4):
            pg = proj4(c)
            if deferred:
                deferred.pop(0)()
            actf(gtmp[:], pg, AF.Copy)
            px = proj4(4 + c)
            tt_("dve", p_pad[:, c, 32:32 + T], px, gtmp[:], ALU.mult)
        grp(gB + 2)
        grp(gB + 3)
        for c in range(4):
            pgb = proj4(8 + c)
            actf(b_fin[:, c, :], pgb, AF.Copy)
        grp(gB + 4)
        for c in range(4):
            pbg = proj4(12 + c)
            actf(stmp[:], pbg, AF.Silu)
            tt_("dve", b_fin[:, c, :], b_fin[:, c, :], stmp[:], ALU.mult)
        spos[0] = gB + 4
        grp(gB + 5)
        dcb = [0]
        for c in range(4):
            bs = (c % 2) * 4
            for k in range(3):
                dg = diagb[dcb[0] % 8]
                dcb[0] += 1
                ts_("dve", dg[:], ident[:], vcol(l, V_CBW + c * 3 + k), None, ALU.mult)
                for tt in range(NTT):
                    t0 = tt * 512 + 30 + k
                    mm(ps[:, bs + tt, :], dg[:], p_pad[:, c, t0:t0 + 512], k == 0, k == 2)
            tt_("dve", b_fin[:, c, :], ps4(bs), b_fin[:, c, :], ALU.mult)

        for hf in range(2):
            gH = spos[0]
            pool_ = (0, 1, 2, 3, 4, 5, 6, 7) if hf == 0 else (0, 1, 2, 3, 4, 5)
            for j in range(8):
                wb_ = wyb[j % 2]
                S.dma("pool", wb_[:].ap, wy_d[l, j], writes=[wb_[:]])
                for t2 in range(2):
                    tt = hf * 2 + t2
                    ts = slice(tt * 512, (tt + 1) * 512)
                    for br in range(3):
                        ci = j * 3 + br
                        wgb = grp(gH + ci // 4)
                        co = (ci % 4) * 128
                        b = nb(pool_)
                        for kc in range(8):
                            mm(ps[:, b, :], wgb[:, kc, co:co + 128], uT[:, kc, ts], kc == 0, kc == 7)
                        actf(sg[br][:], ps[:, b, :], AF.Sigmoid)
                    yb_ = []
                    for br, fin in enumerate((a_fin, b_fin, c_fin)):
                        b = nb(pool_)
                        yb_.append(b)
                        for kc in range(4):
                            mm(ps[:, b, :], wb_[:, br, kc, :], fin[:, kc, ts], kc == 0, kc == 3)
                    tt_("dve", sg[0][:], ps[:, yb_[0], :], sg[0][:], ALU.mult)
                    tt_("dve", sg[1][:], ps[:, yb_[1], :], sg[1][:], ALU.mult)
                    tt_("dve", sg[0][:], sg[0][:], sg[1][:], ALU.add)
                    tt_("dve", sg[2][:], ps[:, yb_[2], :], sg[2][:], ALU.mult)
                    tt_("dve", mT[:, j, t2 * 512:(t2 + 1) * 512], sg[0][:], sg[2][:], ALU.add)
                    drain(3)
                nxt = gH + ((j + 1) * 3 + 2) // 4 if j < 7 else gH + 6
                grp(nxt)
            drain()
            spos[0] = gH + 6
            ada_q = ada_steps(l + 1) if (hf == 0 and l + 1 < L) else []
            for eh in range(2):
                S.dma("pool", wo_h[eh][:].ap, wo_d[l, eh], writes=[wo_h[eh][:]])
            for t2 in range(2):
                tt = hf * 2 + t2
                ts = slice(tt * 512, (tt + 1) * 512)
                for e_ in range(8):
                    wo_ = wo_h[e_ // 4]
                    eo = (e_ % 4) * 128
                    b = nb((0, 1, 2, 3, 4, 5))
                    for kc in range(8):
                        mm(ps[:, b, :], wo_[:, kc, eo:eo + 128], mT[:, kc, t2 * 512:(t2 + 1) * 512], kc == 0, kc == 7)
                    stt(xT[:, e_, ts], ps[:, b, :], gatea[:, e_:e_ + 1], xT[:, e_, ts], ALU.mult, ALU.add)
                    if ada_q:
                        ada_q.pop(0)()
                    drain(2)
                while ada_q:
                    ada_q.pop(0)()
                ln_pieces(l, (tt,), l + 1 < L)
        if l + 1 == L:
            drain()

    ada_compute(0)
    for l in range(L):
        layer(l)
    outs = []
    for c in range(8):
        outs.append(S.dma("sp", y_d[:, c, :], xT[:, c, :].ap, reads=[xT[:, c, :]]))
    S.emit(final_waits=outs)
    return nc


_NC_CACHE = {}


def _get_nc(L):
    if L not in _NC_CACHE:
        _NC_CACHE[L] = build(L)
    return _NC_CACHE[L]


def _core_inputs(x_b, c_b, pos_b):
    xt = np.ascontiguousarray(x_b.T.reshape(8, 128, T).transpose(1, 0, 2))
    ct = np.ascontiguousarray(c_b.reshape(8, 128).T)
    pp = np.ascontiguousarray(np.broadcast_to(pos_b[None, :].astype(np.int32), (128, T)))
    return xt, ct, pp


FUSED = True


def kernel(**inputs):
    inp = {k: np.asarray(v) for k, v in inputs.items()}
    x = inp["x"].astype(np.float32)
    B = x.shape[0]
    groups = [list(range(DEPTH))] if FUSED else [[l] for l in range(DEPTH)]
    cur = [None] * B
    for b in range(B):
        cur[b] = _core_inputs(x[b], inp["c"][b], inp["positions"][b])
    for layers in groups:
        w = _prep_weights(inp, layers)
        nc = _get_nc(len(layers))
        in_maps = []
        for b in range(B):
            m = dict(w)
            m["x"], m["cT"], m["pos"] = cur[b]
            in_maps.append(m)
        res = run_bass_kernel_spmd(nc, in_maps, core_ids=list(range(B)))
        for b in range(B):
            cur[b] = (np.ascontiguousarray(res.results[b]["y"]), cur[b][1], cur[b][2])
    out = np.empty((B, T, D), np.float32)
    for b in range(B):
        out[b] = cur[b][0].transpose(1, 0, 2).reshape(D, T).T
    return out
```

```python
import numpy as np
import concourse.bass as bass
import concourse.mybir as mybir
from concourse.bass_utils import run_bass_kernel_spmd

F32 = mybir.dt.float32
BF16 = mybir.dt.bfloat16
I32 = mybir.dt.int32
ALU = mybir.AluOpType
AF = mybir.ActivationFunctionType
AX = mybir.AxisListType

_ISZ = {F32: 4, BF16: 2, I32: 4}
SB_BASE = 16512
SB_SIZE = 212800


class Op:
    __slots__ = ("eng", "fn", "deps", "idx", "sig", "seq", "is_dma", "dsem", "dval")


class View:
    __slots__ = ("ap", "blocks")

    def __init__(self, ap, blocks):
        self.ap = ap
        self.blocks = blocks

    def re(self, pat, **kw):
        return View(self.ap.rearrange(pat, **kw), self.blocks)


class Buf:
    def __init__(self, sch, name, free_shape, dtype, offset=None, parts=128, space="sb"):
        self.sch = sch
        self.name = name
        self.shape = tuple(free_shape)
        self.dtype = dtype
        self.isz = _ISZ[dtype]
        self.space = space
        self.parts = parts
        nc = sch.nc
        if space == "sb":
            self.off = offset
            self.h = nc.alloc_sbuf_tensor_at(name, [parts] + list(free_shape), dtype, offset=SB_BASE + offset)
            self.gran = 64
        else:
            self.off = 0
            self.h = nc.alloc_psum_tensor(name, [parts] + list(free_shape), dtype)
            self.gran = 2048
        st = []
        s = 1
        for d in reversed(self.shape):
            st.append(s)
            s *= d
        self.strides = tuple(reversed(st))
        self.nbytes = s * self.isz

    def _runs(self, key):
        shape, strides = self.shape, self.strides
        n = len(shape)
        d0 = n - 1
        while d0 > 0 and key[d0] == (0, shape[d0]):
            d0 -= 1
        runs = []

        def rec(d, base):
            a, b = key[d]
            if d == d0:
                runs.append((base + a * strides[d], base + b * strides[d]))
            else:
                for i in range(a, b):
                    rec(d + 1, base + i * strides[d])

        rec(0, 0)
        return runs

    def __getitem__(self, key):
        if not isinstance(key, tuple):
            key = (key,)
        pk = key[0]
        fk = list(key[1:])
        while len(fk) < len(self.shape):
            fk.append(slice(None))
        norm = []
        for d, k in enumerate(fk):
            if isinstance(k, int):
                norm.append((k, k + 1))
            else:
                a = 0 if k.start is None else k.start
                b = self.shape[d] if k.stop is None else k.stop
                norm.append((a, b))
        blocks = set()
        g = self.gran
        for (a, b) in self._runs(norm):
            a = self.off + a * self.isz
            b = self.off + b * self.isz
            for blk in range(a // g, (b - 1) // g + 1):
                blocks.add((self.space, blk))
        ap = self.h[(pk,) + tuple(fk)]
        return View(ap, blocks)


class Sched:
    ENGS = ("pe", "act", "dve", "pool", "sp")

    def __init__(self, nc, n_dsem=6):
        self.nc = nc
        self.q = {e: [] for e in self.ENGS}
        self.mem = {}
        self.n_dsem = n_dsem
        self.dma_count = {"sp": 0, "pool": 0, "act": 0}
        self.dsem_cnt = {}
        self.arena = nc.alloc_sbuf_tensor("arena", [128, SB_SIZE // 4], F32)
        assert nc.lookup_mloc(self.arena).addr == SB_BASE

    def add(self, eng, fn, reads=(), writes=(), dma=False):
        op = Op()
        op.eng = eng
        op.fn = fn
        op.is_dma = dma
        op.sig = dma
        op.seq = 0
        op.idx = len(self.q[eng])
        op.dsem = None
        op.dval = 0
        deps = {}
        mem = self.mem

        def dep(o, raw):
            if o is None or o is op:
                return
            k = id(o)
            if k in deps:
                deps[k] = (o, deps[k][1] or raw)
            else:
                deps[k] = (o, raw)

        rblocks = set()
        for v in reads:
            rblocks |= v.blocks
        wblocks = set()
        for v in writes:
            wblocks |= v.blocks
        for b in rblocks:
            st = mem.get(b)
            if st is not None:
                dep(st[0], True)
        for b in wblocks:
            st = mem.get(b)
            if st is not None:
                dep(st[0], False)
                for r in st[1]:
                    dep(r, False)
        for b in rblocks:
            st = mem.get(b)
            if st is None:
                mem[b] = [None, [op]]
            else:
                st[1].append(op)
        for b in wblocks:
            mem[b] = [op, []]
        best = {}
        final = []
        for (o, raw) in deps.values():
            if o.is_dma:
                final.append(o)
                continue
            if o.eng == eng and not dma:
                if eng == "pe":
                    continue
            cur = best.get(o.eng)
            if cur is None or o.idx > cur.idx:
                best[o.eng] = o
        final.extend(best.values())
        op.deps = final
        for o in final:
            o.sig = True
        if dma:
            k = self.dma_count[eng] % self.n_dsem
            self.dma_count[eng] += 1
            op.dsem = (eng, k)
            c = self.dsem_cnt.get(op.dsem, 0) + 16
            self.dsem_cnt[op.dsem] = c
            op.dval = c
        self.q[eng].append(op)
        return op

    def pe(self, fn, reads=(), writes=()):
        return self.add("pe", fn, reads, writes)

    def act(self, fn, reads=(), writes=()):
        return self.add("act", fn, reads, writes)

    def dve(self, fn, reads=(), writes=()):
        return self.add("dve", fn, reads, writes)

    def pool(self, fn, reads=(), writes=()):
        return self.add("pool", fn, reads, writes)

    def dma(self, eng, out_ap, in_ap, reads=(), writes=()):
        return self.add(eng, lambda e: e.dma_start(out=out_ap, in_=in_ap), reads, writes, dma=True)

    def emit(self, final_waits=()):
        nc = self.nc
        for o in final_waits:
            o.sig = True
        for e in self.ENGS:
            s = 0
            for op in self.q[e]:
                if op.sig and not op.is_dma:
                    s += 1
                    op.seq = s
        from contextlib import ExitStack
        with ExitStack() as es:
            esem = {e: es.enter_context(nc.semaphore("s_" + e)) for e in self.ENGS}
            dsem = {}
            for (q, k) in self.dsem_cnt:
                dsem[(q, k)] = es.enter_context(nc.semaphore("d_%s_%d" % (q, k)))
            block = es.enter_context(nc.Block())

            def run(ename, e):
                waited = {}
                for op in self.q[ename]:
                    for o in op.deps:
                        if o.is_dma:
                            sem, val, key = dsem[o.dsem], o.dval, o.dsem
                        else:
                            sem, val, key = esem[o.eng], o.seq, o.eng
                        if waited.get(key, 0) < val:
                            e.wait_ge(sem, val)
                            waited[key] = val
                    ins = op.fn(e)
                    if op.is_dma:
                        ins.then_inc(dsem[op.dsem], 16)
                    elif op.sig:
                        ins.then_inc(esem[ename], 1)
                if ename == "sp":
                    for o in final_waits:
                        if o.is_dma:
                            e.wait_ge(dsem[o.dsem], o.dval)
                        else:
                            e.wait_ge(esem[o.eng], o.seq)

            @block.tensor
            def _(e):
                run("pe", e)

            @block.scalar
            def _(e):
                run("act", e)

            @block.vector
            def _(e):
                run("dve", e)

            @block.gpsimd
            def _(e):
                run("pool", e)

            @block.sync
            def _(e):
                run("sp", e)


D = 1024
T = 2048
NTT = 4
DEPTH = 4
ALPHA = (2 * DEPTH) ** 0.25
LN_EPS = 1e-5
RMS_EPS = 1e-6
ATT_SCALE = 96 ** -0.5
NV = 200
V_BADA, V_CAW, V_CAB, V_LAG, V_LAB, V_CBW, V_QG, V_KVG, V_LG, V_LB = 0, 24, 148, 152, 156, 160, 172, 175, 177, 185
NGRP = 16
TWO_PI = 6.283185307179586


def _chunk_cols():
    ar = np.arange(128)
    ch = []
    for c in range(3):
        ch.append(3584 + 128 * c + ar)
    for c in range(2):
        ch.append(3968 + 128 * c + ar)
    kr = np.full(128, -1)
    kr[64:96] = 4224 + np.arange(32)
    ch.append(kr)
    krs = np.full(128, -1)
    krs[64:80] = 4240 + np.arange(16)
    krs[80:96] = 4224 + np.arange(16)
    ch.append(krs)
    for c in range(4):
        ch.append(4256 + 128 * c + ar)
    ch.append(np.full(128, -1))
    for base in (512, 0, 1024):
        for c in range(4):
            ch.append(base + 128 * c + ar)
    for base in (2560, 1536, 2048, 3072):
        for c in range(4):
            ch.append(base + 128 * c + ar)
    for j in range(8):
        for base in (4768, 5792, 6816):
            ch.append(base + 128 * j + ar)
    assert len(ch) == 64
    return np.concatenate(ch)


def _prep_weights(inp, layers):
    cols = _chunk_cols()
    L = len(layers)
    f = np.float32
    wada = np.empty((L, 6, 128, 8, 512), f)
    win = np.empty((L, NGRP, 128, 8, 512), f)
    wuq = np.empty((L, 128, 3, 768), f)
    wuqs = np.empty((L, 128, 3, 256), f)
    wukv = np.empty((L, 128, 2, 1024), f)
    wy = np.empty((L, 8, 128, 3, 4, 128), f)
    wo = np.empty((L, 2, 128, 8, 512), f)
    vecs = np.zeros((128, L, NV), f)
    for i, l in enumerate(layers):
        wada[i] = np.asarray(inp["w_ada"][l]).reshape(8, 128, 6, 512).transpose(2, 1, 0, 3)
        w = np.asarray(inp["w_in"][l])
        wp = np.zeros((1024, cols.size), f)
        m = cols >= 0
        wp[:, m] = w[:, cols[m]]
        win[i] = wp.reshape(8, 128, NGRP, 512).transpose(2, 1, 0, 3)
        uq = np.asarray(inp["w_uq"][l])
        wuq[i] = uq.reshape(3, 128, 768).transpose(1, 0, 2)
        sc = np.concatenate([np.concatenate([h * 96 + 80 + np.arange(16), h * 96 + 64 + np.arange(16)]) for h in range(8)])
        wuqs[i] = uq[:, sc].reshape(3, 128, 256).transpose(1, 0, 2)
        ukv = np.asarray(inp["w_ukv"][l])
        kc_ = np.concatenate([h * 128 + np.arange(64) for h in range(8)])
        vc_ = np.concatenate([h * 128 + 64 + np.arange(64) for h in range(8)])
        wukv[i] = np.concatenate([ukv[:, kc_], ukv[:, vc_]], axis=1).reshape(2, 128, 1024).transpose(1, 0, 2)
        for b, nm in enumerate(("w_a_out", "w_b_out", "w_c_out")):
            wx = np.asarray(inp[nm][l])
            wy[i, :, :, b] = wx.reshape(4, 128, 8, 128).transpose(2, 1, 0, 3)
        wo[i] = np.asarray(inp["w_o"][l]).reshape(8, 128, 2, 512).transpose(2, 1, 0, 3)
        v = vecs[:, i]
        v[:, V_BADA:V_BADA + 24] = np.asarray(inp["b_ada"][l]).reshape(24, 128).T
        v[:, V_CAW:V_CAW + 124] = np.asarray(inp["conv_a_w"][l]).reshape(31, 4, 128).transpose(2, 1, 0).reshape(128, 124)
        v[:, V_CAB:V_CAB + 4] = np.asarray(inp["conv_a_b"][l]).reshape(4, 128).T
        v[:, V_LAG:V_LAG + 4] = np.asarray(inp["ln_a_g"][l]).reshape(4, 128).T
        v[:, V_LAB:V_LAB + 4] = np.asarray(inp["ln_a_b"][l]).reshape(4, 128).T
        v[:, V_CBW:V_CBW + 12] = np.asarray(inp["conv_b_w"][l]).reshape(3, 4, 128).transpose(2, 1, 0).reshape(128, 12)
        v[:, V_QG:V_QG + 3] = np.asarray(inp["q_norm_g"][l]).reshape(3, 128).T
        v[:, V_KVG:V_KVG + 2] = np.asarray(inp["kv_norm_g"][l]).reshape(2, 128).T
        v[:, V_LG:V_LG + 8] = np.asarray(inp["ln_g"][l]).reshape(8, 128).T
        v[:, V_LB:V_LB + 8] = np.asarray(inp["ln_b"][l]).reshape(8, 128).T
    cst = np.zeros((128, 256), f)
    invc = np.zeros((128, 1), f)
    cst[:, 0:128] = np.eye(128, dtype=f)
    kk = np.arange(128)[:, None]
    qq = np.arange(128)[None, :]
    cst[:, 128:256] = np.where(kk > qq, -30000.0, 0.0)
    invf = (10000.0 ** (-np.arange(0, 32, 2, dtype=np.float32) / 32)).astype(f)
    for p in range(64, 96):
        invc[p, 0] = invf[(p - 64) % 16]
    return dict(invf=invc, wada=wada, win=win, wuq=wuq, wuqs=wuqs, wukv=wukv, wy=wy, wo=wo, vecs=vecs, cst=cst)


def build(L):
    nc = bass.Bass("TRN2", target_bir_lowering=False)

    def din(name, shape, dt=F32):
        return nc.dram_tensor(name, list(shape), dt, kind="ExternalInput").ap()

    x_d = din("x", [128, 8, T])
    c_d = din("cT", [128, 8])
    pos_d = din("pos", [128, T], I32)
    cst_d = din("cst", [128, 256])
    invf_d = din("invf", [128, 1])
    wada_d = din("wada", [L, 6, 128, 8, 512])
    win_d = din("win", [L, NGRP, 128, 8, 512])
    wuq_d = din("wuq", [L, 128, 3, 768])
    wuqs_d = din("wuqs", [L, 128, 3, 256])
    wukv_d = din("wukv", [L, 128, 2, 1024])
    wy_d = din("wy", [L, 8, 128, 3, 4, 128])
    wo_d = din("wo", [L, 2, 128, 8, 512])
    vecs_d = din("vecs", [128, L, NV])
    y_d = nc.dram_tensor("y", [128, 8, T], F32, kind="ExternalOutput").ap()
    rope_d = nc.dram_tensor("rope_scr", [2, 32, T], F32).ap()

    S = Sched(nc)

    def sb(name, shape, dt, off):
        b = Buf(S, name, shape, dt, offset=off)
        assert off + b.nbytes <= SB_SIZE, name
        return b

    xT = sb("xT", [8, T], F32, 0)
    O_SM = 65536
    ident = sb("ident", [128], BF16, O_SM)
    maskn = sb("maskn", [128], BF16, O_SM + 256)
    ones_bf = sb("ones_bf", [128], BF16, O_SM + 512)
    onesf = sb("onesf", [64], F32, O_SM + 768)
    invf = sb("invf", [1], F32, O_SM + 1024)
    cact = sb("cact", [8], F32, O_SM + 1088)
    cact_bf = sb("cact_bf", [8], BF16, O_SM + 1152)
    ada = sb("ada", [24], F32, O_SM + 1216)
    gq = sb("gq", [8], F32, O_SM + 1344)
    gatea = sb("gatea", [8], F32, O_SM + 1408)
    one1 = sb("one1", [1], F32, O_SM + 1472)
    epsv = sb("epsv", [4], F32, O_SM + 1504)
    vecs = sb("vecs", [L, NV], F32, O_SM + 1536)
    expo = sb("expo", [512], F32, O_SM + 1536 + 3200)
    adarow = sb("adarow", [512], F32, O_SM + 6784)
    ada_t = [ada, sb("ada1", [24], F32, O_SM + 8832)]
    gq_t = [gq, sb("gq1", [8], F32, O_SM + 8928)]
    gatea_t = [gatea, sb("gatea1", [8], F32, O_SM + 8960)]
    Gp = sb("Gp", [8], F32, O_SM + 8992)
    Bp = sb("Bp", [8], F32, O_SM + 9024)
    O_FIN = O_SM + 9216
    c_fin = sb("c_fin", [4, T], BF16, O_FIN)
    a_fin = sb("a_fin", [4, T], BF16, O_FIN + 16384)
    b_fin = sb("b_fin", [4, T], BF16, O_FIN + 32768)
    O_U = O_FIN + 49152
    uT = sb("uT", [8, T], BF16, O_U)
    O_S = O_U + 32768
    S_SIZE = SB_SIZE - O_S
    NSLOT = 2
    wbuf = [sb("wbuf%d" % i, [8, 512], BF16, O_S + 8192 * i) for i in range(NSLOT)]
    SS = O_S + 8192 * NSLOT
    ps = Buf(S, "ps", [8, 512], F32, space="ps")

    gcs = sb("gcs", [4, T], BF16, O_FIN + 16384)
    qn = sb("qn", [3, T], BF16, O_FIN + 32768)
    kpe = sb("kpe", [T], BF16, O_FIN + 32768 + 12288)
    kvn = sb("kvn", [2, T], BF16, SS)
    sqb = sb("sqb", [5, 512], BF16, SS + 8192)
    rq = sb("rq", [512], F32, SS + 13312)
    rkv = sb("rkv", [512], F32, SS + 15360)
    cosT = sb("cosT", [T], F32, SS + 17408)
    sinT = sb("sinT", [T], F32, SS + 25600)
    rt1 = sb("rt1", [512], F32, SS + 33792)
    rt2 = sb("rt2", [512], F32, SS + 35840)
    wuq_sb = sb("wuq_sb", [3, 768], BF16, SS + 8192)
    wukv_sb = sb("wukv_sb", [2, 1024], BF16, SS + 12800)
    wuqs_sb = sb("wuqs_sb", [3, 256], BF16, SS + 37888)
    tang = sb("tang", [T], F32, O_FIN)
    ttmp = sb("ttmp", [T], F32, O_FIN + 8192)
    tki = sb("tki", [T], I32, O_FIN + 8192)
    qTb = [sb("qT%d" % i, [T], BF16, O_U + 4096 * i) for i in range(2)]
    kTb = [sb("kT%d" % i, [T], BF16, O_U + 8192 + 4096 * i) for i in range(2)]
    Vp = sb("Vp", [16, 192], BF16, O_U + 16384)
    rs = sb("rs", [512], F32, O_U + 22528)
    Msb = sb("Msb", [512], F32, O_U + 24576)
    pbuf = [sb("pbuf%d" % i, [512], BF16, O_U + 26624 + 1024 * i) for i in range(4)]
    a_pad = sb("a_pad", [4, 2080], BF16, SS)
    ac = sb("ac", [4, 1024], F32, SS + 16896)
    acbs = [sb("acb%d" % i, [1, 1024], BF16, O_FIN + 32768 + 8192 + 2048 * i) for i in range(2)]
    sqas = [sb("sqa%d" % i, [1, 1024], BF16, O_FIN + 32768 + 12288 + 2048 * i) for i in range(2)]
    mean_a = sb("mean_a", [1024], F32, O_FIN + 32768)
    rstd_a = sb("rstd_a", [1024], F32, O_FIN + 32768 + 4096)
    diag = [sb("diag%d" % i, [128], BF16, SS + 33280 + 256 * i) for i in range(16)]
    p_pad = sb("p_pad", [4, 2112], BF16, SS)
    gtmp = sb("gtmp", [T], BF16, O_FIN + 32768 + 8192)
    stmp = sb("stmp", [T], BF16, SS + 20992)
    diagb = [sb("diagb%d" % i, [128], BF16, SS + 25088 + 256 * i) for i in range(8)]
    mT = sb("mT", [8, 1024], BF16, SS)
    wyb = [sb("wy%d" % i, [3, 4, 128], BF16, SS + 16384 + 3072 * i) for i in range(2)]
    wo_h = [sb("woA", [8, 512], BF16, SS + 16384), sb("woB", [8, 512], BF16, SS + 26624)]
    sg = [sb("sg%d" % i, [512], F32, SS + 26624 + 2048 * i) for i in range(3)]
    zb = [sb("zb%d" % i, [512], BF16, SS + 34816 + 1024 * i) for i in range(2)]
    zq = [sb("zq%d" % i, [512], BF16, SS + 36864 + 1024 * i) for i in range(2)]
    msq = sb("msq", [512], F32, SS + 24576)
    assert SS + 39424 <= SB_SIZE

    bank_rr = [0]

    def nb(pool_=(0, 1, 2, 3, 4, 5, 6, 7)):
        b = pool_[bank_rr[0] % len(pool_)]
        bank_rr[0] += 1
        return b

    def mm(out, lhsT, rhs, start, stop):
        S.pe(lambda e: e.matmul(out.ap, lhsT=lhsT.ap, rhs=rhs.ap, start=start, stop=stop),
             reads=[lhsT, rhs], writes=[out])

    def actf(out, in_, func, bias=None, scale=None, extra_reads=()):
        kw = {}
        if bias is not None:
            kw["bias"] = bias.ap if isinstance(bias, View) else bias
        if scale is not None:
            kw["scale"] = scale.ap if isinstance(scale, View) else scale
        rd = [in_] + [v for v in (bias, scale) if isinstance(v, View)] + list(extra_reads)
        S.act(lambda e: e.activation(out=out.ap, in_=in_.ap, func=func, **kw), reads=rd, writes=[out])

    def tt_(eng, out, a, b, op):
        S.add(eng, lambda e: e.tensor_tensor(out=out.ap, in0=a.ap, in1=b.ap, op=op), reads=[a, b], writes=[out])

    def ts_(eng, out, a, s1, s2, op0, op1=None):
        rd = [a] + [v for v in (s1, s2) if isinstance(v, View)]
        s1a = s1.ap if isinstance(s1, View) else s1
        s2a = s2.ap if isinstance(s2, View) else s2
        if op1 is None:
            S.add(eng, lambda e: e.tensor_scalar(out=out.ap, in0=a.ap, scalar1=s1a, scalar2=None, op0=op0), reads=rd, writes=[out])
        else:
            S.add(eng, lambda e: e.tensor_scalar(out=out.ap, in0=a.ap, scalar1=s1a, scalar2=s2a, op0=op0, op1=op1), reads=rd, writes=[out])

    def stt(out, a, s, b, op0, op1):
        rd = [a, b] + ([s] if isinstance(s, View) else [])
        sa = s.ap if isinstance(s, View) else s
        S.dve(lambda e: e.scalar_tensor_tensor(out=out.ap, in0=a.ap, scalar=sa, in1=b.ap, op0=op0, op1=op1), reads=rd, writes=[out])

    def cp(eng, out, in_):
        S.add(eng, lambda e: e.tensor_copy(out=out.ap, in_=in_.ap), reads=[in_], writes=[out])

    def rsqrt_eps(out, in_, eps_col, pw=-0.5):
        if eps_col is None:
            actf(out, in_, AF.Ln)
        else:
            actf(out, in_, AF.Ln, bias=epsv[in_part(out), eps_col:eps_col + 1])
        actf(out, out, AF.Exp, scale=pw)

    def in_part(v):
        return slice(None)

    def ps4(bs):
        return ps[:, bs:bs + 4, :].re("p a b -> p (a b)")

    def ps2(bs):
        return ps[:, bs:bs + 2, :].re("p a b -> p (a b)")

    bset_c = [0]

    def bset():
        bset_c[0] += 1
        return (bset_c[0] % 2) * 4

    stream = []
    for g in range(6):
        stream.append(wada_d[0, g])
    for l in range(L):
        for g in range(16):
            stream.append(win_d[l, g])
        if l + 1 < L:
            for g in range(6):
                stream.append(wada_d[l + 1, g])
        for g in range(10, 16):
            stream.append(win_d[l, g])
    issued = [0]

    def grp(si):
        while issued[0] < min(len(stream), si + 1):
            k = issued[0]
            b = wbuf[k % NSLOT]
            S.dma("pool", b[:].ap, stream[k], writes=[b[:]])
            issued[0] += 1
        return wbuf[si % NSLOT]

    spos = [0]

    S.dma("sp", xT[:].ap, x_d, writes=[xT[:]])
    S.dma("sp", cact[:].ap, c_d, writes=[cact[:]])
    S.dma("sp", vecs[:].ap, vecs_d, writes=[vecs[:]])
    S.dma("sp", invf[:].ap, invf_d, writes=[invf[:]])
    S.dma("pool", ident[:].ap, cst_d[:, 0:128], writes=[ident[:]])
    S.dma("pool", maskn[:].ap, cst_d[:, 128:256], writes=[maskn[:]])
    grp(1)
    S.pool(lambda e: e.memset(ones_bf[:].ap, 1.0), writes=[ones_bf[:]])
    S.pool(lambda e: e.memset(onesf[:].ap, 1.0), writes=[onesf[:]])
    S.pool(lambda e: e.memset(one1[:].ap, 1.0), writes=[one1[:]])
    S.pool(lambda e: e.memset(epsv[:, 0:1].ap, 384 * RMS_EPS), writes=[epsv[:]])
    S.pool(lambda e: e.memset(epsv[:, 1:2].ap, 256 * RMS_EPS), writes=[epsv[:]])
    S.pool(lambda e: e.memset(epsv[:, 2:3].ap, LN_EPS), writes=[epsv[:]])
    S.pool(lambda e: e.memset(epsv[:, 3:4].ap, LN_EPS / (ALPHA * ALPHA)), writes=[epsv[:]])
    actf(cact[:], cact[:], AF.Silu)
    cp("dve", cact_bf[:], cact[:])

    def vcol(l, idx, n=1):
        return vecs[:, l, idx:idx + n]

    def ada_steps(l):
        ada, gq, gatea = ada_t[l % 2], gq_t[l % 2], gatea_t[l % 2]
        steps = []
        for g in range(6):
            def st(g=g):
                wg = grp(spos[0])
                grp(spos[0] + 1)
                for kc in range(8):
                    mm(ps[0:1, 6, :], cact_bf[:, kc:kc + 1], wg[:, kc, :], kc == 0, kc == 7)
                spos[0] += 1
                cp("dve", adarow[0:1, :], ps[0:1, 6, :])
                for jj in range(4):
                    mm(ps[:, 7, g * 4 + jj: g * 4 + jj + 1], adarow[0:1, jj * 128:(jj + 1) * 128], one1[0:1, :], True, True)
            steps.append(st)

        def fin():
            tt_("dve", ada[:, 0:24], ps[:, 7, 0:24], vcol(l, V_BADA, 24), ALU.add)
            ts_("dve", ada[:, 8:16], ada[:, 8:16], 1.0, None, ALU.add)
            ts_("dve", gatea[:], ada[:, 16:24], 1.0 / ALPHA, None, ALU.mult)
            ts_("dve", gq[:, 0:3], vcol(l, V_QG, 3), float(384 ** 0.5), None, ALU.mult)
            ts_("dve", gq[:, 3:5], vcol(l, V_KVG, 2), float(256 ** 0.5), None, ALU.mult)
            if l >= 1:
                tt_("dve", Gp[:], vcol(l - 1, V_LG, 8), ada[:, 8:16], ALU.mult)
                tt_("dve", Bp[:], vcol(l - 1, V_LB, 8), ada[:, 8:16], ALU.mult)
                tt_("dve", Bp[:], Bp[:], ada[:, 0:8], ALU.add)
        steps.append(fin)
        return steps

    def ada_compute(l):
        for st in ada_steps(l):
            st()

    lnq = []

    def drain(n=10 ** 9):
        while lnq and n > 0:
            lnq.pop(0)()
            n -= 1

    def ln_pieces(l, tts, fuse_u):
        b1, b2 = 6, 7
        for tt in tts:
            ts = slice(tt * 512, (tt + 1) * 512)

            def stat_mm(c):
                mm(ps[:, b1, :], ones_bf[:], zb[c % 2][:], c == 0, c == 7)
                mm(ps[:, b2, :], ones_bf[:], zq[c % 2][:], c == 0, c == 7)

            for c in range(8):
                def p1(c=c, ts=ts, stat_mm=stat_mm):
                    cp("dve", zb[c % 2][:], xT[:, c, ts])
                    actf(zq[c % 2][:], xT[:, c, ts], AF.Square)
                    if c > 0:
                        stat_mm(c - 1)
                lnq.append(p1)

            def chain(stat_mm=stat_mm):
                stat_mm(7)
                actf(msq[:], ps[:, b1, :], AF.Square, scale=1.0 / D)
                ts_("dve", ps[:, b1, :], ps[:, b1, :], 1.0 / D, None, ALU.mult)
                stt(ps[:, b2, :], ps[:, b2, :], 1.0 / D, msq[:], ALU.mult, ALU.subtract)
                rsqrt_eps(ps[:, b2, :], ps[:, b2, :], 3)
            lnq.append(chain)
            for c in range(8):
                def p2(c=c, ts=ts):
                    tt_("dve", xT[:, c, ts], xT[:, c, ts], ps[:, b1, :], ALU.subtract)
                    tt_("dve", xT[:, c, ts], xT[:, c, ts], ps[:, b2, :], ALU.mult)
                    if fuse_u:
                        actf(uT[:, c, ts], xT[:, c, ts], AF.Identity, bias=Bp[:, c:c + 1], scale=Gp[:, c:c + 1])
                    actf(xT[:, c, ts], xT[:, c, ts], AF.Identity, bias=vcol(l, V_LB + c), scale=vcol(l, V_LG + c))
                lnq.append(p2)

    def layer(l):
        ada, gq, gatea = ada_t[l % 2], gq_t[l % 2], gatea_t[l % 2]

        def make_u():
            for tt in range(NTT):
                ts = slice(tt * 512, (tt + 1) * 512)
                for c in range(8):
                    if c % 2 == 0:
                        actf(uT[:, c, ts], xT[:, c, ts], AF.Identity, bias=ada[:, c:c + 1], scale=ada[:, 8 + c:9 + c])
                    else:
                        ts_("dve", uT[:, c, ts], xT[:, c, ts], ada[:, 8 + c:9 + c], ada[:, c:c + 1], ALU.mult, ALU.add)

        if l == 0:
            make_u()
        R = slice(64, 96)
        g0 = spos[0]

        def wchunk(ci):
            gi = g0 + ci // 4
            wgb = grp(gi)
            return wgb, (ci % 4) * 128

        ln_busy = bool(lnq)
        for tt in range(NTT):
            ts = slice(tt * 512, (tt + 1) * 512)
            banks = []
            for ci in range(5):
                wgb, co = wchunk(ci)
                b = nb((0, 1, 2, 3, 4)) if ln_busy else nb((0, 1, 2, 3, 4, 7))
                banks.append(b)
                for kc in range(8):
                    mm(ps[:, b, :], wgb[:, kc, co:co + 128], uT[:, kc, ts], kc == 0, kc == 7)
                actf(sqb[:, ci, :], ps[:, b, :], AF.Square)
                drain(2)
            if tt == 1:
                drain()
            for (lo, hi, sbank, rbuf, n) in ((0, 3, 5, rq, 384), (3, 5, 5 if ln_busy else 6, rkv, 256)):
                for ci in range(lo, hi):
                    mm(ps[:, sbank, :], ones_bf[:], sqb[:, ci, :], ci == lo, ci == hi - 1)
                rsqrt_eps(rbuf[:], ps[:, sbank, :], 0 if n == 384 else 1)
                for ci in range(lo, hi):
                    dst = qn[:, ci, ts] if ci < 3 else kvn[:, ci - 3, ts]
                    stt(dst, ps[:, banks[ci], :], gq[:, ci:ci + 1], rbuf[:], ALU.mult, ALU.mult)
            if tt == 1:
                ln_busy = False
        S.dma("pool", wuq_sb[:].ap, wuq_d[l], writes=[wuq_sb[:]])
        S.dma("pool", wuqs_sb[:].ap, wuqs_d[l], writes=[wuqs_sb[:]])
        S.dma("pool", wukv_sb[:].ap, wukv_d[l], writes=[wukv_sb[:]])
        dr_cos, dr_sin = View(rope_d[0], {("dr", 0)}), View(rope_d[1], {("dr", 1)})
        if l > 0:
            S.dma("sp", cosT[R, :].ap, dr_cos.ap, reads=[dr_cos], writes=[cosT[R, :]])
            S.dma("sp", sinT[R, :].ap, dr_sin.ap, reads=[dr_sin], writes=[sinT[R, :]])
        else:
            S.dma("sp", tki[R, :].ap, pos_d[64:96, :], writes=[tki[R, :]])
            cp("dve", tang[R, :], tki[R, :])
            ts_("dve", tang[R, :], tang[R, :], invf[R, :], None, ALU.mult)
            ts_("dve", ttmp[R, :], tang[R, :], 1.0 / TWO_PI, None, ALU.mult)
            cp("dve", tki[R, :], ttmp[R, :])
            cp("dve", ttmp[R, :], tki[R, :])
            C1 = 6.28125
            C2 = TWO_PI - C1
            stt(tang[R, :], ttmp[R, :], -C1, tang[R, :], ALU.mult, ALU.add)
            stt(tang[R, :], ttmp[R, :], -C2, tang[R, :], ALU.mult, ALU.add)
            ts_("dve", ttmp[R, :], tang[R, :], float(np.pi), None, ALU.is_gt)
            stt(tang[R, :], ttmp[R, :], -TWO_PI, tang[R, :], ALU.mult, ALU.add)
            ts_("dve", ttmp[R, :], tang[R, :], float(-np.pi), None, ALU.is_lt)
            stt(tang[R, :], ttmp[R, :], TWO_PI, tang[R, :], ALU.mult, ALU.add)
            ts_("dve", tang[R, :], tang[R, :], 3.1415925, -3.1415925, ALU.min, ALU.max)
            actf(sinT[R, :], tang[R, :], AF.Sin)
            ts_("dve", ttmp[R, :], tang[R, :], float(np.pi / 2), None, ALU.is_gt)
            stt(tang[R, :], ttmp[R, :], -TWO_PI, tang[R, :], ALU.mult, ALU.add)
            ts_("dve", tang[R, :], tang[R, :], float(np.pi / 2), None, ALU.add)
            ts_("dve", tang[R, :], tang[R, :], 3.1415925, -3.1415925, ALU.min, ALU.max)
            actf(cosT[R, :], tang[R, :], AF.Sin)
            ts_("dve", sinT[64:80, :], sinT[64:80, :], -1.0, None, ALU.mult)

            S.dma("sp", dr_cos.ap, cosT[R, :].ap, reads=[cosT[R, :]], writes=[dr_cos])
            S.dma("sp", dr_sin.ap, sinT[R, :].ap, reads=[sinT[R, :]], writes=[dr_sin])
        grp(g0 + 2)
        for c in range(4):
            wgb, co = wchunk(7 + c)
            bs = (c % 2) * 4
            for tt in range(NTT):
                ts = slice(tt * 512, (tt + 1) * 512)
                for kc in range(8):
                    mm(ps[:, bs + tt, :], wgb[:, kc, co:co + 128], uT[:, kc, ts], kc == 0, kc == 7)
            actf(gcs[:, c, :], ps4(bs), AF.Silu)
        for tt in range(NTT):
            ts = slice(tt * 512, (tt + 1) * 512)
            bk = []
            for ci in (5, 6):
                wgb, co = wchunk(ci)
                b = nb((0, 1, 2, 3, 4))
                bk.append(b)
                for kc in range(8):
                    mm(ps[0:96, b, :], wgb[:, kc, co:co + 96], uT[:, kc, ts], kc == 0, kc == 7)
            tt_("dve", rt1[R, :], ps[R, bk[0], :], cosT[R, ts], ALU.mult)
            tt_("dve", rt2[R, :], ps[R, bk[1], :], sinT[R, ts], ALU.mult)
            tt_("dve", kpe[R, ts], rt1[R, :], rt2[R, :], ALU.add)
        spos[0] = g0 + 3
        grp(g0 + 3)
        grp(g0 + 4)

        for qb_ in qTb + kTb:
            S.pool(lambda e, qb_=qb_: e.memset(qb_[96:128, :].ap, 0.0), writes=[qb_[96:128, :]])
        S.pool(lambda e: e.memset(Vp[:, :, 64:128].ap, 0.0), writes=[Vp[:]])
        S.pool(lambda e: e.memset(Vp[:, :, 64:65].ap, 1.0), writes=[Vp[:]])
        AP_ = (2, 3, 4, 5, 6, 7)

        def prod_qk(h):
            qT, kT = qTb[h % 2], kTb[h % 2]
            for tt in range(NTT):
                ts = slice(tt * 512, (tt + 1) * 512)
                b = nb(AP_)
                for kc in range(2):
                    mm(ps[0:64, b, :], wukv_sb[:, kc, h * 64:(h + 1) * 64], kvn[:, kc, ts], kc == 0, kc == 1)
                cp("dve", kT[0:64, ts], ps[0:64, b, :])
                bq = nb(AP_)
                for kc in range(3):
                    mm(ps[0:96, bq, :], wuq_sb[:, kc, h * 96:(h + 1) * 96], qn[:, kc, ts], kc == 0, kc == 2)
                bs_ = nb(AP_)
                for kc in range(3):
                    mm(ps[64:96, bs_, :], wuqs_sb[:, kc, h * 32:(h + 1) * 32], qn[:, kc, ts], kc == 0, kc == 2)
                cp("dve", qT[0:64, ts], ps[0:64, bq, :])
                tt_("dve", rt1[R, :], ps[R, bq, :], cosT[R, ts], ALU.mult)
                tt_("dve", rt2[R, :], ps[R, bs_, :], sinT[R, ts], ALU.mult)
                tt_("dve", qT[R, ts], rt1[R, :], rt2[R, :], ALU.add)
            cp("dve", kT[R, :], kpe[R, :])

        def prod_v(j):
            for g4 in range(4):
                b = nb(AP_)
                for k4 in range(4):
                    kt = g4 * 4 + k4
                    for kc in range(2):
                        mm(ps[:, b, k4 * 128:(k4 + 1) * 128], kvn[:, kc, kt * 128:(kt + 1) * 128],
                           wukv_sb[:, kc, 512 + j * 128:512 + (j + 1) * 128], kc == 0, kc == 1)
                pv = ps[:, b, :].re("p (k c) -> p k c", c=128)
                S.dve(lambda e, pv=pv, g4=g4: e.tensor_copy(out=Vp[:, g4 * 4:(g4 + 1) * 4, 0:64].ap, in_=pv.ap[:, :, 0:64]),
                      reads=[pv], writes=[Vp[:, g4 * 4:(g4 + 1) * 4, :]])
                S.dve(lambda e, pv=pv, g4=g4: e.tensor_copy(out=Vp[:, g4 * 4:(g4 + 1) * 4, 128:192].ap, in_=pv.ap[:, :, 64:128]),
                      reads=[pv], writes=[Vp[:, g4 * 4:(g4 + 1) * 4, :]])

        def attn(h):
            j, odd = h // 2, h % 2
            qT, kT = qTb[h % 2], kTb[h % 2]
            for tt in range(NTT):
                ob = tt % 2
                nk = 4 * tt + 4
                steps = []
                for kt in range(nk):
                    q0 = max(tt * 512, kt * 128)
                    steps.append((kt, q0, (tt + 1) * 512 - q0, kt >= 4 * tt))
                LA = 3
                sb_ = {}

                def s_mm(i):
                    kt, q0, n, dg = steps[i]
                    b = nb(AP_[0:5])
                    sb_[i] = b
                    mm(ps[:, b, 0:n], kT[:, kt * 128:(kt + 1) * 128], qT[:, q0:q0 + n], True, not dg)
                    if dg:
                        mm(ps[:, b, 0:128], ident[:], maskn[:], False, True)

                for i in range(min(LA, nk)):
                    s_mm(i)
                while pend_norm:
                    pend_norm.pop(0)()
                for i in range(nk):
                    if i + LA < nk:
                        s_mm(i + LA)
                    kt, q0, n, dg = steps[i]
                    pb = pbuf[i % 4]
                    actf(pb[:, 0:n], ps[:, sb_[i], 0:n], AF.Exp, scale=ATT_SCALE)
                    o0 = q0 - tt * 512
                    if not odd:
                        mm(ps[:, ob, o0:512], Vp[:, kt, 0:128], pb[:, 0:n], i == 0, i == nk - 1)
                    else:
                        mm(ps[:, ob, o0:512], Vp[:, kt, 64:192], pb[:, 0:n], i == 0, i == nk - 1)
                ts = slice(tt * 512, (tt + 1) * 512)
                if not odd:
                    sr, orows = slice(64, 65), slice(0, 64)
                else:
                    sr, orows = slice(0, 1), slice(64, 128)
                actf(rs[sr, :], ps[sr, ob, :], AF.Ln)
                actf(rs[sr, :], rs[sr, :], AF.Exp, scale=-1.0)

                def norm_b(sr=sr, orows=orows, ob=ob, j=j, ts=ts):
                    mm(ps[orows, 7, :], onesf[sr, 0:64], rs[sr, :], True, True)
                    tt_("dve", Msb[orows, :], ps[orows, 7, :], gcs[orows, j, ts], ALU.mult)
                    tt_("dve", c_fin[orows, j, ts], ps[orows, ob, :], Msb[orows, :], ALU.mult)
                pend_norm.append(norm_b)

        pend_norm = []
        prod_qk(0)
        for h in range(8):
            if h % 2 == 0:
                prod_v(h // 2)
            if h + 1 < 8:
                prod_qk(h + 1)
            attn(h)
        while pend_norm:
            pend_norm.pop(0)()

        make_u()
        gA = spos[0]

        def wchunkA(ci):
            gi = gA + ci // 4
            return grp(gi), (ci % 4) * 128

        S.pool(lambda e: e.memset(a_pad[:, :, 0:32].ap, 0.0), writes=[a_pad[:, :, 0:32]])
        PADA = 32
        for c in range(4):
            wgb, co = wchunkA(c)
            bs = (c % 2) * 4
            for tt in range(NTT):
                for kc in range(8):
                    mm(ps[:, bs + tt, :], wgb[:, kc, co:co + 128], uT[:, kc, tt * 512:(tt + 1) * 512], kc == 0, kc == 7)
            actf(a_fin[:, c, :], ps4(bs), AF.Sigmoid)
        grp(gA + 2)
        for c in range(4):
            wgb, co = wchunkA(4 + c)
            bs = (c % 2) * 4
            for tt in range(NTT):
                for kc in range(8):
                    mm(ps[:, bs + tt, :], wgb[:, kc, co:co + 128], uT[:, kc, tt * 512:(tt + 1) * 512], kc == 0, kc == 7)
            tt_("dve", a_pad[:, c, PADA:PADA + T], ps4(bs), a_fin[:, c, :], ALU.mult)
        grp(gA + 3)
        for c in range(4):
            wgb, co = wchunkA(8 + c)
            bs = (c % 2) * 4
            for tt in range(NTT):
                for kc in range(8):
                    mm(ps[:, bs + tt, :], wgb[:, kc, co:co + 128], uT[:, kc, tt * 512:(tt + 1) * 512], kc == 0, kc == 7)
            actf(a_fin[:, c, :], ps4(bs), AF.Silu)
        spos[0] = gA + 3
        grp(gA + 4)
        dcnt = [0]
        pend_stats = []
        deferred = []
        ac2 = ac

        def emit_stats(c):
            for t2 in range(2):
                mm(ps[:, 4 + t2, :], ones_bf[:], acbs[c % 2][:, 0, t2 * 512:(t2 + 1) * 512], c == 0, c == 3)
                mm(ps[:, 6 + t2, :], ones_bf[:], sqas[c % 2][:, 0, t2 * 512:(t2 + 1) * 512], c == 0, c == 3)

        for hf in range(2):
            for c in range(4):
                bs = (c % 2) * 2
                acb, sqa = acbs[c % 2], sqas[c % 2]
                for k in range(31):
                    dg = diag[dcnt[0] % 16]
                    dcnt[0] += 1
                    ts_("dve", dg[:], ident[:], vcol(l, V_CAW + c * 31 + k), None, ALU.mult)
                    for t2 in range(2):
                        t0 = (hf * 2 + t2) * 512 + 2 + k
                        mm(ps[:, bs + t2, :], dg[:], a_pad[:, c, t0:t0 + 512], k == 0, k == 30)
                pc = ps2(bs)
                if deferred:
                    deferred.pop(0)()
                actf(ac[:, c, :], pc, AF.Identity, bias=vcol(l, V_CAB + c))
                actf(sqa[:, 0, :], pc, AF.Square, bias=vcol(l, V_CAB + c))
                actf(acb[:, 0, :], pc, AF.Identity, bias=vcol(l, V_CAB + c))
                pend_stats.append(c)
                if c > 0:
                    emit_stats(pend_stats.pop(0))
            emit_stats(pend_stats.pop(0))
            ts_("dve", mean_a[:], ps2(4), 1.0 / 512, None, ALU.mult)
            tt_("dve", rstd_a[:], mean_a[:], mean_a[:], ALU.mult)
            stt(rstd_a[:], ps2(6), 1.0 / 512, rstd_a[:], ALU.mult, ALU.subtract)
            rsqrt_eps(rstd_a[:], rstd_a[:], 2)
            hs_ = slice(hf * 1024, (hf + 1) * 1024)
            for c in range(4):
                def norm_piece(c=c, hs=hs_):
                    tt_("dve", ac2[:, c, :], ac2[:, c, :], mean_a[:], ALU.subtract)
                    stt(ac2[:, c, :], ac2[:, c, :], vcol(l, V_LAG + c), rstd_a[:], ALU.mult, ALU.mult)
                    actf(ac2[:, c, :], ac2[:, c, :], AF.Silu, bias=vcol(l, V_LAB + c))
                    tt_("dve", a_fin[:, c, hs], ac2[:, c, :], a_fin[:, c, hs], ALU.mult)
                deferred.append(norm_piece)

        gB = spos[0]

        def wchunkB(ci):
            gi = gB + ci // 4
            return grp(gi), (ci % 4) * 128

        S.pool(lambda e: e.memset(p_pad[:, :, 0:32].ap, 0.0), writes=[p_pad[:, :, 0:32]])

        def proj4(ci):
            wgb, co = wchunkB(ci)
            bs = bset()
            for tt in range(NTT):
                for kc in range(8):
                    mm(ps[:, bs + tt, :], wgb[:, kc, co:co + 128], uT[:, kc, tt * 512:(tt + 1) * 512], kc == 0, kc == 7)
            return ps4(bs)

        for c in range(4):
            pg = proj4(c)
            if deferred:
                deferred.pop(0)()
            actf(gtmp[:], pg, AF.Copy)
            px = proj4(4 + c)
            tt_("dve", p_pad[:, c, 32:32 + T], px, gtmp[:], ALU.mult)
        grp(gB + 2)
        grp(gB + 3)
        for c in range(4):
            pgb = proj4(8 + c)
            actf(b_fin[:, c, :], pgb, AF.Copy)
        grp(gB + 4)
        for c in range(4):
            pbg = proj4(12 + c)
            actf(stmp[:], pbg, AF.Silu)
            tt_("dve", b_fin[:, c, :], b_fin[:, c, :], stmp[:], ALU.mult)
        spos[0] = gB + 4
        grp(gB + 5)
        dcb = [0]
        for c in range(4):
            bs = (c % 2) * 4
            for k in range(3):
                dg = diagb[dcb[0] % 8]
                dcb[0] += 1
                ts_("dve", dg[:], ident[:], vcol(l, V_CBW + c * 3 + k), None, ALU.mult)
                for tt in range(NTT):
                    t0 = tt * 512 + 30 + k
                    mm(ps[:, bs + tt, :], dg[:], p_pad[:, c, t0:t0 + 512], k == 0, k == 2)
            tt_("dve", b_fin[:, c, :], ps4(bs), b_fin[:, c, :], ALU.mult)

        for hf in range(2):
            gH = spos[0]
            pool_ = (0, 1, 2, 3, 4, 5, 6, 7) if hf == 0 else (0, 1, 2, 3, 4, 5)
            for j in range(8):
                wb_ = wyb[j % 2]
                S.dma("pool", wb_[:].ap, wy_d[l, j], writes=[wb_[:]])
                for t2 in range(2):
                    tt = hf * 2 + t2
                    ts = slice(tt * 512, (tt + 1) * 512)
                    for br in range(3):
                        ci = j * 3 + br
                        wgb = grp(gH + ci // 4)
                        co = (ci % 4) * 128
                        b = nb(pool_)
                        for kc in range(8):
                            mm(ps[:, b, :], wgb[:, kc, co:co + 128], uT[:, kc, ts], kc == 0, kc == 7)
                        actf(sg[br][:], ps[:, b, :], AF.Sigmoid)
                    yb_ = []
                    for br, fin in enumerate((a_fin, b_fin, c_fin)):
                        b = nb(pool_)
                        yb_.append(b)
                        for kc in range(4):
                            mm(ps[:, b, :], wb_[:, br, kc, :], fin[:, kc, ts], kc == 0, kc == 3)
                    tt_("dve", sg[0][:], ps[:, yb_[0], :], sg[0][:], ALU.mult)
                    tt_("dve", sg[1][:], ps[:, yb_[1], :], sg[1][:], ALU.mult)
                    tt_("dve", sg[0][:], sg[0][:], sg[1][:], ALU.add)
                    tt_("dve", sg[2][:], ps[:, yb_[2], :], sg[2][:], ALU.mult)
                    tt_("dve", mT[:, j, t2 * 512:(t2 + 1) * 512], sg[0][:], sg[2][:], ALU.add)
                    drain(3)
                nxt = gH + ((j + 1) * 3 + 2) // 4 if j < 7 else gH + 6
                grp(nxt)
            drain()
            spos[0] = gH + 6
            ada_q = ada_steps(l + 1) if (hf == 0 and l + 1 < L) else []
            for eh in range(2):
                S.dma("pool", wo_h[eh][:].ap, wo_d[l, eh], writes=[wo_h[eh][:]])
            for t2 in range(2):
                tt = hf * 2 + t2
                ts = slice(tt * 512, (tt + 1) * 512)
                for e_ in range(8):
                    wo_ = wo_h[e_ // 4]
                    eo = (e_ % 4) * 128
                    b = nb((0, 1, 2, 3, 4, 5))
                    for kc in range(8):
                        mm(ps[:, b, :], wo_[:, kc, eo:eo + 128], mT[:, kc, t2 * 512:(t2 + 1) * 512], kc == 0, kc == 7)
                    stt(xT[:, e_, ts], ps[:, b, :], gatea[:, e_:e_ + 1], xT[:, e_, ts], ALU.mult, ALU.add)
                    if ada_q:
                        ada_q.pop(0)()
                    drain(2)
                while ada_q:
                    ada_q.pop(0)()
                ln_pieces(l, (tt,), l + 1 < L)
        if l + 1 == L:
            drain()

    ada_compute(0)
    for l in range(L):
        layer(l)
    outs = []
    for c in range(8):
        outs.append(S.dma("sp", y_d[:, c, :], xT[:, c, :].ap, reads=[xT[:, c, :]]))
    S.emit(final_waits=outs)
    return nc


_NC_CACHE = {}


def _get_nc(L):
    if L not in _NC_CACHE:
        _NC_CACHE[L] = build(L)
    return _NC_CACHE[L]


def _core_inputs(x_b, c_b, pos_b):
    xt = np.ascontiguousarray(x_b.T.reshape(8, 128, T).transpose(1, 0, 2))
    ct = np.ascontiguousarray(c_b.reshape(8, 128).T)
    pp = np.ascontiguousarray(np.broadcast_to(pos_b[None, :].astype(np.int32), (128, T)))
    return xt, ct, pp


FUSED = True


def kernel(**inputs):
    inp = {k: np.asarray(v) for k, v in inputs.items()}
    x = inp["x"].astype(np.float32)
    B = x.shape[0]
    groups = [list(range(DEPTH))] if FUSED else [[l] for l in range(DEPTH)]
    cur = [None] * B
    for b in range(B):
        cur[b] = _core_inputs(x[b], inp["c"][b], inp["positions"][b])
    for layers in groups:
        w = _prep_weights(inp, layers)
        nc = _get_nc(len(layers))
        in_maps = []
        for b in range(B):
            m = dict(w)
            m["x"], m["cT"], m["pos"] = cur[b]
            in_maps.append(m)
        res = run_bass_kernel_spmd(nc, in_maps, core_ids=list(range(B)))
        for b in range(B):
            cur[b] = (np.ascontiguousarray(res.results[b]["y"]), cur[b][1], cur[b][2])
    out = np.empty((B, T, D), np.float32)
    for b in range(B):
        out[b] = cur[b][0].transpose(1, 0, 2).reshape(D, T).T
    return out
```

```python
import numpy as np
import concourse.bass as bass
import concourse.mybir as mybir
from concourse.bass_utils import run_bass_kernel_spmd

F32 = mybir.dt.float32
BF16 = mybir.dt.bfloat16
I32 = mybir.dt.int32
ALU = mybir.AluOpType
AF = mybir.ActivationFunctionType
AX = mybir.AxisListType

_ISZ = {F32: 4, BF16: 2, I32: 4}
SB_BASE = 16512
SB_SIZE = 212800


class Op:
    __slots__ = ("eng", "fn", "deps", "idx", "sig", "seq", "is_dma", "dsem", "dval")


class View:
    __slots__ = ("ap", "blocks")

    def __init__(self, ap, blocks):
        self.ap = ap
        self.blocks = blocks

    def re(self, pat, **kw):
        return View(self.ap.rearrange(pat, **kw), self.blocks)


class Buf:
    def __init__(self, sch, name, free_shape, dtype, offset=None, parts=128, space="sb"):
        self.sch = sch
        self.name = name
        self.shape = tuple(free_shape)
        self.dtype = dtype
        self.isz = _ISZ[dtype]
        self.space = space
        self.parts = parts
        nc = sch.nc
        if space == "sb":
            self.off = offset
            self.h = nc.alloc_sbuf_tensor_at(name, [parts] + list(free_shape), dtype, offset=SB_BASE + offset)
            self.gran = 64
        else:
            self.off = 0
            self.h = nc.alloc_psum_tensor(name, [parts] + list(free_shape), dtype)
            self.gran = 2048
        st = []
        s = 1
        for d in reversed(self.shape):
            st.append(s)
            s *= d
        self.strides = tuple(reversed(st))
        self.nbytes = s * self.isz

    def _runs(self, key):
        shape, strides = self.shape, self.strides
        n = len(shape)
        d0 = n - 1
        while d0 > 0 and key[d0] == (0, shape[d0]):
            d0 -= 1
        runs = []

        def rec(d, base):
            a, b = key[d]
            if d == d0:
                runs.append((base + a * strides[d], base + b * strides[d]))
            else:
                for i in range(a, b):
                    rec(d + 1, base + i * strides[d])

        rec(0, 0)
        return runs

    def __getitem__(self, key):
        if not isinstance(key, tuple):
            key = (key,)
        pk = key[0]
        fk = list(key[1:])
        while len(fk) < len(self.shape):
            fk.append(slice(None))
        norm = []
        for d, k in enumerate(fk):
            if isinstance(k, int):
                norm.append((k, k + 1))
            else:
                a = 0 if k.start is None else k.start
                b = self.shape[d] if k.stop is None else k.stop
                norm.append((a, b))
        blocks = set()
        g = self.gran
        for (a, b) in self._runs(norm):
            a = self.off + a * self.isz
            b = self.off + b * self.isz
            for blk in range(a // g, (b - 1) // g + 1):
                blocks.add((self.space, blk))
        ap = self.h[(pk,) + tuple(fk)]
        return View(ap, blocks)


class Sched:
    ENGS = ("pe", "act", "dve", "pool", "sp")

    def __init__(self, nc, n_dsem=6):
        self.nc = nc
        self.q = {e: [] for e in self.ENGS}
        self.mem = {}
        self.n_dsem = n_dsem
        self.dma_count = {"sp": 0, "pool": 0, "act": 0}
        self.dsem_cnt = {}
        self.arena = nc.alloc_sbuf_tensor("arena", [128, SB_SIZE // 4], F32)
        assert nc.lookup_mloc(self.arena).addr == SB_BASE

    def add(self, eng, fn, reads=(), writes=(), dma=False):
        op = Op()
        op.eng = eng
        op.fn = fn
        op.is_dma = dma
        op.sig = dma
        op.seq = 0
        op.idx = len(self.q[eng])
        op.dsem = None
        op.dval = 0
        deps = {}
        mem = self.mem

        def dep(o, raw):
            if o is None or o is op:
                return
            k = id(o)
            if k in deps:
                deps[k] = (o, deps[k][1] or raw)
            else:
                deps[k] = (o, raw)

        rblocks = set()
        for v in reads:
            rblocks |= v.blocks
        wblocks = set()
        for v in writes:
            wblocks |= v.blocks
        for b in rblocks:
            st = mem.get(b)
            if st is not None:
                dep(st[0], True)
        for b in wblocks:
            st = mem.get(b)
            if st is not None:
                dep(st[0], False)
                for r in st[1]:
                    dep(r, False)
        for b in rblocks:
            st = mem.get(b)
            if st is None:
                mem[b] = [None, [op]]
            else:
                st[1].append(op)
        for b in wblocks:
            mem[b] = [op, []]
        best = {}
        final = []
        for (o, raw) in deps.values():
            if o.is_dma:
                final.append(o)
                continue
            if o.eng == eng and not dma:
                if eng == "pe":
                    continue
            cur = best.get(o.eng)
            if cur is None or o.idx > cur.idx:
                best[o.eng] = o
        final.extend(best.values())
        op.deps = final
        for o in final:
            o.sig = True
        if dma:
            k = self.dma_count[eng] % self.n_dsem
            self.dma_count[eng] += 1
            op.dsem = (eng, k)
            c = self.dsem_cnt.get(op.dsem, 0) + 16
            self.dsem_cnt[op.dsem] = c
            op.dval = c
        self.q[eng].append(op)
        return op

    def pe(self, fn, reads=(), writes=()):
        return self.add("pe", fn, reads, writes)

    def act(self, fn, reads=(), writes=()):
        return self.add("act", fn, reads, writes)

    def dve(self, fn, reads=(), writes=()):
        return self.add("dve", fn, reads, writes)

    def pool(self, fn, reads=(), writes=()):
        return self.add("pool", fn, reads, writes)

    def dma(self, eng, out_ap, in_ap, reads=(), writes=()):
        return self.add(eng, lambda e: e.dma_start(out=out_ap, in_=in_ap), reads, writes, dma=True)

    def emit(self, final_waits=()):
        nc = self.nc
        for o in final_waits:
            o.sig = True
        for e in self.ENGS:
            s = 0
            for op in self.q[e]:
                if op.sig and not op.is_dma:
                    s += 1
                    op.seq = s
        from contextlib import ExitStack
        with ExitStack() as es:
            esem = {e: es.enter_context(nc.semaphore("s_" + e)) for e in self.ENGS}
            dsem = {}
            for (q, k) in self.dsem_cnt:
                dsem[(q, k)] = es.enter_context(nc.semaphore("d_%s_%d" % (q, k)))
            block = es.enter_context(nc.Block())

            def run(ename, e):
                waited = {}
                for op in self.q[ename]:
                    for o in op.deps:
                        if o.is_dma:
                            sem, val, key = dsem[o.dsem], o.dval, o.dsem
                        else:
                            sem, val, key = esem[o.eng], o.seq, o.eng
                        if waited.get(key, 0) < val:
                            e.wait_ge(sem, val)
                            waited[key] = val
                    ins = op.fn(e)
                    if op.is_dma:
                        ins.then_inc(dsem[op.dsem], 16)
                    elif op.sig:
                        ins.then_inc(esem[ename], 1)
                if ename == "sp":
                    fin = {}
                    for o in final_waits:
                        if o.is_dma:
                            fin[o.dsem] = max(fin.get(o.dsem, 0), o.dval)
                        else:
                            e.wait_ge(esem[o.eng], o.seq)
                    for key in sorted(fin):
                        e.wait_ge(dsem[key], max(fin[key], self.dsem_cnt[key]))

            @block.tensor
            def _(e):
                run("pe", e)

            @block.scalar
            def _(e):
                run("act", e)

            @block.vector
            def _(e):
                run("dve", e)

            @block.gpsimd
            def _(e):
                run("pool", e)

            @block.sync
            def _(e):
                run("sp", e)


D = 1024
T = 2048
NTT = 4
DEPTH = 4
ALPHA = (2 * DEPTH) ** 0.25
LN_EPS = 1e-5
RMS_EPS = 1e-6
ATT_SCALE = 96 ** -0.5
NV = 200
V_BADA, V_CAW, V_CAB, V_LAG, V_LAB, V_CBW, V_QG, V_KVG, V_LG, V_LB = 0, 24, 148, 152, 156, 160, 172, 175, 177, 185
NGRP = 16
TWO_PI = 6.283185307179586


def _chunk_cols():
    ar = np.arange(128)
    ch = []
    for c in range(3):
        ch.append(3584 + 128 * c + ar)
    for c in range(2):
        ch.append(3968 + 128 * c + ar)
    kr = np.full(128, -1)
    kr[64:96] = 4224 + np.arange(32)
    ch.append(kr)
    krs = np.full(128, -1)
    krs[64:80] = 4240 + np.arange(16)
    krs[80:96] = 4224 + np.arange(16)
    ch.append(krs)
    for c in range(4):
        ch.append(4256 + 128 * c + ar)
    ch.append(np.full(128, -1))
    for base in (512, 0, 1024):
        for c in range(4):
            ch.append(base + 128 * c + ar)
    for base in (2560, 1536, 2048, 3072):
        for c in range(4):
            ch.append(base + 128 * c + ar)
    for j in range(8):
        for base in (4768, 5792, 6816):
            ch.append(base + 128 * j + ar)
    assert len(ch) == 64
    return np.concatenate(ch)


def _prep_weights(inp, layers):
    cols = _chunk_cols()
    L = len(layers)
    f = np.float32
    wada = np.empty((L, 6, 128, 8, 512), f)
    win = np.empty((L, NGRP, 128, 8, 512), f)
    wuq = np.empty((L, 128, 3, 768), f)
    wuqs = np.empty((L, 128, 3, 256), f)
    wukv = np.empty((L, 128, 2, 1024), f)
    wy = np.empty((L, 8, 128, 3, 4, 128), f)
    wo = np.empty((L, 2, 128, 8, 512), f)
    vecs = np.zeros((128, L, NV), f)
    for i, l in enumerate(layers):
        wada[i] = np.asarray(inp["w_ada"][l]).reshape(8, 128, 6, 512).transpose(2, 1, 0, 3)
        w = np.asarray(inp["w_in"][l])
        wp = np.zeros((1024, cols.size), f)
        m = cols >= 0
        wp[:, m] = w[:, cols[m]]
        win[i] = wp.reshape(8, 128, NGRP, 512).transpose(2, 1, 0, 3)
        uq = np.asarray(inp["w_uq"][l])
        wuq[i] = uq.reshape(3, 128, 768).transpose(1, 0, 2)
        sc = np.concatenate([np.concatenate([h * 96 + 80 + np.arange(16), h * 96 + 64 + np.arange(16)]) for h in range(8)])
        wuqs[i] = uq[:, sc].reshape(3, 128, 256).transpose(1, 0, 2)
        ukv = np.asarray(inp["w_ukv"][l])
        kc_ = np.concatenate([h * 128 + np.arange(64) for h in range(8)])
        vc_ = np.concatenate([h * 128 + 64 + np.arange(64) for h in range(8)])
        wukv[i] = np.concatenate([ukv[:, kc_], ukv[:, vc_]], axis=1).reshape(2, 128, 1024).transpose(1, 0, 2)
        for b, nm in enumerate(("w_a_out", "w_b_out", "w_c_out")):
            wx = np.asarray(inp[nm][l])
            wy[i, :, :, b] = wx.reshape(4, 128, 8, 128).transpose(2, 1, 0, 3)
        wo[i] = np.asarray(inp["w_o"][l]).reshape(8, 128, 2, 512).transpose(2, 1, 0, 3)
        v = vecs[:, i]
        v[:, V_BADA:V_BADA + 24] = np.asarray(inp["b_ada"][l]).reshape(24, 128).T
        v[:, V_CAW:V_CAW + 124] = np.asarray(inp["conv_a_w"][l]).reshape(31, 4, 128).transpose(2, 1, 0).reshape(128, 124)
        v[:, V_CAB:V_CAB + 4] = np.asarray(inp["conv_a_b"][l]).reshape(4, 128).T
        v[:, V_LAG:V_LAG + 4] = np.asarray(inp["ln_a_g"][l]).reshape(4, 128).T
        v[:, V_LAB:V_LAB + 4] = np.asarray(inp["ln_a_b"][l]).reshape(4, 128).T
        v[:, V_CBW:V_CBW + 12] = np.asarray(inp["conv_b_w"][l]).reshape(3, 4, 128).transpose(2, 1, 0).reshape(128, 12)
        v[:, V_QG:V_QG + 3] = np.asarray(inp["q_norm_g"][l]).reshape(3, 128).T
        v[:, V_KVG:V_KVG + 2] = np.asarray(inp["kv_norm_g"][l]).reshape(2, 128).T
        v[:, V_LG:V_LG + 8] = np.asarray(inp["ln_g"][l]).reshape(8, 128).T
        v[:, V_LB:V_LB + 8] = np.asarray(inp["ln_b"][l]).reshape(8, 128).T
    cst = np.zeros((128, 256), f)
    invc = np.zeros((128, 1), f)
    cst[:, 0:128] = np.eye(128, dtype=f)
    kk = np.arange(128)[:, None]
    qq = np.arange(128)[None, :]
    cst[:, 128:256] = np.where(kk > qq, -30000.0, 0.0)
    invf = (10000.0 ** (-np.arange(0, 32, 2, dtype=np.float32) / 32)).astype(f)
    for p in range(64, 96):
        invc[p, 0] = invf[(p - 64) % 16]
    return dict(invf=invc, wada=wada, win=win, wuq=wuq, wuqs=wuqs, wukv=wukv, wy=wy, wo=wo, vecs=vecs, cst=cst)


def build(L):
    nc = bass.Bass("TRN2", target_bir_lowering=False)

    def din(name, shape, dt=F32):
        return nc.dram_tensor(name, list(shape), dt, kind="ExternalInput").ap()

    x_d = din("x", [128, 8, T])
    c_d = din("cT", [128, 8])
    pos_d = din("pos", [128, T], I32)
    cst_d = din("cst", [128, 256])
    invf_d = din("invf", [128, 1])
    wada_d = din("wada", [L, 6, 128, 8, 512])
    win_d = din("win", [L, NGRP, 128, 8, 512])
    wuq_d = din("wuq", [L, 128, 3, 768])
    wuqs_d = din("wuqs", [L, 128, 3, 256])
    wukv_d = din("wukv", [L, 128, 2, 1024])
    wy_d = din("wy", [L, 8, 128, 3, 4, 128])
    wo_d = din("wo", [L, 2, 128, 8, 512])
    vecs_d = din("vecs", [128, L, NV])
    y_d = nc.dram_tensor("y", [128, 8, T], F32, kind="ExternalOutput").ap()
    rope_d = nc.dram_tensor("rope_scr", [2, 32, T], F32).ap()

    S = Sched(nc)

    def sb(name, shape, dt, off):
        b = Buf(S, name, shape, dt, offset=off)
        assert off + b.nbytes <= SB_SIZE, name
        return b

    xT = sb("xT", [8, T], F32, 0)
    O_SM = 65536
    ident = sb("ident", [128], BF16, O_SM)
    maskn = sb("maskn", [128], BF16, O_SM + 256)
    ones_bf = sb("ones_bf", [128], BF16, O_SM + 512)
    onesf = sb("onesf", [64], F32, O_SM + 768)
    invf = sb("invf", [1], F32, O_SM + 1024)
    cact = sb("cact", [8], F32, O_SM + 1088)
    cact_bf = sb("cact_bf", [8], BF16, O_SM + 1152)
    ada = sb("ada", [24], F32, O_SM + 1216)
    gq = sb("gq", [8], F32, O_SM + 1344)
    gatea = sb("gatea", [8], F32, O_SM + 1408)
    one1 = sb("one1", [1], F32, O_SM + 1472)
    epsv = sb("epsv", [4], F32, O_SM + 1504)
    vecs = sb("vecs", [L, NV], F32, O_SM + 1536)
    expo = sb("expo", [512], F32, O_SM + 1536 + 3200)
    adarow = sb("adarow", [512], F32, O_SM + 6784)
    ada_t = [ada, sb("ada1", [24], F32, O_SM + 8832)]
    gq_t = [gq, sb("gq1", [8], F32, O_SM + 8928)]
    gatea_t = [gatea, sb("gatea1", [8], F32, O_SM + 8960)]
    Gp = sb("Gp", [8], F32, O_SM + 8992)
    Bp = sb("Bp", [8], F32, O_SM + 9024)
    O_FIN = O_SM + 9216
    c_fin = sb("c_fin", [4, T], BF16, O_FIN)
    a_fin = sb("a_fin", [4, T], BF16, O_FIN + 16384)
    b_fin = sb("b_fin", [4, T], BF16, O_FIN + 32768)
    O_U = O_FIN + 49152
    uT = sb("uT", [8, T], BF16, O_U)
    O_S = O_U + 32768
    S_SIZE = SB_SIZE - O_S
    NSLOT = 2
    wbuf = [sb("wbuf%d" % i, [8, 512], BF16, O_S + 8192 * i) for i in range(NSLOT)]
    SS = O_S + 8192 * NSLOT
    ps = Buf(S, "ps", [8, 512], F32, space="ps")

    gcs = sb("gcs", [4, T], BF16, O_FIN + 16384)
    qn = sb("qn", [3, T], BF16, O_FIN + 32768)
    kpe = sb("kpe", [T], BF16, O_FIN + 32768 + 12288)
    kvn = sb("kvn", [2, T], BF16, SS)
    sqb = sb("sqb", [5, 512], BF16, SS + 8192)
    rq = sb("rq", [512], F32, SS + 13312)
    rkv = sb("rkv", [512], F32, SS + 15360)
    cosT = sb("cosT", [T], F32, SS + 17408)
    sinT = sb("sinT", [T], F32, SS + 25600)
    rt1 = sb("rt1", [512], F32, SS + 33792)
    rt2 = sb("rt2", [512], F32, SS + 35840)
    wuq_sb = sb("wuq_sb", [3, 768], BF16, SS + 8192)
    wukv_sb = sb("wukv_sb", [2, 1024], BF16, SS + 12800)
    wuqs_sb = sb("wuqs_sb", [3, 256], BF16, SS + 37888)
    tang = sb("tang", [T], F32, O_FIN)
    ttmp = sb("ttmp", [T], F32, O_FIN + 8192)
    tki = sb("tki", [T], I32, O_FIN + 8192)
    qTb = [sb("qT%d" % i, [T], BF16, O_U + 4096 * i) for i in range(2)]
    kTb = [sb("kT%d" % i, [T], BF16, O_U + 8192 + 4096 * i) for i in range(2)]
    Vp = sb("Vp", [16, 192], BF16, O_U + 16384)
    rs = sb("rs", [512], F32, O_U + 22528)
    Msb = sb("Msb", [512], F32, O_U + 24576)
    pbuf = [sb("pbuf%d" % i, [512], BF16, O_U + 26624 + 1024 * i) for i in range(4)]
    a_pad = sb("a_pad", [4, 2080], BF16, SS)
    ac = sb("ac", [4, 1024], F32, SS + 16896)
    acbs = [sb("acb%d" % i, [1, 1024], BF16, O_FIN + 32768 + 8192 + 2048 * i) for i in range(2)]
    sqas = [sb("sqa%d" % i, [1, 1024], BF16, O_FIN + 32768 + 12288 + 2048 * i) for i in range(2)]
    mean_a = sb("mean_a", [1024], F32, O_FIN + 32768)
    rstd_a = sb("rstd_a", [1024], F32, O_FIN + 32768 + 4096)
    diag = [sb("diag%d" % i, [128], BF16, SS + 33280 + 256 * i) for i in range(16)]
    p_pad = sb("p_pad", [4, 2112], BF16, SS)
    gtmp = sb("gtmp", [T], BF16, O_FIN + 32768 + 8192)
    stmp = sb("stmp", [T], BF16, SS + 20992)
    diagb = [sb("diagb%d" % i, [128], BF16, SS + 25088 + 256 * i) for i in range(8)]
    mT = sb("mT", [8, 1024], BF16, SS)
    wyb = [sb("wy%d" % i, [3, 4, 128], BF16, SS + 16384 + 3072 * i) for i in range(2)]
    wo_h = [sb("woA", [8, 512], BF16, SS + 16384), sb("woB", [8, 512], BF16, SS + 26624)]
    sg = [sb("sg%d" % i, [512], F32, SS + 26624 + 2048 * i) for i in range(3)]
    zb = [sb("zb%d" % i, [512], BF16, SS + 34816 + 1024 * i) for i in range(2)]
    zq = [sb("zq%d" % i, [512], BF16, SS + 36864 + 1024 * i) for i in range(2)]
    msq = sb("msq", [512], F32, SS + 24576)
    assert SS + 39424 <= SB_SIZE

    bank_rr = [0]

    def nb(pool_=(0, 1, 2, 3, 4, 5, 6, 7)):
        b = pool_[bank_rr[0] % len(pool_)]
        bank_rr[0] += 1
        return b

    def mm(out, lhsT, rhs, start, stop):
        S.pe(lambda e: e.matmul(out.ap, lhsT=lhsT.ap, rhs=rhs.ap, start=start, stop=stop),
             reads=[lhsT, rhs], writes=[out])

    def actf(out, in_, func, bias=None, scale=None, extra_reads=()):
        kw = {}
        if bias is not None:
            kw["bias"] = bias.ap if isinstance(bias, View) else bias
        if scale is not None:
            kw["scale"] = scale.ap if isinstance(scale, View) else scale
        rd = [in_] + [v for v in (bias, scale) if isinstance(v, View)] + list(extra_reads)
        S.act(lambda e: e.activation(out=out.ap, in_=in_.ap, func=func, **kw), reads=rd, writes=[out])

    def tt_(eng, out, a, b, op):
        S.add(eng, lambda e: e.tensor_tensor(out=out.ap, in0=a.ap, in1=b.ap, op=op), reads=[a, b], writes=[out])

    def ts_(eng, out, a, s1, s2, op0, op1=None):
        rd = [a] + [v for v in (s1, s2) if isinstance(v, View)]
        s1a = s1.ap if isinstance(s1, View) else s1
        s2a = s2.ap if isinstance(s2, View) else s2
        if op1 is None:
            S.add(eng, lambda e: e.tensor_scalar(out=out.ap, in0=a.ap, scalar1=s1a, scalar2=None, op0=op0), reads=rd, writes=[out])
        else:
            S.add(eng, lambda e: e.tensor_scalar(out=out.ap, in0=a.ap, scalar1=s1a, scalar2=s2a, op0=op0, op1=op1), reads=rd, writes=[out])

    def stt(out, a, s, b, op0, op1):
        rd = [a, b] + ([s] if isinstance(s, View) else [])
        sa = s.ap if isinstance(s, View) else s
        S.dve(lambda e: e.scalar_tensor_tensor(out=out.ap, in0=a.ap, scalar=sa, in1=b.ap, op0=op0, op1=op1), reads=rd, writes=[out])

    def cp(eng, out, in_):
        S.add(eng, lambda e: e.tensor_copy(out=out.ap, in_=in_.ap), reads=[in_], writes=[out])

    def rsqrt_eps(out, in_, eps_col, pw=-0.5):
        if eps_col is None:
            actf(out, in_, AF.Ln)
        else:
            actf(out, in_, AF.Ln, bias=epsv[in_part(out), eps_col:eps_col + 1])
        actf(out, out, AF.Exp, scale=pw)

    def in_part(v):
        return slice(None)

    def ps4(bs):
        return ps[:, bs:bs + 4, :].re("p a b -> p (a b)")

    def ps2(bs):
        return ps[:, bs:bs + 2, :].re("p a b -> p (a b)")

    bset_c = [0]

    def bset():
        bset_c[0] += 1
        return (bset_c[0] % 2) * 4

    stream = []
    for g in range(6):
        stream.append(wada_d[0, g])
    for l in range(L):
        for g in range(16):
            stream.append(win_d[l, g])
        if l + 1 < L:
            for g in range(6):
                stream.append(wada_d[l + 1, g])
        for g in range(10, 16):
            stream.append(win_d[l, g])
    issued = [0]

    def grp(si):
        while issued[0] < min(len(stream), si + 1):
            k = issued[0]
            b = wbuf[k % NSLOT]
            S.dma("pool", b[:].ap, stream[k], writes=[b[:]])
            issued[0] += 1
        return wbuf[si % NSLOT]

    spos = [0]

    S.dma("sp", xT[:].ap, x_d, writes=[xT[:]])
    S.dma("sp", cact[:].ap, c_d, writes=[cact[:]])
    S.dma("sp", vecs[:].ap, vecs_d, writes=[vecs[:]])
    S.dma("sp", invf[:].ap, invf_d, writes=[invf[:]])
    S.dma("pool", ident[:].ap, cst_d[:, 0:128], writes=[ident[:]])
    S.dma("pool", maskn[:].ap, cst_d[:, 128:256], writes=[maskn[:]])
    grp(1)
    S.pool(lambda e: e.memset(ones_bf[:].ap, 1.0), writes=[ones_bf[:]])
    S.pool(lambda e: e.memset(onesf[:].ap, 1.0), writes=[onesf[:]])
    S.pool(lambda e: e.memset(one1[:].ap, 1.0), writes=[one1[:]])
    S.pool(lambda e: e.memset(epsv[:, 0:1].ap, 384 * RMS_EPS), writes=[epsv[:]])
    S.pool(lambda e: e.memset(epsv[:, 1:2].ap, 256 * RMS_EPS), writes=[epsv[:]])
    S.pool(lambda e: e.memset(epsv[:, 2:3].ap, LN_EPS), writes=[epsv[:]])
    S.pool(lambda e: e.memset(epsv[:, 3:4].ap, LN_EPS / (ALPHA * ALPHA)), writes=[epsv[:]])
    actf(cact[:], cact[:], AF.Silu)
    cp("dve", cact_bf[:], cact[:])

    def vcol(l, idx, n=1):
        return vecs[:, l, idx:idx + n]

    def ada_steps(l):
        ada, gq, gatea = ada_t[l % 2], gq_t[l % 2], gatea_t[l % 2]
        steps = []
        for g in range(6):
            def st(g=g):
                wg = grp(spos[0])
                grp(spos[0] + 1)
                for kc in range(8):
                    mm(ps[0:1, 6, :], cact_bf[:, kc:kc + 1], wg[:, kc, :], kc == 0, kc == 7)
                spos[0] += 1
                cp("dve", adarow[0:1, :], ps[0:1, 6, :])
                for jj in range(4):
                    mm(ps[:, 7, g * 4 + jj: g * 4 + jj + 1], adarow[0:1, jj * 128:(jj + 1) * 128], one1[0:1, :], True, True)
            steps.append(st)

        def fin():
            tt_("dve", ada[:, 0:24], ps[:, 7, 0:24], vcol(l, V_BADA, 24), ALU.add)
            ts_("dve", ada[:, 8:16], ada[:, 8:16], 1.0, None, ALU.add)
            ts_("dve", gatea[:], ada[:, 16:24], 1.0 / ALPHA, None, ALU.mult)
            ts_("dve", gq[:, 0:3], vcol(l, V_QG, 3), float(384 ** 0.5), None, ALU.mult)
            ts_("dve", gq[:, 3:5], vcol(l, V_KVG, 2), float(256 ** 0.5), None, ALU.mult)
            if l >= 1:
                tt_("dve", Gp[:], vcol(l - 1, V_LG, 8), ada[:, 8:16], ALU.mult)
                tt_("dve", Bp[:], vcol(l - 1, V_LB, 8), ada[:, 8:16], ALU.mult)
                tt_("dve", Bp[:], Bp[:], ada[:, 0:8], ALU.add)
        steps.append(fin)
        return steps

    def ada_compute(l):
        for st in ada_steps(l):
            st()

    lnq = []

    def drain(n=10 ** 9):
        while lnq and n > 0:
            lnq.pop(0)()
            n -= 1

    def ln_pieces(l, tts, fuse_u):
        b1, b2 = 6, 7
        for tt in tts:
            ts = slice(tt * 512, (tt + 1) * 512)

            def stat_mm(c):
                mm(ps[:, b1, :], ones_bf[:], zb[c % 2][:], c == 0, c == 7)
                mm(ps[:, b2, :], ones_bf[:], zq[c % 2][:], c == 0, c == 7)

            for c in range(8):
                def p1(c=c, ts=ts, stat_mm=stat_mm):
                    cp("dve", zb[c % 2][:], xT[:, c, ts])
                    actf(zq[c % 2][:], xT[:, c, ts], AF.Square)
                    if c > 0:
                        stat_mm(c - 1)
                lnq.append(p1)

            def chain(stat_mm=stat_mm):
                stat_mm(7)
                actf(msq[:], ps[:, b1, :], AF.Square, scale=1.0 / D)
                ts_("dve", ps[:, b1, :], ps[:, b1, :], 1.0 / D, None, ALU.mult)
                stt(ps[:, b2, :], ps[:, b2, :], 1.0 / D, msq[:], ALU.mult, ALU.subtract)
                rsqrt_eps(ps[:, b2, :], ps[:, b2, :], 3)
            lnq.append(chain)
            for c in range(8):
                def p2(c=c, ts=ts):
                    tt_("dve", xT[:, c, ts], xT[:, c, ts], ps[:, b1, :], ALU.subtract)
                    tt_("dve", xT[:, c, ts], xT[:, c, ts], ps[:, b2, :], ALU.mult)
                    if fuse_u:
                        actf(uT[:, c, ts], xT[:, c, ts], AF.Identity, bias=Bp[:, c:c + 1], scale=Gp[:, c:c + 1])
                    actf(xT[:, c, ts], xT[:, c, ts], AF.Identity, bias=vcol(l, V_LB + c), scale=vcol(l, V_LG + c))
                lnq.append(p2)

    def layer(l):
        ada, gq, gatea = ada_t[l % 2], gq_t[l % 2], gatea_t[l % 2]

        def make_u():
            for tt in range(NTT):
                ts = slice(tt * 512, (tt + 1) * 512)
                for c in range(8):
                    if c % 2 == 0:
                        actf(uT[:, c, ts], xT[:, c, ts], AF.Identity, bias=ada[:, c:c + 1], scale=ada[:, 8 + c:9 + c])
                    else:
                        ts_("dve", uT[:, c, ts], xT[:, c, ts], ada[:, 8 + c:9 + c], ada[:, c:c + 1], ALU.mult, ALU.add)

        if l == 0:
            make_u()
        R = slice(64, 96)
        g0 = spos[0]

        def wchunk(ci):
            gi = g0 + ci // 4
            wgb = grp(gi)
            return wgb, (ci % 4) * 128

        ln_busy = bool(lnq)
        for tt in range(NTT):
            ts = slice(tt * 512, (tt + 1) * 512)
            banks = []
            for ci in range(5):
                wgb, co = wchunk(ci)
                b = nb((0, 1, 2, 3, 4)) if ln_busy else nb((0, 1, 2, 3, 4, 7))
                banks.append(b)
                for kc in range(8):
                    mm(ps[:, b, :], wgb[:, kc, co:co + 128], uT[:, kc, ts], kc == 0, kc == 7)
                actf(sqb[:, ci, :], ps[:, b, :], AF.Square)
                drain(4)
            if tt == 1:
                drain()
            for (lo, hi, sbank, rbuf, n) in ((0, 3, 5, rq, 384), (3, 5, 5 if ln_busy else 6, rkv, 256)):
                for ci in range(lo, hi):
                    mm(ps[:, sbank, :], ones_bf[:], sqb[:, ci, :], ci == lo, ci == hi - 1)
                rsqrt_eps(rbuf[:], ps[:, sbank, :], 0 if n == 384 else 1)
                for ci in range(lo, hi):
                    dst = qn[:, ci, ts] if ci < 3 else kvn[:, ci - 3, ts]
                    stt(dst, ps[:, banks[ci], :], gq[:, ci:ci + 1], rbuf[:], ALU.mult, ALU.mult)
            if tt == 1:
                ln_busy = False
        S.dma("pool", wuq_sb[:].ap, wuq_d[l], writes=[wuq_sb[:]])
        S.dma("pool", wuqs_sb[:].ap, wuqs_d[l], writes=[wuqs_sb[:]])
        S.dma("pool", wukv_sb[:].ap, wukv_d[l], writes=[wukv_sb[:]])
        dr_cos, dr_sin = View(rope_d[0], {("dr", 0)}), View(rope_d[1], {("dr", 1)})
        if l > 0:
            S.dma("sp", cosT[R, :].ap, dr_cos.ap, reads=[dr_cos], writes=[cosT[R, :]])
            S.dma("sp", sinT[R, :].ap, dr_sin.ap, reads=[dr_sin], writes=[sinT[R, :]])
        else:
            S.dma("sp", tki[R, :].ap, pos_d[64:96, :], writes=[tki[R, :]])
            cp("dve", tang[R, :], tki[R, :])
            ts_("dve", tang[R, :], tang[R, :], invf[R, :], None, ALU.mult)
            ts_("dve", ttmp[R, :], tang[R, :], 1.0 / TWO_PI, None, ALU.mult)
            cp("dve", tki[R, :], ttmp[R, :])
            cp("dve", ttmp[R, :], tki[R, :])
            C1 = 6.28125
            C2 = TWO_PI - C1
            stt(tang[R, :], ttmp[R, :], -C1, tang[R, :], ALU.mult, ALU.add)
            stt(tang[R, :], ttmp[R, :], -C2, tang[R, :], ALU.mult, ALU.add)
            ts_("dve", ttmp[R, :], tang[R, :], float(np.pi), None, ALU.is_gt)
            stt(tang[R, :], ttmp[R, :], -TWO_PI, tang[R, :], ALU.mult, ALU.add)
            ts_("dve", ttmp[R, :], tang[R, :], float(-np.pi), None, ALU.is_lt)
            stt(tang[R, :], ttmp[R, :], TWO_PI, tang[R, :], ALU.mult, ALU.add)
            ts_("dve", tang[R, :], tang[R, :], 3.1415925, -3.1415925, ALU.min, ALU.max)
            actf(sinT[R, :], tang[R, :], AF.Sin)
            ts_("dve", ttmp[R, :], tang[R, :], float(np.pi / 2), None, ALU.is_gt)
            stt(tang[R, :], ttmp[R, :], -TWO_PI, tang[R, :], ALU.mult, ALU.add)
            ts_("dve", tang[R, :], tang[R, :], float(np.pi / 2), None, ALU.add)
            ts_("dve", tang[R, :], tang[R, :], 3.1415925, -3.1415925, ALU.min, ALU.max)
            actf(cosT[R, :], tang[R, :], AF.Sin)
            ts_("dve", sinT[64:80, :], sinT[64:80, :], -1.0, None, ALU.mult)

            S.dma("sp", dr_cos.ap, cosT[R, :].ap, reads=[cosT[R, :]], writes=[dr_cos])
            S.dma("sp", dr_sin.ap, sinT[R, :].ap, reads=[sinT[R, :]], writes=[dr_sin])
        grp(g0 + 2)
        for c in range(4):
            wgb, co = wchunk(7 + c)
            bs = (c % 2) * 4
            for tt in range(NTT):
                ts = slice(tt * 512, (tt + 1) * 512)
                for kc in range(8):
                    mm(ps[:, bs + tt, :], wgb[:, kc, co:co + 128], uT[:, kc, ts], kc == 0, kc == 7)
            actf(gcs[:, c, :], ps4(bs), AF.Silu)
        for tt in range(NTT):
            ts = slice(tt * 512, (tt + 1) * 512)
            bk = []
            for ci in (5, 6):
                wgb, co = wchunk(ci)
                b = nb((0, 1, 2, 3, 4))
                bk.append(b)
                for kc in range(8):
                    mm(ps[0:96, b, :], wgb[:, kc, co:co + 96], uT[:, kc, ts], kc == 0, kc == 7)
            tt_("dve", rt1[R, :], ps[R, bk[0], :], cosT[R, ts], ALU.mult)
            tt_("dve", rt2[R, :], ps[R, bk[1], :], sinT[R, ts], ALU.mult)
            tt_("dve", kpe[R, ts], rt1[R, :], rt2[R, :], ALU.add)
        spos[0] = g0 + 3
        grp(g0 + 3)
        grp(g0 + 4)

        for qb_ in qTb + kTb:
            S.pool(lambda e, qb_=qb_: e.memset(qb_[96:128, :].ap, 0.0), writes=[qb_[96:128, :]])
        S.pool(lambda e: e.memset(Vp[:, :, 64:128].ap, 0.0), writes=[Vp[:]])
        S.pool(lambda e: e.memset(Vp[:, :, 64:65].ap, 1.0), writes=[Vp[:]])
        AP_ = (2, 3, 4, 5, 6, 7)

        def prod_qk(h):
            qT, kT = qTb[h % 2], kTb[h % 2]
            for tt in range(NTT):
                ts = slice(tt * 512, (tt + 1) * 512)
                b = nb(AP_)
                for kc in range(2):
                    mm(ps[0:64, b, :], wukv_sb[:, kc, h * 64:(h + 1) * 64], kvn[:, kc, ts], kc == 0, kc == 1)
                cp("dve", kT[0:64, ts], ps[0:64, b, :])
                bq = nb(AP_)
                for kc in range(3):
                    mm(ps[0:96, bq, :], wuq_sb[:, kc, h * 96:(h + 1) * 96], qn[:, kc, ts], kc == 0, kc == 2)
                bs_ = nb(AP_)
                for kc in range(3):
                    mm(ps[64:96, bs_, :], wuqs_sb[:, kc, h * 32:(h + 1) * 32], qn[:, kc, ts], kc == 0, kc == 2)
                cp("dve", qT[0:64, ts], ps[0:64, bq, :])
                tt_("dve", rt1[R, :], ps[R, bq, :], cosT[R, ts], ALU.mult)
                tt_("dve", rt2[R, :], ps[R, bs_, :], sinT[R, ts], ALU.mult)
                tt_("dve", qT[R, ts], rt1[R, :], rt2[R, :], ALU.add)
            cp("dve", kT[R, :], kpe[R, :])

        def prod_v(j):
            for g4 in range(4):
                b = nb(AP_)
                for k4 in range(4):
                    kt = g4 * 4 + k4
                    for kc in range(2):
                        mm(ps[:, b, k4 * 128:(k4 + 1) * 128], kvn[:, kc, kt * 128:(kt + 1) * 128],
                           wukv_sb[:, kc, 512 + j * 128:512 + (j + 1) * 128], kc == 0, kc == 1)
                pv = ps[:, b, :].re("p (k c) -> p k c", c=128)
                S.dve(lambda e, pv=pv, g4=g4: e.tensor_copy(out=Vp[:, g4 * 4:(g4 + 1) * 4, 0:64].ap, in_=pv.ap[:, :, 0:64]),
                      reads=[pv], writes=[Vp[:, g4 * 4:(g4 + 1) * 4, :]])
                S.dve(lambda e, pv=pv, g4=g4: e.tensor_copy(out=Vp[:, g4 * 4:(g4 + 1) * 4, 128:192].ap, in_=pv.ap[:, :, 64:128]),
                      reads=[pv], writes=[Vp[:, g4 * 4:(g4 + 1) * 4, :]])

        def attn(h):
            j, odd = h // 2, h % 2
            qT, kT = qTb[h % 2], kTb[h % 2]
            for tt in range(NTT):
                ob = tt % 2
                nk = 4 * tt + 4
                steps = []
                for kt in range(nk):
                    q0 = max(tt * 512, kt * 128)
                    steps.append((kt, q0, (tt + 1) * 512 - q0, kt >= 4 * tt))
                LA = 3
                sb_ = {}

                def s_mm(i):
                    kt, q0, n, dg = steps[i]
                    b = nb(AP_[0:5])
                    sb_[i] = b
                    mm(ps[:, b, 0:n], kT[:, kt * 128:(kt + 1) * 128], qT[:, q0:q0 + n], True, not dg)
                    if dg:
                        mm(ps[:, b, 0:128], ident[:], maskn[:], False, True)

                for i in range(min(LA, nk)):
                    s_mm(i)
                while pend_norm:
                    pend_norm.pop(0)()
                for i in range(nk):
                    if i + LA < nk:
                        s_mm(i + LA)
                    kt, q0, n, dg = steps[i]
                    pb = pbuf[i % 4]
                    actf(pb[:, 0:n], ps[:, sb_[i], 0:n], AF.Exp, scale=ATT_SCALE)
                    o0 = q0 - tt * 512
                    if not odd:
                        mm(ps[:, ob, o0:512], Vp[:, kt, 0:128], pb[:, 0:n], i == 0, i == nk - 1)
                    else:
                        mm(ps[:, ob, o0:512], Vp[:, kt, 64:192], pb[:, 0:n], i == 0, i == nk - 1)
                ts = slice(tt * 512, (tt + 1) * 512)
                if not odd:
                    sr, orows = slice(64, 65), slice(0, 64)
                else:
                    sr, orows = slice(0, 1), slice(64, 128)
                actf(rs[sr, :], ps[sr, ob, :], AF.Ln)
                actf(rs[sr, :], rs[sr, :], AF.Exp, scale=-1.0)

                def norm_b(sr=sr, orows=orows, ob=ob, j=j, ts=ts):
                    mm(ps[orows, 7, :], onesf[sr, 0:64], rs[sr, :], True, True)
                    tt_("dve", Msb[orows, :], ps[orows, 7, :], gcs[orows, j, ts], ALU.mult)
                    tt_("dve", c_fin[orows, j, ts], ps[orows, ob, :], Msb[orows, :], ALU.mult)
                pend_norm.append(norm_b)

        pend_norm = []
        prod_qk(0)
        for h in range(8):
            if h % 2 == 0:
                prod_v(h // 2)
            if h + 1 < 8:
                prod_qk(h + 1)
            attn(h)
        while pend_norm:
            pend_norm.pop(0)()

        make_u()
        gA = spos[0]

        def wchunkA(ci):
            gi = gA + ci // 4
            return grp(gi), (ci % 4) * 128

        S.pool(lambda e: e.memset(a_pad[:, :, 0:32].ap, 0.0), writes=[a_pad[:, :, 0:32]])
        PADA = 32
        for c in range(4):
            wgb, co = wchunkA(c)
            bs = (c % 2) * 4
            for tt in range(NTT):
                for kc in range(8):
                    mm(ps[:, bs + tt, :], wgb[:, kc, co:co + 128], uT[:, kc, tt * 512:(tt + 1) * 512], kc == 0, kc == 7)
            actf(a_fin[:, c, :], ps4(bs), AF.Sigmoid)
        grp(gA + 2)
        for c in range(4):
            wgb, co = wchunkA(4 + c)
            bs = (c % 2) * 4
            for tt in range(NTT):
                for kc in range(8):
                    mm(ps[:, bs + tt, :], wgb[:, kc, co:co + 128], uT[:, kc, tt * 512:(tt + 1) * 512], kc == 0, kc == 7)
            tt_("dve", a_pad[:, c, PADA:PADA + T], ps4(bs), a_fin[:, c, :], ALU.mult)
        grp(gA + 3)
        for c in range(4):
            wgb, co = wchunkA(8 + c)
            bs = (c % 2) * 4
            for tt in range(NTT):
                for kc in range(8):
                    mm(ps[:, bs + tt, :], wgb[:, kc, co:co + 128], uT[:, kc, tt * 512:(tt + 1) * 512], kc == 0, kc == 7)
            actf(a_fin[:, c, :], ps4(bs), AF.Silu)
        spos[0] = gA + 3
        grp(gA + 4)
        dcnt = [0]
        pend_stats = []
        deferred = []
        ac2 = ac

        def emit_stats(c):
            for t2 in range(2):
                mm(ps[:, 4 + t2, :], ones_bf[:], acbs[c % 2][:, 0, t2 * 512:(t2 + 1) * 512], c == 0, c == 3)
                mm(ps[:, 6 + t2, :], ones_bf[:], sqas[c % 2][:, 0, t2 * 512:(t2 + 1) * 512], c == 0, c == 3)

        for hf in range(2):
            for c in range(4):
                bs = (c % 2) * 2
                acb, sqa = acbs[c % 2], sqas[c % 2]
                for k in range(31):
                    dg = diag[dcnt[0] % 16]
                    dcnt[0] += 1
                    ts_("dve", dg[:], ident[:], vcol(l, V_CAW + c * 31 + k), None, ALU.mult)
                    for t2 in range(2):
                        t0 = (hf * 2 + t2) * 512 + 2 + k
                        mm(ps[:, bs + t2, :], dg[:], a_pad[:, c, t0:t0 + 512], k == 0, k == 30)
                pc = ps2(bs)
                if deferred:
                    deferred.pop(0)()
                actf(ac[:, c, :], pc, AF.Identity, bias=vcol(l, V_CAB + c))
                actf(sqa[:, 0, :], pc, AF.Square, bias=vcol(l, V_CAB + c))
                actf(acb[:, 0, :], pc, AF.Identity, bias=vcol(l, V_CAB + c))
                pend_stats.append(c)
                if c > 0:
                    emit_stats(pend_stats.pop(0))
            emit_stats(pend_stats.pop(0))
            ts_("dve", mean_a[:], ps2(4), 1.0 / 512, None, ALU.mult)
            tt_("dve", rstd_a[:], mean_a[:], mean_a[:], ALU.mult)
            stt(rstd_a[:], ps2(6), 1.0 / 512, rstd_a[:], ALU.mult, ALU.subtract)
            rsqrt_eps(rstd_a[:], rstd_a[:], 2)
            hs_ = slice(hf * 1024, (hf + 1) * 1024)
            for c in range(4):
                def norm_piece(c=c, hs=hs_):
                    tt_("dve", ac2[:, c, :], ac2[:, c, :], mean_a[:], ALU.subtract)
                    stt(ac2[:, c, :], ac2[:, c, :], vcol(l, V_LAG + c), rstd_a[:], ALU.mult, ALU.mult)
                    actf(ac2[:, c, :], ac2[:, c, :], AF.Silu, bias=vcol(l, V_LAB + c))
                    tt_("dve", a_fin[:, c, hs], ac2[:, c, :], a_fin[:, c, hs], ALU.mult)
                deferred.append(norm_piece)

        gB = spos[0]

        def wchunkB(ci):
            gi = gB + ci // 4
            return grp(gi), (ci % 4) * 128

        S.pool(lambda e: e.memset(p_pad[:, :, 0:32].ap, 0.0), writes=[p_pad[:, :, 0:32]])

        def proj4(ci):
            wgb, co = wchunkB(ci)
            bs = bset()
            for tt in range(NTT):
                for kc in range(8):
                    mm(ps[:, bs + tt, :], wgb[:, kc, co:co + 128], uT[:, kc, tt * 512:(tt + 1) * 512], kc == 0, kc == 7)
            return ps4(bs)

        for c in range(4):
            pg = proj4(c)
            if deferred:
                deferred.pop(0)()
            actf(gtmp[:], pg, AF.Copy)
            px = proj4(4 + c)
            tt_("dve", p_pad[:, c, 32:32 + T], px, gtmp[:], ALU.mult)
        grp(gB + 2)
        grp(gB + 3)
        for c in range(4):
            pgb = proj4(8 + c)
            actf(b_fin[:, c, :], pgb, AF.Copy)
        grp(gB + 4)
        for c in range(4):
            pbg = proj4(12 + c)
            actf(stmp[:], pbg, AF.Silu)
            tt_("dve", b_fin[:, c, :], b_fin[:, c, :], stmp[:], ALU.mult)
        spos[0] = gB + 4
        grp(gB + 5)
        dcb = [0]
        for c in range(4):
            bs = (c % 2) * 4
            for k in range(3):
                dg = diagb[dcb[0] % 8]
                dcb[0] += 1
                ts_("dve", dg[:], ident[:], vcol(l, V_CBW + c * 3 + k), None, ALU.mult)
                for tt in range(NTT):
                    t0 = tt * 512 + 30 + k
                    mm(ps[:, bs + tt, :], dg[:], p_pad[:, c, t0:t0 + 512], k == 0, k == 2)
            tt_("dve", b_fin[:, c, :], ps4(bs), b_fin[:, c, :], ALU.mult)

        for hf in range(2):
            gH = spos[0]
            pool_ = (0, 1, 2, 3, 4, 5, 6, 7) if hf == 0 else (0, 1, 2, 3, 4, 5)
            for j in range(8):
                wb_ = wyb[j % 2]
                S.dma("pool", wb_[:].ap, wy_d[l, j], writes=[wb_[:]])
                for t2 in range(2):
                    tt = hf * 2 + t2
                    ts = slice(tt * 512, (tt + 1) * 512)
                    for br in range(3):
                        ci = j * 3 + br
                        wgb = grp(gH + ci // 4)
                        co = (ci % 4) * 128
                        b = nb(pool_)
                        for kc in range(8):
                            mm(ps[:, b, :], wgb[:, kc, co:co + 128], uT[:, kc, ts], kc == 0, kc == 7)
                        actf(sg[br][:], ps[:, b, :], AF.Sigmoid)
                    yb_ = []
                    for br, fin in enumerate((a_fin, b_fin, c_fin)):
                        b = nb(pool_)
                        yb_.append(b)
                        for kc in range(4):
                            mm(ps[:, b, :], wb_[:, br, kc, :], fin[:, kc, ts], kc == 0, kc == 3)
                    tt_("dve", sg[0][:], ps[:, yb_[0], :], sg[0][:], ALU.mult)
                    tt_("dve", sg[1][:], ps[:, yb_[1], :], sg[1][:], ALU.mult)
                    tt_("dve", sg[0][:], sg[0][:], sg[1][:], ALU.add)
                    tt_("dve", sg[2][:], ps[:, yb_[2], :], sg[2][:], ALU.mult)
                    tt_("dve", mT[:, j, t2 * 512:(t2 + 1) * 512], sg[0][:], sg[2][:], ALU.add)
                    drain(3)
                nxt = gH + ((j + 1) * 3 + 2) // 4 if j < 7 else gH + 6
                grp(nxt)
            drain()
            spos[0] = gH + 6
            ada_q = ada_steps(l + 1) if (hf == 0 and l + 1 < L) else []
            for eh in range(2):
                S.dma("pool", wo_h[eh][:].ap, wo_d[l, eh], writes=[wo_h[eh][:]])
            for t2 in range(2):
                tt = hf * 2 + t2
                ts = slice(tt * 512, (tt + 1) * 512)
                for e_ in range(8):
                    wo_ = wo_h[e_ // 4]
                    eo = (e_ % 4) * 128
                    b = nb((0, 1, 2, 3, 4, 5))
                    for kc in range(8):
                        mm(ps[:, b, :], wo_[:, kc, eo:eo + 128], mT[:, kc, t2 * 512:(t2 + 1) * 512], kc == 0, kc == 7)
                    stt(xT[:, e_, ts], ps[:, b, :], gatea[:, e_:e_ + 1], xT[:, e_, ts], ALU.mult, ALU.add)
                    if ada_q:
                        ada_q.pop(0)()
                    drain(2)
                while ada_q:
                    ada_q.pop(0)()
                ln_pieces(l, (tt,), l + 1 < L)
        if l + 1 == L:
            drain()

    ada_compute(0)
    for l in range(L):
        layer(l)
    outs = []
    for c in range(8):
        outs.append(S.dma("sp", y_d[:, c, :], xT[:, c, :].ap, reads=[xT[:, c, :]]))
    S.emit(final_waits=outs)
    return nc


_NC_CACHE = {}


def _get_nc(L):
    if L not in _NC_CACHE:
        _NC_CACHE[L] = build(L)
    return _NC_CACHE[L]


def _core_inputs(x_b, c_b, pos_b):
    xt = np.ascontiguousarray(x_b.T.reshape(8, 128, T).transpose(1, 0, 2))
    ct = np.ascontiguousarray(c_b.reshape(8, 128).T)
    pp = np.ascontiguousarray(np.broadcast_to(pos_b[None, :].astype(np.int32), (128, T)))
    return xt, ct, pp


FUSED = True


def kernel(**inputs):
    inp = {k: np.asarray(v) for k, v in inputs.items()}
    x = inp["x"].astype(np.float32)
    B = x.shape[0]
    groups = [list(range(DEPTH))] if FUSED else [[l] for l in range(DEPTH)]
    cur = [None] * B
    for b in range(B):
        cur[b] = _core_inputs(x[b], inp["c"][b], inp["positions"][b])
    for layers in groups:
        w = _prep_weights(inp, layers)
        nc = _get_nc(len(layers))
        in_maps = []
        for b in range(B):
            m = dict(w)
            m["x"], m["cT"], m["pos"] = cur[b]
            in_maps.append(m)
        res = run_bass_kernel_spmd(nc, in_maps, core_ids=list(range(B)))
        for b in range(B):
            cur[b] = (np.ascontiguousarray(res.results[b]["y"]), cur[b][1], cur[b][2])
    out = np.empty((B, T, D), np.float32)
    for b in range(B):
        out[b] = cur[b][0].transpose(1, 0, 2).reshape(D, T).T
    return out
```
